# Optimizing a Trainium2 kernel written in Bass

```python
import math
import jax, jax.numpy as jnp
from jax import lax
import numpy as np

D_MODEL = 2048
BATCH = 4
SEQ = 4096
DEPTH = 4

GRID_W = 64
CTX_LEN = 256
RMS_EPS = 1e-6
NEG_INF = -1e30

BRANCH_WIDTH = D_MODEL // 4
N_BRANCHES = 4
NA_HEAD_DIM = 64
NA_HEADS = BRANCH_WIDTH // NA_HEAD_DIM
NA_WIDTH = NA_HEADS * NA_HEAD_DIM
NA_WIN_ROWS = 8
NA_WIN_COLS = 16
POOL_WIDTH = BRANCH_WIDTH
POOL_WINDOWS = (2, 4, 8, 16)
POOL_GROUPS = 4
POOL_GROUP_DIM = POOL_WIDTH // POOL_GROUPS
CONV_WIDTH = BRANCH_WIDTH
CONV_K = 3
SSM_WIDTH = BRANCH_WIDTH
SSM_GROUP_DIM = 16
SSM_GROUPS = SSM_WIDTH // SSM_GROUP_DIM
SSM_STATE = 64
SSM_DT_MIN = 1e-3
SSM_DT_MAX = 1e-1

BRANCH_TOTAL = NA_WIDTH + POOL_WIDTH + CONV_WIDTH + SSM_WIDTH
IN_LAYOUT = (
    ("na_q", NA_WIDTH), ("na_k", NA_WIDTH), ("na_v", NA_WIDTH), ("na_z", NA_WIDTH),
    ("pool_u", POOL_WIDTH), ("pool_z", POOL_WIDTH),
    ("conv_x", CONV_WIDTH), ("conv_b", CONV_WIDTH), ("conv_c", CONV_WIDTH), ("conv_z", CONV_WIDTH),
    ("ssm_u", SSM_WIDTH), ("ssm_z", SSM_WIDTH),
    ("merge", N_BRANCHES * D_MODEL),
)
IN_TOTAL = sum(size for _, size in IN_LAYOUT)

kernel_name = "hybrid_gated_mixer_dit_block"


def _in_slices():
    out, start = {}, 0
    for name, size in IN_LAYOUT:
        out[name] = (start, start + size)
        start += size
    return out


def rms_norm(x, g):
    xf = x.astype(jnp.float32)
    y = xf * lax.rsqrt(jnp.mean(xf * xf, axis=-1, keepdims=True) + RMS_EPS)
    return (y * g.astype(jnp.float32)).astype(x.dtype)


def neighbourhood_attention(q, k, v, qc, kc, vc, rpb):
    B, L, H, Dh = q.shape
    rows = L // GRID_W
    kr = min(NA_WIN_ROWS, rows)
    kcw = NA_WIN_COLS
    scale = Dh ** -0.5
    r = jnp.arange(rows)
    row_idx = jnp.clip(r - kr // 2, 0, rows - kr)[:, None] + jnp.arange(kr)[None, :]
    col = jnp.arange(GRID_W)
    col_start = jnp.clip(col - kcw // 2, 0, GRID_W - kcw)
    in_win = (col[None, :] >= col_start[:, None]) & (col[None, :] < col_start[:, None] + kcw)
    drow = row_idx - r[:, None] + (NA_WIN_ROWS - 1)
    dcol = jnp.clip(col[None, :] - col[:, None] + (NA_WIN_COLS - 1), 0, 2 * NA_WIN_COLS - 2)
    bias = rpb[:, drow[:, None, :, None], dcol[None, :, None, :]].astype(jnp.float32)
    bias = jnp.where(in_win[None, None, :, None, :], bias, NEG_INF)

    qg = q.reshape(B, rows, GRID_W, H, Dh)
    kg = k.reshape(B, rows, GRID_W, H, Dh)[:, row_idx]
    vg = v.reshape(B, rows, GRID_W, H, Dh)[:, row_idx]
    s_band = jnp.einsum('brqhd,brkwhd->bhrqkw', qg, kg,
                        preferred_element_type=jnp.float32) * scale + bias[None]
    s_ctx = jnp.einsum('brqhd,bnhd->bhrqn', qg, kc, preferred_element_type=jnp.float32) * scale
    n_band = kr * GRID_W
    s = jnp.concatenate([s_band.reshape(B, H, rows, GRID_W, n_band), s_ctx], axis=-1)
    p = jax.nn.softmax(s, axis=-1)
    p_band = p[..., :n_band].reshape(B, H, rows, GRID_W, kr, GRID_W).astype(v.dtype)
    p_ctx = p[..., n_band:].astype(v.dtype)
    o = (jnp.einsum('bhrqkw,brkwhd->brqhd', p_band, vg)
         + jnp.einsum('bhrqn,bnhd->brqhd', p_ctx, vc))
    o = o.reshape(B, L, H * Dh)
    oc = None
    if qc is not None:
        sc = jnp.einsum('bnhd,bmhd->bhnm', qc, kc, preferred_element_type=jnp.float32) * scale
        pc = jax.nn.softmax(sc, axis=-1).astype(vc.dtype)
        oc = jnp.einsum('bhnm,bmhd->bnhd', pc, vc).reshape(qc.shape[0], qc.shape[1], H * Dh)
    return o, oc


def centred_pool_minus_identity(x, window):
    B, L, C = x.shape
    xf = x.astype(jnp.float32)
    cs = jnp.concatenate([jnp.zeros((B, 1, C), jnp.float32), jnp.cumsum(xf, axis=1)], axis=1)
    t = jnp.arange(L)
    lo = jnp.clip(t - window // 2, 0, L)
    hi = jnp.clip(t + window - window // 2, 0, L)
    cnt = (hi - lo).astype(jnp.float32)[None, :, None]
    return ((cs[:, hi] - cs[:, lo]) / cnt - xf).astype(x.dtype)


def pool_branch(u, pool_w, pool_scale):
    B, L, _ = u.shape
    groups = jnp.split(u, POOL_GROUPS, axis=-1)
    pooled = jnp.stack([centred_pool_minus_identity(g, w) for g, w in zip(groups, POOL_WINDOWS)], axis=2)
    mixed = jnp.einsum('blgc,gcd->blgd', pooled, pool_w).reshape(B, L, POOL_WIDTH)
    return mixed * pool_scale


def dwconv3(x, w):
    L = x.shape[1]
    xp = jnp.pad(x, ((0, 0), (1, 1), (0, 0)))
    return xp[:, :L] * w[0] + xp[:, 1:L + 1] * w[1] + xp[:, 2:] * w[2]


def conv_branch(xv, gb, gc, conv_w):
    return gb * dwconv3(gc * xv, conv_w)


def _cmul(ar, ai, br, bi):
    return ar * br - ai * bi, ar * bi + ai * br


def s5_discretise(a_re, a_im, log_dt, b_re, b_im):
    f32 = jnp.float32
    a_re, a_im = a_re.astype(f32), a_im.astype(f32)
    dt = jnp.exp(log_dt.astype(f32))[:, None]
    mag = jnp.exp(a_re * dt)
    abar_re, abar_im = mag * jnp.cos(a_im * dt), mag * jnp.sin(a_im * dt)
    den = a_re * a_re + a_im * a_im
    num_re, num_im = abar_re - 1.0, abar_im
    f_re = (num_re * a_re + num_im * a_im) / den
    f_im = (num_im * a_re - num_re * a_im) / den
    bbar_re, bbar_im = _cmul(f_re[..., None], f_im[..., None], b_re.astype(f32), b_im.astype(f32))
    return abar_re, abar_im, bbar_re, bbar_im


def diag_scan(abar_re, abar_im, bu_re, bu_im, s0_re, s0_im, reverse):
    L = bu_re.shape[1]
    a_re = jnp.broadcast_to(abar_re, (1, L) + abar_re.shape)
    a_im = jnp.broadcast_to(abar_im, (1, L) + abar_im.shape)

    def combine(e1, e2):
        a1r, a1i, b1r, b1i = e1
        a2r, a2i, b2r, b2i = e2
        ar, ai = _cmul(a2r, a2i, a1r, a1i)
        br, bi = _cmul(a2r, a2i, b1r, b1i)
        return ar, ai, br + b2r, bi + b2i

    pr, pi, sr, si = lax.associative_scan(combine, (a_re, a_im, bu_re, bu_im), axis=1, reverse=reverse)
    if s0_re is not None:
        ir, ii = _cmul(pr, pi, s0_re[:, None], s0_im[:, None])
        sr, si = sr + ir, si + ii
    return sr, si


def s5_readout(s_re, s_im, c_re, c_im):
    B, L = s_re.shape[:2]
    y = (jnp.einsum('blgp,ghp->blgh', s_re, c_re.astype(jnp.float32))
         - jnp.einsum('blgp,ghp->blgh', s_im, c_im.astype(jnp.float32)))
    return y.reshape(B, L, SSM_WIDTH)


def s5_glu(y, glu_w, dtype):
    g = jax.nn.gelu(y).astype(dtype)
    ga, gb = jnp.split(g @ glu_w, 2, axis=-1)
    return ga * jax.nn.sigmoid(gb)


def s5_branch(u, uc, a_re, a_im, log_dt, b_re, b_im, c_re, c_im, d_skip, glu_w, with_ctx_out):
    dtype = u.dtype
    B, L, _ = u.shape
    N = uc.shape[1]
    uf, ucf = u.astype(jnp.float32), uc.astype(jnp.float32)
    ug = uf.reshape(B, L, SSM_GROUPS, SSM_GROUP_DIM)
    ucg = ucf.reshape(B, N, SSM_GROUPS, SSM_GROUP_DIM)
    dsk = d_skip.astype(jnp.float32)
    y = dsk * uf
    yc = dsk * ucf if with_ctx_out else None
    for direction, reverse in ((0, False), (1, True)):
        abr, abi, bbr, bbi = s5_discretise(a_re[direction], a_im[direction], log_dt[direction],
                                           b_re[direction], b_im[direction])
        buc_r = jnp.einsum('blgh,gph->blgp', ucg, bbr)
        buc_i = jnp.einsum('blgh,gph->blgp', ucg, bbi)
        sc_r, sc_i = diag_scan(abr, abi, buc_r, buc_i, None, None, reverse)
        last = 0 if reverse else N - 1
        bu_r = jnp.einsum('blgh,gph->blgp', ug, bbr)
        bu_i = jnp.einsum('blgh,gph->blgp', ug, bbi)
        s_r, s_i = diag_scan(abr, abi, bu_r, bu_i, sc_r[:, last], sc_i[:, last], reverse)
        y = y + s5_readout(s_r, s_i, c_re[direction], c_im[direction])
        if with_ctx_out:
            yc = yc + s5_readout(sc_r, sc_i, c_re[direction], c_im[direction])
    out = s5_glu(y, glu_w, dtype)
    outc = s5_glu(yc, glu_w, dtype) if with_ctx_out else None
    return out, outc


def gated_merge(outs, zs, gate_logits, b_gate, w_br, w_o):
    gates = jax.nn.sigmoid((gate_logits + b_gate).astype(jnp.float32)).astype(gate_logits.dtype)
    merged = None
    start = 0
    for i, (o, z) in enumerate(zip(outs, zs)):
        width = o.shape[-1]
        br = (o * jax.nn.silu(z)) @ w_br[start:start + width]
        term = gates[..., i * D_MODEL:(i + 1) * D_MODEL] * br
        merged = term if merged is None else merged + term
        start += width
    return merged @ w_o


def hybrid_mixer(h, hc, w_in, b_gate, na_rpb, pool_w, pool_scale, conv_w,
                 ssm_a_re, ssm_a_im, ssm_log_dt, ssm_b_re, ssm_b_im, ssm_c_re, ssm_c_im, ssm_d,
                 glu_w, w_br, w_o, with_ctx_out):
    sl = _in_slices()
    B, L, _ = h.shape
    N = hc.shape[1]
    proj = h @ w_in

    def part(name):
        a, b = sl[name]
        return proj[..., a:b]

    if with_ctx_out:
        projc = hc @ w_in

        def partc(name):
            a, b = sl[name]
            return projc[..., a:b]
    else:
        def partc(name):
            a, b = sl[name]
            return hc @ w_in[:, a:b]

    heads = lambda t, n: t.reshape(B if t.shape[1] == L else t.shape[0], n, NA_HEADS, NA_HEAD_DIM)
    q, k, v = heads(part("na_q"), L), heads(part("na_k"), L), heads(part("na_v"), L)
    kc, vc = heads(partc("na_k"), N), heads(partc("na_v"), N)
    qc = heads(partc("na_q"), N) if with_ctx_out else None
    o_na, oc_na = neighbourhood_attention(q, k, v, qc, kc, vc, na_rpb)

    o_pool = pool_branch(part("pool_u"), pool_w, pool_scale)
    o_conv = conv_branch(part("conv_x"), part("conv_b"), part("conv_c"), conv_w)
    o_ssm, oc_ssm = s5_branch(part("ssm_u"), partc("ssm_u"), ssm_a_re, ssm_a_im, ssm_log_dt,
                              ssm_b_re, ssm_b_im, ssm_c_re, ssm_c_im, ssm_d, glu_w, with_ctx_out)
    y = gated_merge([o_na, o_pool, o_conv, o_ssm],
                    [part("na_z"), part("pool_z"), part("conv_z"), part("ssm_z")],
                    part("merge"), b_gate, w_br, w_o)
    yc = None
    if with_ctx_out:
        oc_pool = pool_branch(partc("pool_u"), pool_w, pool_scale)
        oc_conv = conv_branch(partc("conv_x"), partc("conv_b"), partc("conv_c"), conv_w)
        yc = gated_merge([oc_na, oc_pool, oc_conv, oc_ssm],
                         [partc("na_z"), partc("pool_z"), partc("conv_z"), partc("ssm_z")],
                         partc("merge"), b_gate, w_br, w_o)
    return y, yc


def setup_inputs(seed: int = 0) -> dict:
    key = jax.random.key(seed)
    ks = jax.random.split(key, 32)
    f32 = jnp.float32

    def nrm(k, shape, std):
        return jax.random.normal(k, shape, f32) * std

    D = D_MODEL
    n_idx = jnp.arange(SSM_STATE, dtype=f32)
    sp = (DEPTH, 2, SSM_GROUPS, SSM_STATE)
    return {
        "x": nrm(ks[0], (BATCH, SEQ, D), 1.0),
        "c": nrm(ks[1], (BATCH, D), 1.0),
        "ctx": nrm(ks[2], (BATCH, CTX_LEN, D), 1.0),
        "c_ctx": nrm(ks[3], (D,), 1.0),
        "w_mod": nrm(ks[4], (DEPTH, D, 3 * D), 0.5 * D ** -0.5),
        "b_mod": nrm(ks[5], (DEPTH, 3 * D), 0.02),
        "g_pre": 1.0 + nrm(ks[6], (DEPTH, D), 0.02),
        "g_post": 1.0 + nrm(ks[7], (DEPTH, D), 0.02),
        "w_in": nrm(ks[8], (DEPTH, D, IN_TOTAL), D ** -0.5),
        "b_gate": nrm(ks[9], (DEPTH, N_BRANCHES * D), 0.02),
        "na_rpb": nrm(ks[10], (DEPTH, NA_HEADS, 2 * NA_WIN_ROWS - 1, 2 * NA_WIN_COLS - 1), 0.02),
        "pool_w": nrm(ks[11], (DEPTH, POOL_GROUPS, POOL_GROUP_DIM, POOL_GROUP_DIM), POOL_GROUP_DIM ** -0.5),
        "pool_scale": 1.0 + nrm(ks[12], (DEPTH, POOL_WIDTH), 0.02),
        "conv_w": nrm(ks[13], (DEPTH, CONV_K, CONV_WIDTH), CONV_K ** -0.5),
        "ssm_a_re": -0.5 + nrm(ks[14], sp, 0.01),
        "ssm_a_im": math.pi * n_idx + nrm(ks[15], sp, 0.01),
        "ssm_log_dt": jax.random.uniform(ks[16], (DEPTH, 2, SSM_GROUPS), f32,
                                         math.log(SSM_DT_MIN), math.log(SSM_DT_MAX)),
        "ssm_b_re": nrm(ks[17], sp + (SSM_GROUP_DIM,), (2 * SSM_GROUP_DIM) ** -0.5),
        "ssm_b_im": nrm(ks[18], sp + (SSM_GROUP_DIM,), (2 * SSM_GROUP_DIM) ** -0.5),
        "ssm_c_re": nrm(ks[19], (DEPTH, 2, SSM_GROUPS, SSM_GROUP_DIM, SSM_STATE), SSM_STATE ** -0.5),
        "ssm_c_im": nrm(ks[20], (DEPTH, 2, SSM_GROUPS, SSM_GROUP_DIM, SSM_STATE), SSM_STATE ** -0.5),
        "ssm_d": nrm(ks[21], (DEPTH, SSM_WIDTH), 1.0),
        "glu_w": nrm(ks[22], (DEPTH, SSM_WIDTH, 2 * SSM_WIDTH), SSM_WIDTH ** -0.5),
        "w_br": nrm(ks[23], (DEPTH, BRANCH_TOTAL, D), BRANCH_WIDTH ** -0.5),
        "w_o": nrm(ks[24], (DEPTH, D, D), D ** -0.5),
    }


def reference(x, c, ctx, c_ctx, w_mod, b_mod, g_pre, g_post, w_in, b_gate, na_rpb, pool_w,
              pool_scale, conv_w, ssm_a_re, ssm_a_im, ssm_log_dt, ssm_b_re, ssm_b_im,
              ssm_c_re, ssm_c_im, ssm_d, glu_w, w_br, w_o):
    c_act = jax.nn.silu(c)
    cc_act = jax.nn.silu(c_ctx)
    xc = ctx
    for i in range(DEPTH):
        with_ctx_out = i < DEPTH - 1
        shift, scale, gate = jnp.split(c_act @ w_mod[i] + b_mod[i], 3, axis=-1)
        shift_c, scale_c, gate_c = jnp.split(cc_act @ w_mod[i] + b_mod[i], 3, axis=-1)
        h = rms_norm(x, g_pre[i]) * (1.0 + scale[:, None]) + shift[:, None]
        hc = rms_norm(xc, g_pre[i]) * (1.0 + scale_c) + shift_c
        y, yc = hybrid_mixer(h, hc, w_in[i], b_gate[i], na_rpb[i], pool_w[i], pool_scale[i], conv_w[i],
                             ssm_a_re[i], ssm_a_im[i], ssm_log_dt[i], ssm_b_re[i], ssm_b_im[i],
                             ssm_c_re[i], ssm_c_im[i], ssm_d[i], glu_w[i], w_br[i], w_o[i], with_ctx_out)
        x = x + gate[:, None] * rms_norm(y, g_post[i])
        if with_ctx_out:
            xc = xc + gate_c * rms_norm(yc, g_post[i])
    return x
```

```python
import numpy as np
import concourse.bass as bass
import concourse.mybir as mybir
from concourse.bass_utils import run_bass_kernel_spmd

F32 = mybir.dt.float32
BF16 = mybir.dt.bfloat16
I32 = mybir.dt.int32
ALU = mybir.AluOpType
AF = mybir.ActivationFunctionType

DEPTH = 4
D = 2048
NL = 4096
NCX = 256
NT = NL + NCX
KT = 16
TILES = [(i * 512, 512) for i in range(8)] + [(4096, 256)]
TWO_PI = float(2 * np.pi)
PI = float(np.pi)

COMPUTE_Q = ("pe", "act", "dve", "pool")
EPOCH = 20000
NDMA_SEM = 12


class Buf:
    __slots__ = ("name", "last_w", "readers", "t", "bufs")

    def __init__(self, name, t=None):
        self.name = name
        self.last_w = None
        self.readers = []
        self.t = t
        self.bufs = [self]

    def __getitem__(self, k):
        return self.t[k]


class View:
    __slots__ = ("t", "bufs")

    def __init__(self, t, bufs):
        self.t = t
        self.bufs = bufs

    def __getitem__(self, k):
        return self.t[k]


class Op:
    __slots__ = ("q", "fn", "deps", "dma", "sig", "sem", "val", "idx", "prev_same_sem", "seqno")

    def __init__(self, q, fn, dma):
        self.q = q
        self.fn = fn
        self.deps = []
        self.dma = dma
        self.sig = dma
        self.sem = None
        self.val = None
        self.idx = -1
        self.prev_same_sem = None


def _expand(lst):
    out = []
    for b in lst:
        out.extend(b.bufs)
    return out


class Sched:
    def __init__(self, nc):
        self.nc = nc
        self.qops = {q: [] for q in ("pe", "act", "dve", "pool", "sp")}

    def sb(self, name, shape, dtype):
        return Buf(name, self.nc.alloc_sbuf_tensor(name, list(shape), dtype))

    def _add(self, q, fn, reads, writes, dma):
        op = Op(q, fn, dma)
        self.seq = getattr(self, "seq", 0) + 1
        op.idx = self.seq
        reads = _expand(reads)
        writes = _expand(writes)
        deps = []
        for b in reads:
            if b.last_w is not None:
                deps.append((b.last_w, "raw"))
        for b in writes:
            if b.last_w is not None:
                deps.append((b.last_w, "waw"))
            for r in b.readers:
                deps.append((r, "war"))
        seen = set()
        best = {}
        for d, kind in deps:
            if d is op or id(d) in seen:
                continue
            seen.add(id(d))
            if d.q == q and not d.dma and not dma:
                if q == "pe" or kind == "war":
                    continue
            if d.dma:
                op.deps.append(d)
                d.sig = True
            else:
                cur = best.get(d.q)
                if cur is None or d.seqno > cur.seqno:
                    best[d.q] = d
        for d in best.values():
            op.deps.append(d)
            d.sig = True
        op.seqno = self.seq
        for b in reads:
            if not dma:
                b.readers = [r for r in b.readers if r.dma or r.q != q]
            b.readers.append(op)
        for b in writes:
            b.last_w = op
            b.readers = []
        self.qops[q].append(op)
        return op

    def op(self, q, fn, reads=(), writes=()):
        return self._add(q, fn, reads, writes, False)

    def dma(self, q, out, in_, reads=(), writes=()):
        return self._add(q, lambda e: e.dma_start(out=out, in_=in_), reads, writes, True)

    def emit(self, final_ops=()):
        nc = self.nc
        csem = {}
        for q in COMPUTE_Q:
            n = sum(1 for o in self.qops[q] if o.sig and not o.dma)
            ne = max(1, (n + EPOCH - 1) // EPOCH)
            csem[q] = [nc.alloc_semaphore(name=f"s_{q}{i}") for i in range(ne)]
        dsem = {}
        for q in ("sp", "act", "pool"):
            if any(o.dma for o in self.qops[q]):
                dsem[q] = [nc.alloc_semaphore(name=f"d_{q}{i}") for i in range(NDMA_SEM)]
        for q, lst in self.qops.items():
            cnt = 0
            dcnt = 0
            last_on_sem = {}
            for o in lst:
                if o.dma:
                    k = dcnt % NDMA_SEM
                    o.sem = dsem[q][k]
                    o.val = 16 * (dcnt // NDMA_SEM + 1)
                    o.prev_same_sem = last_on_sem.get(k)
                    last_on_sem[k] = o
                    dcnt += 1
                elif o.sig:
                    o.sem = csem[q][cnt // EPOCH]
                    o.val = cnt % EPOCH + 1
                    o.idx = cnt
                    cnt += 1
        engs = {"pe": "tensor", "act": "scalar", "dve": "vector", "pool": "gpsimd", "sp": "sync"}
        with nc.Block() as block:
            for q, lst in self.qops.items():
                if not lst:
                    continue

                def body(e, q=q, lst=lst):
                    waited = {}
                    dwaited = set()

                    def wait_for(d):
                        if d.dma:
                            if id(d) in dwaited:
                                return
                            dwaited.add(id(d))
                            e.wait_ge(d.sem, d.val)
                        else:
                            if waited.get(d.q, -1) >= d.idx:
                                return
                            waited[d.q] = d.idx
                            e.wait_ge(d.sem, d.val)

                    for o in lst:
                        for d in o.deps:
                            wait_for(d)
                        if o.dma and o.prev_same_sem is not None:
                            wait_for(o.prev_same_sem)
                        ins = o.fn(e)
                        if o.dma:
                            ins.then_inc(o.sem, 16)
                        elif o.sig:
                            ins.then_inc(o.sem, 1)
                    if q == "sp":
                        for d in final_ops:
                            wait_for(d)

                getattr(block, engs[q])(body)


CH = 512
NCHUNK = 164


class Arena:
    def __init__(self, S):
        self.t = S.nc.alloc_sbuf_tensor("arena", [128, CH * NCHUNK], BF16)
        self.chunks = [Buf(f"ch{i}") for i in range(NCHUNK)]

    def view(self, off_b, nbytes, dtype=BF16, pat=None, **kw):
        assert off_b % 4 == 0 and off_b + nbytes <= CH * NCHUNK * 2, (off_b, nbytes)
        e0 = off_b // 2
        e1 = (off_b + nbytes) // 2
        ap = self.t[:, e0:e1]
        if dtype != BF16:
            ap = ap.bitcast(dtype)
        if pat is not None:
            ap = ap.rearrange(pat, **kw)
        c0 = e0 // CH
        c1 = (e1 - 1) // CH
        return View(ap, self.chunks[c0:c1 + 1])


class Alloc:
    def __init__(self, arena):
        self.a = arena
        self.off = 0

    def get(self, nbytes, dtype=BF16, pat=None, **kw):
        self.off = (self.off + 1023) // 1024 * 1024
        v = self.a.view(self.off, nbytes, dtype, pat, **kw)
        self.off += nbytes
        return v


class _Stop(Exception):
    pass


def build_nc(depth=DEPTH, dbg=None, stop=None):
    nc = bass.Bass("TRN2", target_bir_lowering=False)
    S = Sched(nc)
    AR = Arena(S)

    def din(name, shape, dt=F32):
        return nc.dram_tensor(name, list(shape), dt, kind="ExternalInput").ap()

    def dscr(name, shape, dt):
        return nc.dram_tensor(name, list(shape), dt, kind=("ExternalOutput" if dbg else "Internal")).ap()

    x0 = din("x0", [D, NT])
    cvec = din("cvec", [128, KT, 2])
    w_mod = din("w_mod", [DEPTH, D, 3 * D])
    b_mod = din("b_mod", [DEPTH, 128, 48])
    g_pre = din("g_pre", [DEPTH, 128, KT])
    g_post = din("g_post", [DEPTH, 128, KT])
    w_in = din("w_in", [DEPTH, D, 14336])
    b_gate = din("b_gate", [DEPTH, 128, 64])
    na_bias = din("na_bias", [DEPTH, 8, 128, 5, 640])
    pool_w = din("pool_w", [DEPTH, 128, 4, 128])
    pool_sc = din("pool_sc", [DEPTH, 128, 4])
    pool_edge = din("pool_edge", [128, 2, 4, 2, 8])
    conv_w = din("conv_w", [DEPTH, 128, 4, 3])
    ssm_are = din("ssm_are", [DEPTH, 128, 32])
    ssm_aim = din("ssm_aim", [DEPTH, 128, 32])
    ssm_ldt = din("ssm_ldt", [DEPTH, 128, 32])
    ssm_B = din("ssm_B", [DEPTH, 128, 32, 2, 128])
    ssm_C = din("ssm_C", [DEPTH, 128, 32, 2, 128])
    ssm_d = din("ssm_d", [DEPTH, 128, 4])
    glu_w = din("glu_w", [DEPTH, 512, 1024])
    w_br = din("w_br", [DEPTH, D, D])
    w_o = din("w_o", [DEPTH, D, D])
    ident_in = din("ident", [128, 128])
    iota17 = din("iota17", [128, 17])
    iota_nt = din("iota_nt", [128, NT])
    outT = nc.dram_tensor("outT", [D, NL], F32, kind="ExternalOutput").ap()

    xs = dscr("xs", [D, NT], F32)
    hT = dscr("hT", [D, NT], BF16)
    proj = dscr("proj", [14336, NT], BF16)
    vT = dscr("vT", [NT, 512], BF16)
    a_d = dscr("a_d", [D, NT], BF16)
    mg = dscr("mg", [D, NT], BF16)
    yss = dscr("yss", [512, NT], BF16)
    dbufs = {}
    dbg_pob = nc.dram_tensor("dbg_pob", [4, 128, NL], BF16, kind="ExternalOutput").ap() if dbg else None

    def DB(name, rb, ti):
        k = (name, rb, ti)
        if k not in dbufs:
            dbufs[k] = Buf(str(k))
        return dbufs[k]

    def DBrow(name, rb):
        return [DB(name, rb, ti) for ti in range(9)]

    NTI = len(TILES)

    ident = S.sb("identb", [128, 128], BF16)
    ones = S.sb("onesb", [128, 128], BF16)
    epsb = S.sb("epsb", [128, 1], F32)
    cact = S.sb("cact", [128, KT, 2], BF16)
    cv32 = S.sb("cv32", [128, KT, 2], F32)
    modt = S.sb("modt", [128, 48, 2], F32)
    bmod = S.sb("bmod", [128, 48], F32)
    gpre = S.sb("gpre", [128, KT], F32)
    gpost = S.sb("gpost", [128, KT], F32)
    Amod = S.sb("Amod", [128, KT, 2], F32)
    Gmod = S.sb("Gmod", [128, KT, 2], F32)
    bgate = S.sb("bgate", [128, 64], F32)
    psc = S.sb("psc", [128, 4], F32)
    pedge = S.sb("pedge", [128, 2, 4, 2, 8], F32)
    cw = S.sb("cw", [128, 4, 3], F32)
    sdk = S.sb("sdk", [128, 4], F32)
    io17 = S.sb("io17", [128, 17], F32)
    (s_are, s_aim, s_ldt, s_dt, s_x1, s_th, s_rho, s_thr, s_sn1, s_cs1, s_nr, s_ni, s_den, s_fr, s_fi, s_t1, s_t2) = [
        S.sb(f"ss{i}", [128, 32], F32) for i in range(17)]
    s_ti = S.sb("ssti", [128, 32], I32)
    sB = S.sb("sB", [128, 2, 128], F32)
    sC = S.sb("sC", [128, 2, 128], F32)
    sT = S.sb("sT", [128, 128], F32)
    sBb = S.sb("sBb", [128, 2, 128], BF16)
    sCb = S.sb("sCb", [128, 2, 128], BF16)
    sWT = S.sb("sWT", [128, 2, 128], BF16)
    stg = [S.sb(f"stg{i}", [128, 512], BF16) for i in range(4)]
    stgf = [S.sb(f"stgf{i}", [128, 512], F32) for i in range(3)]
    rstd = S.sb("rstd", [128, 512], F32)
    rtmp = S.sb("rtmp", [128, 512], F32)

    PSB = nc.alloc_psum_tensor("psall", [128, 8, 512], F32)
    BK = [Buf(f"bank{i}") for i in range(8)]

    def bank(i):
        return View(PSB[:, i, :], [BK[i]])

    def bank2(i):
        return View(PSB[:, i:i + 2, :].rearrange("p a b -> p (a b)"), [BK[i], BK[i + 1]])

    def bankbf(i):
        return View(PSB[:, i, :].bitcast(BF16), [BK[i]])

    S.dma("pool", ident[:, :], ident_in, writes=[ident])
    S.op("dve", lambda e: e.memset(ones[:, :], 1.0), writes=[ones])
    S.op("dve", lambda e: e.memset(epsb[:, :], 1e-6), writes=[epsb])
    S.dma("sp", cv32[:, :, :], cvec, writes=[cv32])
    S.op("act", lambda e: e.activation(out=cact[:, :, :], in_=cv32[:, :, :], func=AF.Silu), reads=[cv32], writes=[cact])
    S.dma("sp", pedge[:, :, :, :, :], pool_edge, writes=[pedge])
    S.dma("sp", io17[:, :], iota17, writes=[io17])

    stg_i = [0]

    def next_stg():
        stg_i[0] += 1
        return stg[stg_i[0] % 4]

    bank_i = [0]

    def next_bank(n=4):
        bank_i[0] += 1
        return bank(bank_i[0] % n)

    def w_panel_view(al):
        return al.get(KT * 1024 * 2, BF16, "p (k n) -> p k n", k=KT)

    def load_w_panel(dst, src_l, col0, ncols=1024):
        src = src_l.rearrange("(k p) n -> p k n", p=128)
        for kk in range(0, KT, 4):
            S.dma("pool", dst[:, kk:kk + 4, 0:ncols], src[:, kk:kk + 4, col0:col0 + ncols], writes=[dst])

    for L in range(depth):
      try:
        xsrc = x0 if L == 0 else xs
        xsn = "x0" if L == 0 else "xs"
        last = (L == depth - 1) and not dbg
        with_ctx = (L < DEPTH - 1) and not last
        al = Alloc(AR)
        WP = [w_panel_view(al), w_panel_view(al)]
        S.dma("sp", bmod[:, :], b_mod[L], writes=[bmod])
        S.dma("sp", gpre[:, :], g_pre[L], writes=[gpre])
        S.dma("sp", gpost[:, :], g_post[L], writes=[gpost])
        S.dma("sp", bgate[:, :], b_gate[L], writes=[bgate])
        S.dma("sp", psc[:, :], pool_sc[L], writes=[psc])
        S.dma("sp", cw[:, :, :], conv_w[L], writes=[cw])
        S.dma("sp", sdk[:, :], ssm_d[L], writes=[sdk])
        for pp in range(6):
            wp = WP[pp % 2]
            load_w_panel(wp, w_mod[L], pp * 1024)
            for bl in range(8):
                j = pp * 8 + bl
                pb = next_bank()
                for k in range(KT):
                    S.op("pe", lambda e, k=k, bl=bl, wp=wp, pb=pb: e.matmul(pb[:, 0:2], lhsT=wp[:, k, bl * 128:(bl + 1) * 128],
                                                                    rhs=cact[:, k, :], start=(k == 0), stop=(k == KT - 1)),
                         reads=[wp, cact], writes=[pb])
                S.op("dve", lambda e, j=j, pb=pb: e.tensor_scalar(out=modt[:, j, :], in0=pb[:, 0:2], scalar1=bmod[:, j:j + 1], scalar2=None, op0=ALU.add),
                     reads=[pb, bmod], writes=[modt])
        S.op("dve", lambda e: e.scalar_tensor_tensor(out=Amod[:, :, :], in0=modt[:, 16:32, :], scalar=1.0,
                                                     in1=gpre[:, :].unsqueeze(2).to_broadcast([128, KT, 2]), op0=ALU.add, op1=ALU.mult),
             reads=[modt, gpre], writes=[Amod])
        S.op("dve", lambda e: e.tensor_tensor(out=Gmod[:, :, :], in0=modt[:, 32:48, :],
                                              in1=gpost[:, :].unsqueeze(2).to_broadcast([128, KT, 2]), op=ALU.mult),
             reads=[modt, gpost], writes=[Gmod])

        if stop == "S1":
            raise _Stop()
        xt = al.get(KT * 512 * 4, F32, "p (k n) -> p k n", k=KT)
        hb = al.get(KT * 512 * 2, BF16, "p (k n) -> p k n", k=KT)
        for ti, (c0, n) in enumerate(TILES):
            s = 0 if ti < 8 else 1
            S.dma("sp", xt[:, :, 0:n], xsrc.rearrange("(k p) n -> p k n", p=128)[:, :, c0:c0 + n],
                  reads=[DB(xsn, 0, ti)], writes=[xt])
            S.op("act", lambda e, n=n: e.activation(out=hb[:, :, 0:n], in_=xt[:, :, 0:n], func=AF.Square), reads=[xt], writes=[hb])
            pb = next_bank()
            for k in range(KT):
                S.op("pe", lambda e, k=k, n=n, pb=pb: e.matmul(pb[:, 0:n], lhsT=ones[:, :], rhs=hb[:, k, 0:n], start=(k == 0), stop=(k == KT - 1)),
                     reads=[ones, hb], writes=[pb])
            S.op("act", lambda e, n=n, pb=pb: e.activation(out=rtmp[:, 0:n], in_=pb[:, 0:n], func=AF.Sqrt, bias=epsb[:, 0:1], scale=1.0 / D),
                 reads=[pb, epsb], writes=[rtmp])
            S.op("dve", lambda e, n=n: e.reciprocal(out=rstd[:, 0:n], in_=rtmp[:, 0:n]), reads=[rtmp], writes=[rstd])
            S.op("dve", lambda e, n=n: e.tensor_tensor(out=xt[:, :, 0:n], in0=xt[:, :, 0:n],
                                                       in1=rstd[:, 0:n].unsqueeze(1).to_broadcast([128, KT, n]), op=ALU.mult),
                 reads=[xt, rstd], writes=[xt])
            for k in range(KT):
                S.op("act", lambda e, k=k, n=n, s=s: e.activation(out=hb[:, k, 0:n], in_=xt[:, k, 0:n], func=AF.Identity,
                                                                 bias=modt[:, k, s:s + 1], scale=Amod[:, k, s:s + 1]),
                     reads=[xt, modt, Amod], writes=[hb])
            S.dma("sp", hT.rearrange("(k p) n -> p k n", p=128)[:, :, c0:c0 + n], hb[:, :, 0:n], reads=[hb], writes=[DB("hT", 0, ti)])

        if stop == "S2":
            raise _Stop()
        al = Alloc(AR)
        WP = [w_panel_view(al), w_panel_view(al)]
        HB = [al.get(KT * 512 * 2, BF16, "p (k n) -> p k n", k=KT) for _ in range(2)]
        vst = al.get(512 * 2 * 2, BF16, "p (a n) -> p a n", a=2)
        load_w_panel(WP[0], w_in[L], 0)
        hcnt = 0
        for pp in range(14):
            wp = WP[pp % 2]
            if pp + 1 < 14:
                load_w_panel(WP[(pp + 1) % 2], w_in[L], (pp + 1) * 1024)
            for ti, (c0, n) in enumerate(TILES):
                hbt = HB[hcnt % 2]
                hcnt += 1
                S.dma("sp", hbt[:, :, 0:n], hT.rearrange("(k p) n -> p k n", p=128)[:, :, c0:c0 + n], reads=[DB("hT", 0, ti)], writes=[hbt])
                for bl in range(8):
                    bi = pp * 8 + bl
                    if 8 <= bi < 12:
                        if bi > 8:
                            continue
                        for tb in range(n // 128):
                            pb = next_bank()
                            for k in range(KT):
                                S.op("pe", lambda e, k=k, tb=tb, pb=pb, hbt=hbt, wp=wp: e.matmul(pb[:, :], lhsT=hbt[:, k, tb * 128:(tb + 1) * 128],
                                                                                         rhs=wp[:, k, 0:512], start=(k == 0), stop=(k == KT - 1)),
                                     reads=[hbt, wp], writes=[pb])
                            st = next_stg()
                            S.op("dve", lambda e, pb=pb, st=st: e.tensor_copy(out=st[:, :], in_=pb[:, :]), reads=[pb], writes=[st])
                            S.dma("sp", vT[c0 + tb * 128:c0 + (tb + 1) * 128, :], st[:, :], reads=[st], writes=[DB("vT", 0, ti)])
                        continue
                    pb = next_bank()
                    for k in range(KT):
                        S.op("pe", lambda e, k=k, bl=bl, pb=pb, hbt=hbt, wp=wp, n=n: e.matmul(pb[:, 0:n], lhsT=wp[:, k, bl * 128:(bl + 1) * 128],
                                                                                      rhs=hbt[:, k, 0:n], start=(k == 0), stop=(k == KT - 1)),
                             reads=[hbt, wp], writes=[pb])
                    st = next_stg()
                    if bi < 4:
                        S.op("dve", lambda e, pb=pb, st=st, n=n: e.tensor_scalar(out=st[:, 0:n], in0=pb[:, 0:n], scalar1=0.125, scalar2=None, op0=ALU.mult),
                             reads=[pb], writes=[st])
                    elif bi >= 48:
                        S.op("act", lambda e, pb=pb, st=st, n=n, bi=bi: e.activation(out=st[:, 0:n], in_=pb[:, 0:n], func=AF.Sigmoid,
                                                                                  bias=bgate[:, bi - 48:bi - 47], scale=1.0),
                             reads=[pb, bgate], writes=[st])
                    elif (12 <= bi < 16) or (20 <= bi < 24) or (36 <= bi < 40) or (44 <= bi < 48):
                        S.op("act", lambda e, pb=pb, st=st, n=n: e.activation(out=st[:, 0:n], in_=pb[:, 0:n], func=AF.Silu), reads=[pb], writes=[st])
                    else:
                        S.op("dve", lambda e, pb=pb, st=st, n=n: e.tensor_copy(out=st[:, 0:n], in_=pb[:, 0:n]), reads=[pb], writes=[st])
                    S.dma("sp", proj[bi * 128:(bi + 1) * 128, c0:c0 + n], st[:, 0:n], reads=[st], writes=[DB("proj", bi, ti)])

        if stop == "S3":
            raise _Stop()
        al = Alloc(AR)
        qs = al.get(NT * 2)
        ks = al.get(NT * 2)
        szs = al.get(NT * 2)
        ao = al.get(NT * 2)
        vs = al.get(34 * 128 * 2, BF16, "p (b c) -> p b c", b=34)
        bt = al.get(2 * 5 * 640 * 2, BF16, "p (h t k) -> p h t k", h=2, t=5)
        PT = [al.get(896 * 2) for _ in range(2)]
        rden = S.sb(f"rden{L}", [128, 128], F32) if L == 0 else rden
        otmp = S.sb(f"otmp{L}", [128, 128], F32) if L == 0 else otmp
        acnt = 0
        for hp in range(4):
            S.dma("sp", qs[:, :], proj[hp * 128:(hp + 1) * 128, :], reads=DBrow("proj", hp), writes=[qs])
            S.dma("sp", ks[:, :], proj[512 + hp * 128:512 + (hp + 1) * 128, :], reads=DBrow("proj", 4 + hp), writes=[ks])
            S.dma("sp", szs[:, :], proj[1536 + hp * 128:1536 + (hp + 1) * 128, :], reads=DBrow("proj", 12 + hp), writes=[szs])
            S.dma("sp", vs[:, :, :], vT.rearrange("(b p) c -> p b c", p=128)[:, :, hp * 128:(hp + 1) * 128], reads=DBrow("vT", 0), writes=[vs])
            S.dma("pool", bt[:, :, :, :], na_bias[L, 2 * hp:2 * hp + 2].rearrange("h p t k -> p h t k"), writes=[bt])
            qblocks = [("band", i) for i in range(32)] + ([("ctx", 0), ("ctx", 1)] if with_ctx else [])
            for kind, i in qblocks:
                for hd in range(2):
                    pbs = 64 * hd
                    sb2 = bank2(4 + 2 * (acnt % 2))
                    oc = bank(acnt % 2 + 2) if False else bank(2 + acnt % 2)
                    pt = PT[acnt % 2]
                    acnt += 1
                    if kind == "band":
                        r0 = 2 * i
                        pat = 0 if i == 0 else 1 if i == 1 else 3 if i == 30 else 4 if i == 31 else 2
                        lo = min(max(r0 - 4, 0), 54)
                        k0 = lo * 64
                        q0 = r0 * 64
                        nkb = 7
                        for b in range(5):
                            S.op("pe", lambda e, b=b, sb2=sb2, pbs=pbs, k0=k0, q0=q0: e.matmul(sb2[:, b * 128:(b + 1) * 128], lhsT=ks[pbs:pbs + 64, k0 + b * 128:k0 + (b + 1) * 128],
                                                                                     rhs=qs[pbs:pbs + 64, q0:q0 + 128], start=True, stop=False),
                                 reads=[ks, qs], writes=[sb2])
                            S.op("pe", lambda e, b=b, sb2=sb2, hd=hd, pat=pat: e.matmul(sb2[:, b * 128:(b + 1) * 128], lhsT=bt[:, hd, pat, b * 128:(b + 1) * 128],
                                                                                    rhs=ident[:, :], start=False, stop=True),
                                 reads=[bt, ident], writes=[sb2])
                        for cb in range(2):
                            S.op("pe", lambda e, cb=cb, sb2=sb2, pbs=pbs, q0=q0: e.matmul(sb2[:, 640 + cb * 128:640 + (cb + 1) * 128],
                                                                                      lhsT=ks[pbs:pbs + 64, NL + cb * 128:NL + (cb + 1) * 128],
                                                                                      rhs=qs[pbs:pbs + 64, q0:q0 + 128], start=True, stop=True),
                                 reads=[ks, qs], writes=[sb2])
                        vblk = [k0 // 128 + b for b in range(5)] + [32, 33]
                    else:
                        q0 = NL + i * 128
                        nkb = 2
                        for cb in range(2):
                            S.op("pe", lambda e, cb=cb, sb2=sb2, pbs=pbs, q0=q0: e.matmul(sb2[:, cb * 128:(cb + 1) * 128],
                                                                                      lhsT=ks[pbs:pbs + 64, NL + cb * 128:NL + (cb + 1) * 128],
                                                                                      rhs=qs[pbs:pbs + 64, q0:q0 + 128], start=True, stop=True),
                                 reads=[ks, qs], writes=[sb2])
                        vblk = [32, 33]
                    nk = nkb * 128
                    S.op("act", lambda e, sb2=sb2, pt=pt, nk=nk: e.activation(out=pt[:, 0:nk], in_=sb2[:, 0:nk], func=AF.Exp), reads=[sb2], writes=[pt])
                    for kb in range(nkb):
                        S.op("pe", lambda e, kb=kb, oc=oc, pt=pt, vb=vblk[kb], nkb=nkb: e.matmul(oc[:, 0:128], lhsT=vs[:, vb, :], rhs=pt[:, kb * 128:(kb + 1) * 128],
                                                                                       start=(kb == 0), stop=(kb == nkb - 1)),
                             reads=[vs, pt], writes=[oc])
                    for kb in range(nkb):
                        S.op("pe", lambda e, kb=kb, oc=oc, pt=pt, nkb=nkb: e.matmul(oc[:, 128:256], lhsT=ones[:, :], rhs=pt[:, kb * 128:(kb + 1) * 128],
                                                                             start=(kb == 0), stop=(kb == nkb - 1)),
                             reads=[ones, pt], writes=[oc])
                    S.op("dve", lambda e, oc=oc, pbs=pbs: e.reciprocal(out=rden[pbs:pbs + 64, :], in_=oc[pbs:pbs + 64, 128:256]), reads=[oc], writes=[rden])
                    S.op("dve", lambda e, oc=oc, pbs=pbs: e.tensor_tensor(out=otmp[pbs:pbs + 64, :], in0=oc[pbs:pbs + 64, 0:128], in1=rden[pbs:pbs + 64, :], op=ALU.mult),
                         reads=[oc, rden], writes=[otmp])
                    S.op("dve", lambda e, pbs=pbs, q0=q0: e.tensor_tensor(out=ao[pbs:pbs + 64, q0:q0 + 128], in0=otmp[pbs:pbs + 64, :], in1=szs[pbs:pbs + 64, q0:q0 + 128], op=ALU.mult),
                         reads=[otmp, szs], writes=[ao])
            ncol = NT if with_ctx else NL
            S.dma("sp", a_d[hp * 128:(hp + 1) * 128, 0:ncol], ao[:, 0:ncol], reads=[ao], writes=DBrow("a", hp))

        if stop == "S4":
            raise _Stop()
        al = Alloc(AR)
        ubp = al.get(NT * 2)
        zbp = al.get(NT * 2)
        pob = al.get(NT * 2)
        U = al.get((NL + 16) * 4, F32)
        T1 = al.get((NL + 16) * 4, F32)
        T2 = al.get((NL + 16) * 4, F32)
        pwb = al.get(4 * 128 * 2, BF16, "p (g d) -> p g d", g=4)
        pab = al.get(NT * 2)
        S.dma("pool", pwb[:, :, :], pool_w[L], writes=[pwb])
        seqs = [(0, NL, 0)] + ([(NL, NCX, 1)] if with_ctx else [])
        for g in range(4):
            S.dma("sp", ubp[:, :], proj[2048 + g * 128:2048 + (g + 1) * 128, :], reads=DBrow("proj", 16 + g), writes=[ubp])
            S.dma("sp", zbp[:, :], proj[2560 + g * 128:2560 + (g + 1) * 128, :], reads=DBrow("proj", 20 + g), writes=[zbp])
            w = (2, 4, 8, 16)[g]
            for (c0, n, sq) in seqs:
                Ln = n + 16
                S.op("dve", lambda e, Ln=Ln: e.memset(U[:, 0:Ln], 0.0), writes=[U])
                S.op("dve", lambda e, c0=c0, n=n: e.tensor_copy(out=U[:, 8:8 + n], in_=ubp[:, c0:c0 + n]), reads=[ubp, U], writes=[U])
                S.op("dve", lambda e, Ln=Ln: e.tensor_tensor(out=T1[:, 1:Ln], in0=U[:, 0:Ln - 1], in1=U[:, 1:Ln], op=ALU.add), reads=[U], writes=[T1])
                cur, oth = T1, T2
                if w >= 4:
                    S.op("dve", lambda e, Ln=Ln: e.tensor_tensor(out=T2[:, 2:Ln - 1], in0=T1[:, 1:Ln - 2], in1=T1[:, 3:Ln], op=ALU.add), reads=[T1], writes=[T2])
                    cur, oth = T2, T1
                if w >= 8:
                    S.op("dve", lambda e, Ln=Ln: e.tensor_tensor(out=T1[:, 4:Ln - 3], in0=T2[:, 2:Ln - 5], in1=T2[:, 6:Ln - 1], op=ALU.add), reads=[T2], writes=[T1])
                    cur, oth = T1, T2
                if w >= 16:
                    S.op("dve", lambda e, Ln=Ln: e.tensor_tensor(out=T2[:, 8:Ln - 7], in0=T1[:, 4:Ln - 11], in1=T1[:, 12:Ln - 3], op=ALU.add), reads=[T1], writes=[T2])
                    cur, oth = T2, T1
                S.op("dve", lambda e, cur=cur, oth=oth, n=n, w=w: e.scalar_tensor_tensor(out=oth[:, 8:8 + n], in0=cur[:, 8:8 + n], scalar=1.0 / w, in1=U[:, 8:8 + n],
                                                                                 op0=ALU.mult, op1=ALU.subtract), reads=[cur, U], writes=[oth])
                for side, e0 in ((0, 8), (1, 8 + n - 8)):
                    S.op("dve", lambda e, cur=cur, sq=sq, g=g, side=side, e0=e0: e.tensor_tensor(out=rtmp[:, 0:8], in0=cur[:, e0:e0 + 8], in1=pedge[:, sq, g, side, :], op=ALU.mult),
                         reads=[cur, pedge], writes=[rtmp])
                    S.op("dve", lambda e, oth=oth, e0=e0: e.tensor_tensor(out=oth[:, e0:e0 + 8], in0=rtmp[:, 0:8], in1=U[:, e0:e0 + 8], op=ALU.subtract),
                         reads=[rtmp, U, oth], writes=[oth])
                S.op("act", lambda e, oth=oth, c0=c0, n=n: e.activation(out=pob[:, c0:c0 + n], in_=oth[:, 8:8 + n], func=AF.Copy), reads=[oth], writes=[pob])
                if dbg and L == 0 and sq == 0:
                    S.dma("sp", dbg_pob[g], pob[:, 0:NL], reads=[pob], writes=[DB("dbgpob", g, 0)])
                for cc in range(0, n, 512):
                    m = min(512, n - cc)
                    pb = next_bank()
                    S.op("pe", lambda e, pb=pb, g=g, c=c0 + cc, m=m: e.matmul(pb[:, 0:m], lhsT=pwb[:, g, :], rhs=pob[:, c:c + m], start=True, stop=True),
                         reads=[pwb, pob], writes=[pb])
                    S.op("dve", lambda e, pb=pb, g=g, c=c0 + cc, m=m: e.scalar_tensor_tensor(out=pab[:, c:c + m], in0=pb[:, 0:m], scalar=psc[:, g:g + 1], in1=zbp[:, c:c + m],
                                                                                    op0=ALU.mult, op1=ALU.mult), reads=[pb, psc, zbp], writes=[pab])
            ncol = NT if with_ctx else NL
            S.dma("sp", a_d[512 + g * 128:512 + (g + 1) * 128, 0:ncol], pab[:, 0:ncol], reads=[pab], writes=DBrow("a", 4 + g))

        if stop == "S5":
            raise _Stop()
        al = Alloc(AR)
        xb_ = al.get(NT * 2)
        bb_ = al.get(NT * 2)
        cb_ = al.get(NT * 2)
        zb = al.get(NT * 2)
        cab = al.get(NT * 2)
        Tt = al.get((NL + 2) * 4, F32)
        Dw = al.get(NL * 4, F32)
        for g in range(4):
            S.dma("sp", xb_[:, :], proj[3072 + g * 128:3072 + (g + 1) * 128, :], reads=DBrow("proj", 24 + g), writes=[xb_])
            S.dma("sp", bb_[:, :], proj[3584 + g * 128:3584 + (g + 1) * 128, :], reads=DBrow("proj", 28 + g), writes=[bb_])
            S.dma("sp", cb_[:, :], proj[4096 + g * 128:4096 + (g + 1) * 128, :], reads=DBrow("proj", 32 + g), writes=[cb_])
            S.dma("sp", zb[:, :], proj[4608 + g * 128:4608 + (g + 1) * 128, :], reads=DBrow("proj", 36 + g), writes=[zb])
            for (c0, n, sq) in seqs:
                S.op("dve", lambda e, n=n: e.memset(Tt[:, 0:n + 2], 0.0), writes=[Tt])
                S.op("dve", lambda e, c0=c0, n=n: e.tensor_tensor(out=Tt[:, 1:n + 1], in0=cb_[:, c0:c0 + n], in1=xb_[:, c0:c0 + n], op=ALU.mult),
                     reads=[cb_, xb_, Tt], writes=[Tt])
                S.op("dve", lambda e, n=n, g=g: e.tensor_scalar(out=Dw[:, 0:n], in0=Tt[:, 0:n], scalar1=cw[:, g, 0:1], scalar2=None, op0=ALU.mult), reads=[Tt, cw], writes=[Dw])
                for j in (1, 2):
                    S.op("dve", lambda e, n=n, g=g, j=j: e.scalar_tensor_tensor(out=Dw[:, 0:n], in0=Tt[:, j:j + n], scalar=cw[:, g, j:j + 1], in1=Dw[:, 0:n],
                                                                           op0=ALU.mult, op1=ALU.add), reads=[Tt, cw, Dw], writes=[Dw])
                S.op("dve", lambda e, c0=c0, n=n: e.tensor_tensor(out=Dw[:, 0:n], in0=Dw[:, 0:n], in1=bb_[:, c0:c0 + n], op=ALU.mult), reads=[Dw, bb_], writes=[Dw])
                S.op("dve", lambda e, c0=c0, n=n: e.tensor_tensor(out=cab[:, c0:c0 + n], in0=Dw[:, 0:n], in1=zb[:, c0:c0 + n], op=ALU.mult), reads=[Dw, zb], writes=[cab])
            ncol = NT if with_ctx else NL
            S.dma("sp", a_d[1024 + g * 128:1024 + (g + 1) * 128, 0:ncol], cab[:, 0:ncol], reads=[cab], writes=DBrow("a", 8 + g))

        if stop == "S6":
            raise _Stop()
        al = Alloc(AR)
        NH = NT // 2
        t1 = al.get(NH * 4, F32)
        t2 = al.get(NH * 4, F32)
        csT = al.get(NT * 4, F32)
        snT = al.get(NT * 4, F32)
        BTr = al.get(NT * 4, F32)
        BTi = al.get(NT * 4, F32)
        Yacc = al.get(NT * 4, F32)
        iot = al.get(NT * 4, F32)
        usb = al.get(NT * 2)
        usbb = al.get(NT * 2)
        sre = al.get(NT * 2)
        sim = al.get(NT * 2)
        tii = View(sre.t.bitcast(I32), sre.bufs)
        S.dma("sp", iot[:, :], iota_nt, writes=[iot])
        S.dma("sp", s_are[:, :], ssm_are[L], writes=[s_are])
        S.dma("sp", s_aim[:, :], ssm_aim[L], writes=[s_aim])
        S.dma("sp", s_ldt[:, :], ssm_ldt[L], writes=[s_ldt])
        S.op("act", lambda e: e.activation(out=s_dt[:, :], in_=s_ldt[:, :], func=AF.Exp), reads=[s_ldt], writes=[s_dt])
        S.op("dve", lambda e: e.tensor_tensor(out=s_x1[:, :], in0=s_are[:, :], in1=s_dt[:, :], op=ALU.mult), reads=[s_are, s_dt], writes=[s_x1])
        S.op("dve", lambda e: e.tensor_tensor(out=s_th[:, :], in0=s_aim[:, :], in1=s_dt[:, :], op=ALU.mult), reads=[s_aim, s_dt], writes=[s_th])
        S.op("act", lambda e: e.activation(out=s_rho[:, :], in_=s_x1[:, :], func=AF.Exp), reads=[s_x1], writes=[s_rho])

        def rr(dst, src, tmp, tint, n, shift=0.0):
            S.op("dve", lambda e: e.tensor_scalar(out=tmp[:, 0:n], in0=src[:, 0:n], scalar1=shift, scalar2=1.0 / TWO_PI, op0=ALU.add, op1=ALU.mult),
                 reads=[src], writes=[tmp])
            S.op("dve", lambda e: e.tensor_copy(out=tint[:, 0:n], in_=tmp[:, 0:n]), reads=[tmp], writes=[tint])
            S.op("dve", lambda e: e.tensor_copy(out=tmp[:, 0:n], in_=tint[:, 0:n]), reads=[tint], writes=[tmp])
            S.op("dve", lambda e: e.scalar_tensor_tensor(out=tmp[:, 0:n], in0=tmp[:, 0:n], scalar=-TWO_PI, in1=src[:, 0:n], op0=ALU.mult, op1=ALU.add),
                 reads=[tmp, src], writes=[tmp])
            if shift != 0.0:
                S.op("dve", lambda e: e.tensor_scalar(out=tmp[:, 0:n], in0=tmp[:, 0:n], scalar1=shift, scalar2=None, op0=ALU.add), reads=[tmp], writes=[tmp])
            S.op("dve", lambda e: e.tensor_scalar(out=dst[:, 0:n], in0=tmp[:, 0:n], scalar1=PI, scalar2=-TWO_PI, op0=ALU.is_gt, op1=ALU.mult), reads=[tmp], writes=[dst])
            S.op("dve", lambda e: e.tensor_tensor(out=tmp[:, 0:n], in0=tmp[:, 0:n], in1=dst[:, 0:n], op=ALU.add), reads=[tmp, dst], writes=[tmp])
            S.op("dve", lambda e: e.tensor_scalar(out=dst[:, 0:n], in0=tmp[:, 0:n], scalar1=-PI, scalar2=TWO_PI, op0=ALU.is_lt, op1=ALU.mult), reads=[tmp], writes=[dst])
            S.op("dve", lambda e: e.tensor_tensor(out=dst[:, 0:n], in0=tmp[:, 0:n], in1=dst[:, 0:n], op=ALU.add), reads=[tmp, dst], writes=[dst])

        rr(s_thr, s_th, s_t1, s_ti, 32)
        S.op("act", lambda e: e.activation(out=s_sn1[:, :], in_=s_thr[:, :], func=AF.Sin), reads=[s_thr], writes=[s_sn1])
        rr(s_t2, s_thr, s_t1, s_ti, 32, shift=PI / 2)
        S.op("act", lambda e: e.activation(out=s_cs1[:, :], in_=s_t2[:, :], func=AF.Sin), reads=[s_t2], writes=[s_cs1])
        S.op("dve", lambda e: e.tensor_tensor(out=s_nr[:, :], in0=s_rho[:, :], in1=s_cs1[:, :], op=ALU.mult), reads=[s_rho, s_cs1], writes=[s_nr])
        S.op("dve", lambda e: e.tensor_scalar(out=s_nr[:, :], in0=s_nr[:, :], scalar1=-1.0, scalar2=None, op0=ALU.add), reads=[s_nr], writes=[s_nr])
        S.op("dve", lambda e: e.tensor_tensor(out=s_ni[:, :], in0=s_rho[:, :], in1=s_sn1[:, :], op=ALU.mult), reads=[s_rho, s_sn1], writes=[s_ni])
        S.op("dve", lambda e: e.tensor_tensor(out=s_t1[:, :], in0=s_are[:, :], in1=s_are[:, :], op=ALU.mult), reads=[s_are], writes=[s_t1])
        S.op("dve", lambda e: e.tensor_tensor(out=s_t2[:, :], in0=s_aim[:, :], in1=s_aim[:, :], op=ALU.mult), reads=[s_aim], writes=[s_t2])
        S.op("dve", lambda e: e.tensor_tensor(out=s_t1[:, :], in0=s_t1[:, :], in1=s_t2[:, :], op=ALU.add), reads=[s_t1, s_t2], writes=[s_t1])
        S.op("dve", lambda e: e.reciprocal(out=s_den[:, :], in_=s_t1[:, :]), reads=[s_t1], writes=[s_den])
        S.op("dve", lambda e: e.tensor_tensor(out=s_t1[:, :], in0=s_nr[:, :], in1=s_are[:, :], op=ALU.mult), reads=[s_nr, s_are], writes=[s_t1])
        S.op("dve", lambda e: e.tensor_tensor(out=s_t2[:, :], in0=s_ni[:, :], in1=s_aim[:, :], op=ALU.mult), reads=[s_ni, s_aim], writes=[s_t2])
        S.op("dve", lambda e: e.tensor_tensor(out=s_t1[:, :], in0=s_t1[:, :], in1=s_t2[:, :], op=ALU.add), reads=[s_t1, s_t2], writes=[s_t1])
        S.op("dve", lambda e: e.tensor_tensor(out=s_fr[:, :], in0=s_t1[:, :], in1=s_den[:, :], op=ALU.mult), reads=[s_t1, s_den], writes=[s_fr])
        S.op("dve", lambda e: e.tensor_tensor(out=s_t1[:, :], in0=s_ni[:, :], in1=s_are[:, :], op=ALU.mult), reads=[s_ni, s_are], writes=[s_t1])
        S.op("dve", lambda e: e.tensor_tensor(out=s_t2[:, :], in0=s_nr[:, :], in1=s_aim[:, :], op=ALU.mult), reads=[s_nr, s_aim], writes=[s_t2])
        S.op("dve", lambda e: e.tensor_tensor(out=s_t1[:, :], in0=s_t1[:, :], in1=s_t2[:, :], op=ALU.subtract), reads=[s_t1, s_t2], writes=[s_t1])
        S.op("dve", lambda e: e.tensor_tensor(out=s_fi[:, :], in0=s_t1[:, :], in1=s_den[:, :], op=ALU.mult), reads=[s_t1, s_den], writes=[s_fi])

        HALVES = [(0, NH), (NH, NH)]
        for o in range(4):
            S.dma("sp", usb[:, 0:NCX], proj[5120 + o * 128:5120 + (o + 1) * 128, NL:NT], reads=[DB("proj", 40 + o, 8)], writes=[usb])
            S.dma("sp", usb[:, NCX:NT], proj[5120 + o * 128:5120 + (o + 1) * 128, 0:NL], reads=DBrow("proj", 40 + o), writes=[usb])
            S.dma("sp", usbb[:, :], proj[5120 + o * 128:5120 + (o + 1) * 128, :], reads=DBrow("proj", 40 + o), writes=[usbb])
            first = True
            for d in range(2):
                for jj in range(4):
                    u = d * 16 + o * 4 + jj
                    S.dma("sp", sB[:, :, :], ssm_B[L][:, u], writes=[sB])
                    S.dma("sp", sC[:, :, :], ssm_C[L][:, u], writes=[sC])
                    S.op("dve", lambda e, u=u: e.tensor_scalar(out=sT[:, :], in0=sB[:, 1, :], scalar1=s_fi[:, u:u + 1], scalar2=None, op0=ALU.mult), reads=[sB, s_fi], writes=[sT])
                    S.op("dve", lambda e, u=u: e.scalar_tensor_tensor(out=sBb[:, 0, :], in0=sB[:, 0, :], scalar=s_fr[:, u:u + 1], in1=sT[:, :], op0=ALU.mult, op1=ALU.subtract),
                         reads=[sB, s_fr, sT], writes=[sBb])
                    S.op("dve", lambda e, u=u: e.tensor_scalar(out=sT[:, :], in0=sB[:, 0, :], scalar1=s_fi[:, u:u + 1], scalar2=None, op0=ALU.mult), reads=[sB, s_fi], writes=[sT])
                    S.op("dve", lambda e, u=u: e.scalar_tensor_tensor(out=sBb[:, 1, :], in0=sB[:, 1, :], scalar=s_fr[:, u:u + 1], in1=sT[:, :], op0=ALU.mult, op1=ALU.add),
                         reads=[sB, s_fr, sT], writes=[sBb])
                    S.op("act", lambda e: e.activation(out=sCb[:, 0, :], in_=sC[:, 0, :], func=AF.Copy), reads=[sC], writes=[sCb])
                    S.op("act", lambda e: e.activation(out=sCb[:, 1, :], in_=sC[:, 1, :], func=AF.Copy, scale=-1.0), reads=[sC], writes=[sCb])
                    pbt = bankbf(0)
                    for ri in range(2):
                        S.op("pe", lambda e, ri=ri, pbt=pbt: e.transpose(out=pbt[:, ri * 128:(ri + 1) * 128], in_=sBb[:, ri, :], identity=ident[:, :]),
                             reads=[sBb, ident], writes=[pbt])
                    S.op("act", lambda e, pbt=pbt: e.activation(out=sWT[:, :, :], in_=pbt[:, 0:256].rearrange("p (a b) -> p a b", a=2), func=AF.Copy), reads=[pbt], writes=[sWT])
                    for (h0, hn) in HALVES:
                        S.op("dve", lambda e, u=u, h0=h0, hn=hn: e.tensor_scalar(out=t1[:, 0:hn], in0=iot[:, h0:h0 + hn], scalar1=s_thr[:, u:u + 1], scalar2=None, op0=ALU.mult),
                             reads=[iot, s_thr], writes=[t1])
                        rr(View(snT[:, h0:h0 + hn], snT.bufs), t1, t2, tii, hn)
                        rr(View(csT[:, h0:h0 + hn], csT.bufs), t1, t2, tii, hn, shift=PI / 2)
                    S.op("act", lambda e: e.activation(out=snT[:, :], in_=snT[:, :], func=AF.Sin), reads=[snT], writes=[snT])
                    S.op("act", lambda e: e.activation(out=csT[:, :], in_=csT[:, :], func=AF.Sin), reads=[csT], writes=[csT])
                    def rsl(buf, c0, n, d=d):
                        if d == 0:
                            return buf[:, c0:c0 + n]
                        st_ = NT - 1 - c0
                        sp_ = st_ - n
                        return buf[:, st_::-1] if sp_ < 0 else buf[:, st_:sp_:-1]
                    ub = usb if d == 0 else usbb
                    for c0 in range(0, NT, 512):
                        n = min(512, NT - c0)
                        p0 = bank(1)
                        p1 = bank(2)
                        S.op("pe", lambda e, c0=c0, n=n, p0=p0, ub=ub: e.matmul(p0[:, 0:n], lhsT=sWT[:, 0, :], rhs=ub[:, c0:c0 + n], start=True, stop=True), reads=[sWT, ub], writes=[p0])
                        S.op("pe", lambda e, c0=c0, n=n, p1=p1, ub=ub: e.matmul(p1[:, 0:n], lhsT=sWT[:, 1, :], rhs=ub[:, c0:c0 + n], start=True, stop=True), reads=[sWT, ub], writes=[p1])
                        ta, tb = stgf[0], stgf[1]
                        S.op("dve", lambda e, c0=c0, n=n, p0=p0, v=rsl(csT, c0, min(512, NT - c0)): e.tensor_tensor(out=ta[:, 0:n], in0=p0[:, 0:n], in1=v, op=ALU.mult), reads=[p0, csT], writes=[ta])
                        S.op("dve", lambda e, c0=c0, n=n, p1=p1, v=rsl(snT, c0, min(512, NT - c0)): e.tensor_tensor(out=tb[:, 0:n], in0=p1[:, 0:n], in1=v, op=ALU.mult), reads=[p1, snT], writes=[tb])
                        S.op("dve", lambda e, c0=c0, n=n: e.tensor_tensor(out=BTr[:, c0:c0 + n], in0=ta[:, 0:n], in1=tb[:, 0:n], op=ALU.add), reads=[ta, tb], writes=[BTr])
                        S.op("dve", lambda e, c0=c0, n=n, p1=p1, v=rsl(csT, c0, min(512, NT - c0)): e.tensor_tensor(out=ta[:, 0:n], in0=p1[:, 0:n], in1=v, op=ALU.mult), reads=[p1, csT], writes=[ta])
                        S.op("dve", lambda e, c0=c0, n=n, p0=p0, v=rsl(snT, c0, min(512, NT - c0)): e.tensor_tensor(out=tb[:, 0:n], in0=p0[:, 0:n], in1=v, op=ALU.mult), reads=[p0, snT], writes=[tb])
                        S.op("dve", lambda e, c0=c0, n=n: e.tensor_tensor(out=BTi[:, c0:c0 + n], in0=ta[:, 0:n], in1=tb[:, 0:n], op=ALU.subtract), reads=[ta, tb], writes=[BTi])
                    for BT in (BTr, BTi):
                        bv = BT[:, :] if d == 0 else BT[:, ::-1]
                        S.op("dve", lambda e, bv=bv, u=u: e.tensor_tensor_scan(out=bv, data0=s_rho[:, u:u + 1].to_broadcast([128, NT]), data1=bv, initial=0.0,
                                                                          op0=ALU.mult, op1=ALU.add), reads=[BT, s_rho], writes=[BT])
                    for (h0, hn) in HALVES:
                        cvh = rsl(csT, h0, hn)
                        svh = rsl(snT, h0, hn)
                        S.op("dve", lambda e, h0=h0, hn=hn, cvh=cvh: e.tensor_tensor(out=t1[:, 0:hn], in0=BTr[:, h0:h0 + hn], in1=cvh, op=ALU.mult), reads=[BTr, csT], writes=[t1])
                        S.op("dve", lambda e, h0=h0, hn=hn, svh=svh: e.tensor_tensor(out=t2[:, 0:hn], in0=BTi[:, h0:h0 + hn], in1=svh, op=ALU.mult), reads=[BTi, snT], writes=[t2])
                        S.op("dve", lambda e, h0=h0, hn=hn: e.tensor_tensor(out=sre[:, h0:h0 + hn], in0=t1[:, 0:hn], in1=t2[:, 0:hn], op=ALU.subtract), reads=[t1, t2], writes=[sre])
                        S.op("dve", lambda e, h0=h0, hn=hn, cvh=cvh: e.tensor_tensor(out=t1[:, 0:hn], in0=BTi[:, h0:h0 + hn], in1=cvh, op=ALU.mult), reads=[BTi, csT], writes=[t1])
                        S.op("dve", lambda e, h0=h0, hn=hn, svh=svh: e.tensor_tensor(out=t2[:, 0:hn], in0=BTr[:, h0:h0 + hn], in1=svh, op=ALU.mult), reads=[BTr, snT], writes=[t2])
                        S.op("dve", lambda e, h0=h0, hn=hn: e.tensor_tensor(out=sim[:, h0:h0 + hn], in0=t1[:, 0:hn], in1=t2[:, 0:hn], op=ALU.add), reads=[t1, t2], writes=[sim])
                    for c0 in range(0, NT, 512):
                        n = min(512, NT - c0)
                        p0 = bank(3)
                        yc0 = c0 if d == 0 else (NCX + c0 if c0 < NL else 0)
                        S.op("pe", lambda e, c0=c0, n=n, p0=p0: e.matmul(p0[:, 0:n], lhsT=sCb[:, 0, :], rhs=sre[:, c0:c0 + n], start=True, stop=False), reads=[sCb, sre], writes=[p0])
                        S.op("pe", lambda e, c0=c0, n=n, p0=p0: e.matmul(p0[:, 0:n], lhsT=sCb[:, 1, :], rhs=sim[:, c0:c0 + n], start=False, stop=True), reads=[sCb, sim], writes=[p0])
                        if first:
                            S.op("dve", lambda e, c0=yc0, n=n, p0=p0: e.tensor_copy(out=Yacc[:, c0:c0 + n], in_=p0[:, 0:n]), reads=[p0], writes=[Yacc])
                        else:
                            S.op("dve", lambda e, c0=yc0, n=n, p0=p0: e.tensor_tensor(out=Yacc[:, c0:c0 + n], in0=p0[:, 0:n], in1=Yacc[:, c0:c0 + n], op=ALU.add), reads=[p0, Yacc], writes=[Yacc])
                    first = False
            for (h0, hn) in HALVES:
                S.op("dve", lambda e, o=o, h0=h0, hn=hn: e.scalar_tensor_tensor(out=Yacc[:, h0:h0 + hn], in0=usb[:, h0:h0 + hn], scalar=sdk[:, o:o + 1], in1=Yacc[:, h0:h0 + hn],
                                                                         op0=ALU.mult, op1=ALU.add), reads=[usb, sdk, Yacc], writes=[Yacc])
                S.op("dve", lambda e, h0=h0, hn=hn: e.tensor_tensor(out=t1[:, 0:hn], in0=Yacc[:, h0:h0 + hn], in1=Yacc[:, h0:h0 + hn], op=ALU.mult), reads=[Yacc], writes=[t1])
                S.op("dve", lambda e, hn=hn: e.tensor_scalar(out=t1[:, 0:hn], in0=t1[:, 0:hn], scalar1=0.044715, scalar2=1.0, op0=ALU.mult, op1=ALU.add), reads=[t1], writes=[t1])
                S.op("dve", lambda e, h0=h0, hn=hn: e.tensor_tensor(out=t1[:, 0:hn], in0=t1[:, 0:hn], in1=Yacc[:, h0:h0 + hn], op=ALU.mult), reads=[t1, Yacc], writes=[t1])
                S.op("act", lambda e, hn=hn: e.activation(out=t2[:, 0:hn], in_=t1[:, 0:hn], func=AF.Sigmoid, scale=float(2.0 * np.sqrt(2.0 / np.pi))), reads=[t1], writes=[t2])
                S.op("dve", lambda e, h0=h0, hn=hn: e.tensor_tensor(out=sre[:, h0:h0 + hn], in0=t2[:, 0:hn], in1=Yacc[:, h0:h0 + hn], op=ALU.mult), reads=[t2, Yacc], writes=[sre])
            S.dma("sp", yss[o * 128:(o + 1) * 128, NL:NT], sre[:, 0:NCX], reads=[sre], writes=[DB("yss", o, 1)])
            S.dma("sp", yss[o * 128:(o + 1) * 128, 0:NL], sre[:, NCX:NT], reads=[sre], writes=[DB("yss", o, 0)])
        al = Alloc(AR)
        gw = al.get(4 * 1024 * 2, BF16, "p (k n) -> p k n", k=4)
        YT = [al.get(4 * 512 * 2, BF16, "p (k n) -> p k n", k=4) for _ in range(2)]
        ZT = [al.get(4 * 512 * 2, BF16, "p (k n) -> p k n", k=4) for _ in range(2)]
        S.dma("pool", gw[:, :, :], glu_w[L].rearrange("(k p) n -> p k n", p=128), writes=[gw])
        tiles7 = TILES if with_ctx else TILES[:8]
        for ti, (c0, n) in enumerate(tiles7):
            yt = YT[ti % 2]
            zt = ZT[ti % 2]
            S.dma("sp", yt[:, :, 0:n], yss.rearrange("(k p) n -> p k n", p=128)[:, :, c0:c0 + n], reads=[DB("yss", oo, hh) for oo in range(4) for hh in range(2)], writes=[yt])
            S.dma("sp", zt[:, :, 0:n], proj[5632:6144, :].rearrange("(k p) n -> p k n", p=128)[:, :, c0:c0 + n], reads=[DB("proj", 44 + oo, ti) for oo in range(4)], writes=[zt])
            for ob in range(4):
                pa = next_bank()
                for k in range(4):
                    S.op("pe", lambda e, k=k, ob=ob, pa=pa, yt=yt, n=n: e.matmul(pa[:, 0:n], lhsT=gw[:, k, ob * 128:(ob + 1) * 128], rhs=yt[:, k, 0:n], start=(k == 0), stop=(k == 3)),
                         reads=[gw, yt], writes=[pa])
                pg = next_bank()
                for k in range(4):
                    S.op("pe", lambda e, k=k, ob=ob, pg=pg, yt=yt, n=n: e.matmul(pg[:, 0:n], lhsT=gw[:, k, 512 + ob * 128:512 + (ob + 1) * 128], rhs=yt[:, k, 0:n], start=(k == 0), stop=(k == 3)),
                         reads=[gw, yt], writes=[pg])
                S.op("act", lambda e, pg=pg, n=n: e.activation(out=stgf[2][:, 0:n], in_=pg[:, 0:n], func=AF.Sigmoid), reads=[pg], writes=[stgf[2]])
                S.op("dve", lambda e, pa=pa, n=n: e.tensor_tensor(out=stgf[2][:, 0:n], in0=pa[:, 0:n], in1=stgf[2][:, 0:n], op=ALU.mult), reads=[pa, stgf[2]], writes=[stgf[2]])
                st = next_stg()
                S.op("dve", lambda e, st=st, zt=zt, ob=ob, n=n: e.tensor_tensor(out=st[:, 0:n], in0=stgf[2][:, 0:n], in1=zt[:, ob, 0:n], op=ALU.mult), reads=[stgf[2], zt], writes=[st])
                S.dma("sp", a_d[1536 + ob * 128:1536 + (ob + 1) * 128, c0:c0 + n], st[:, 0:n], reads=[st], writes=[DB("a", 12 + ob, ti)])

        if stop == "S7":
            raise _Stop()
        al = Alloc(AR)
        wbr = al.get(KT * D * 2, BF16, "p (k n) -> p k n", k=KT)
        AT = [al.get(KT * 512 * 2, BF16, "p (k n) -> p k n", k=KT) for _ in range(2)]
        GT = [al.get(4 * 512 * 2, BF16, "p (i n) -> p i n", i=4) for _ in range(2)]
        MT = al.get(KT * 512 * 2, BF16, "p (k n) -> p k n", k=KT)
        for half in range(2):
            src = w_br[L].rearrange("(k p) n -> p k n", p=128)
            for kk in range(0, KT, 4):
                S.dma("pool", wbr[:, kk:kk + 4, half * 1024:(half + 1) * 1024], src[:, kk:kk + 4, half * 1024:(half + 1) * 1024], writes=[wbr])
        tiles8 = TILES if with_ctx else TILES[:8]
        gcnt = 0
        for ti, (c0, n) in enumerate(tiles8):
            at = AT[ti % 2]
            S.dma("sp", at[:, :, 0:n], a_d.rearrange("(k p) n -> p k n", p=128)[:, :, c0:c0 + n],
                  reads=[DB("a", rb, t2) for rb in range(16) for t2 in range(9)], writes=[at])
            for j in range(16):
                gt = GT[gcnt % 2]
                gcnt += 1
                S.dma("sp", gt[:, :, 0:n], proj[6144:14336, :].rearrange("(i j p) n -> j p i n", i=4, j=16)[j][:, :, c0:c0 + n],
                      reads=[DB("proj", 48 + i * 16 + j, ti) for i in range(4)], writes=[gt])
                acc = stgf[0]
                tmpf = stgf[1]
                for i in range(4):
                    pb = next_bank()
                    for kk in range(4):
                        S.op("pe", lambda e, pb=pb, i=i, kk=kk, j=j, at=at, n=n: e.matmul(pb[:, 0:n], lhsT=wbr[:, i * 4 + kk, j * 128:(j + 1) * 128],
                                                                                  rhs=at[:, i * 4 + kk, 0:n], start=(kk == 0), stop=(kk == 3)),
                             reads=[wbr, at], writes=[pb])
                    if i == 0:
                        S.op("dve", lambda e, pb=pb, gt=gt, n=n: e.tensor_tensor(out=acc[:, 0:n], in0=pb[:, 0:n], in1=gt[:, 0, 0:n], op=ALU.mult),
                             reads=[pb, gt], writes=[acc])
                    else:
                        S.op("dve", lambda e, pb=pb, gt=gt, n=n, i=i: e.tensor_tensor(out=tmpf[:, 0:n], in0=pb[:, 0:n], in1=gt[:, i, 0:n], op=ALU.mult),
                             reads=[pb, gt], writes=[tmpf])
                        if i < 3:
                            S.op("dve", lambda e, n=n: e.tensor_tensor(out=acc[:, 0:n], in0=acc[:, 0:n], in1=tmpf[:, 0:n], op=ALU.add), reads=[acc, tmpf], writes=[acc])
                        else:
                            S.op("dve", lambda e, n=n, j=j: e.tensor_tensor(out=MT[:, j, 0:n], in0=acc[:, 0:n], in1=tmpf[:, 0:n], op=ALU.add), reads=[acc, tmpf], writes=[MT])
            S.dma("sp", mg.rearrange("(k p) n -> p k n", p=128)[:, :, c0:c0 + n], MT[:, :, 0:n], reads=[MT], writes=[DB("mg", 0, ti)])

        if stop == "S8":
            raise _Stop()
        al = Alloc(AR)
        wo = al.get(KT * D * 2, BF16, "p (k n) -> p k n", k=KT)
        MT2 = [al.get(KT * 256 * 2, BF16, "p (k n) -> p k n", k=KT) for _ in range(2)]
        Y = al.get(KT * 256 * 4, F32, "p (k n) -> p k n", k=KT)
        SQ = al.get(KT * 256 * 2, BF16, "p (k n) -> p k n", k=KT)
        XT = al.get(KT * 256 * 4, F32, "p (k n) -> p k n", k=KT)
        for half in range(2):
            src = w_o[L].rearrange("(k p) n -> p k n", p=128)
            for kk in range(0, KT, 4):
                S.dma("pool", wo[:, kk:kk + 4, half * 1024:(half + 1) * 1024], src[:, kk:kk + 4, half * 1024:(half + 1) * 1024], writes=[wo])
        ntok = NT if with_ctx else NL
        for qi, c0 in enumerate(range(0, ntok, 256)):
            n = 256
            ti = min(c0 // 512, 8)
            s = 0 if c0 < NL else 1
            mt = MT2[qi % 2]
            S.dma("sp", mt[:, :, :], mg.rearrange("(k p) n -> p k n", p=128)[:, :, c0:c0 + n], reads=[DB("mg", 0, ti)], writes=[mt])
            S.dma("sp", XT[:, :, :], xsrc.rearrange("(k p) n -> p k n", p=128)[:, :, c0:c0 + n], reads=[DB(xsn, 0, ti)], writes=[XT])
            for j in range(16):
                pb = next_bank()
                for k in range(KT):
                    S.op("pe", lambda e, pb=pb, k=k, j=j, mt=mt: e.matmul(pb[:, 0:256], lhsT=wo[:, k, j * 128:(j + 1) * 128], rhs=mt[:, k, :],
                                                                     start=(k == 0), stop=(k == KT - 1)), reads=[wo, mt], writes=[pb])
                S.op("dve", lambda e, pb=pb, j=j: e.tensor_copy(out=Y[:, j, :], in_=pb[:, 0:256]), reads=[pb], writes=[Y])
                S.op("act", lambda e, j=j: e.activation(out=SQ[:, j, :], in_=Y[:, j, :], func=AF.Square), reads=[Y], writes=[SQ])
            if stop == "S9a":
                raise _Stop()
            pb = next_bank()
            for k in range(KT):
                S.op("pe", lambda e, k=k, pb=pb: e.matmul(pb[:, 0:256], lhsT=ones[:, :], rhs=SQ[:, k, :], start=(k == 0), stop=(k == KT - 1)),
                     reads=[ones, SQ], writes=[pb])
            S.op("act", lambda e, pb=pb: e.activation(out=rtmp[:, 0:256], in_=pb[:, 0:256], func=AF.Sqrt, bias=epsb[:, 0:1], scale=1.0 / D),
                 reads=[pb, epsb], writes=[rtmp])
            S.op("dve", lambda e: e.reciprocal(out=rstd[:, 0:256], in_=rtmp[:, 0:256]), reads=[rtmp], writes=[rstd])
            S.op("dve", lambda e: e.tensor_tensor(out=Y[:, :, :], in0=Y[:, :, :], in1=rstd[:, 0:256].unsqueeze(1).to_broadcast([128, KT, 256]), op=ALU.mult),
                 reads=[Y, rstd], writes=[Y])
            if stop == "S9b":
                raise _Stop()
            for k in range(KT):
                S.op("dve", lambda e, k=k, s=s: e.scalar_tensor_tensor(out=XT[:, k, :], in0=Y[:, k, :], scalar=Gmod[:, k, s:s + 1], in1=XT[:, k, :],
                                                                     op0=ALU.mult, op1=ALU.add), reads=[Y, Gmod, XT], writes=[XT])
            if last:
                S.dma("sp", outT.rearrange("(k p) n -> p k n", p=128)[:, :, c0:c0 + n], XT[:, :, :], reads=[XT], writes=[DB("out", 0, qi)])
            else:
                S.dma("sp", xs.rearrange("(k p) n -> p k n", p=128)[:, :, c0:c0 + n], XT[:, :, :], reads=[XT], writes=[DB("xs", 0, ti)])
            if stop == "S9c":
                raise _Stop()
      except _Stop:
        break

    finals = [o for o in S.qops["sp"] if o.dma][-40:]
    S.emit(final_ops=finals)
    return nc


def _fm(v, nb):
    return np.ascontiguousarray(np.swapaxes(v.reshape(v.shape[:-1] + (nb, 128)), -1, -2))


def _na_bias(rpb):
    pats = [(0, 0), (2, 0), (8, 4), (60, 54), (62, 54)]
    drow = np.zeros((5, 128, 640), np.int64)
    dcol = np.zeros((5, 128, 640), np.int64)
    mask = np.zeros((5, 128, 640), bool)
    qc = np.arange(64)
    kc = np.arange(64)
    cs = np.clip(qc - 8, 0, 48)
    inwin = (kc[None, :] >= cs[:, None]) & (kc[None, :] < cs[:, None] + 16)
    dc = np.clip(kc[None, :] - qc[:, None] + 15, 0, 30)
    for pi, (r0, lo) in enumerate(pats):
        for dr in range(2):
            r = r0 + dr
            bs = min(max(r - 4, 0), 56)
            for ko in range(10):
                kr = lo + ko
                inb = bs <= kr < bs + 8
                drw = min(max(kr - r + 7, 0), 14)
                drow[pi, dr * 64:(dr + 1) * 64, ko * 64:(ko + 1) * 64] = drw
                dcol[pi, dr * 64:(dr + 1) * 64, ko * 64:(ko + 1) * 64] = dc
                mask[pi, dr * 64:(dr + 1) * 64, ko * 64:(ko + 1) * 64] = inwin & inb
    g = rpb[:, :, drow, dcol]
    g = np.where(mask[None, None], g, np.float32(-30000.0)).astype(np.float32)
    return np.ascontiguousarray(g.transpose(0, 1, 3, 2, 4))


def _pool_edge():
    out = np.zeros((128, 2, 4, 2, 8), np.float32)
    for si, n in enumerate((NL, NCX)):
        for g, w in enumerate((2, 4, 8, 16)):
            t = np.arange(n)
            lo = np.clip(t - w // 2, 0, n)
            hi = np.clip(t + w - w // 2, 0, n)
            inv = (1.0 / (hi - lo)).astype(np.float32)
            out[:, si, g, 0, :] = inv[None, 0:8]
            out[:, si, g, 1, :] = inv[None, n - 8:n]
    return out


_NC_CACHE = {}


def _prep(x, c, ctx, c_ctx, w_mod, b_mod, g_pre, g_post, w_in, b_gate, na_rpb, pool_w,
           pool_scale, conv_w, ssm_a_re, ssm_a_im, ssm_log_dt, ssm_b_re, ssm_b_im,
           ssm_c_re, ssm_c_im, ssm_d, glu_w, w_br, w_o):
    f = np.float32
    x = np.asarray(x, f); ctx = np.asarray(ctx, f); c = np.asarray(c, f); c_ctx = np.asarray(c_ctx, f)
    shared = {
        "w_mod": np.ascontiguousarray(w_mod, f),
        "b_mod": _fm(np.asarray(b_mod, f), 48),
        "g_pre": _fm(np.asarray(g_pre, f), 16),
        "g_post": _fm(np.asarray(g_post, f), 16),
        "w_in": np.ascontiguousarray(w_in, f),
        "b_gate": _fm(np.asarray(b_gate, f), 64),
        "na_bias": _na_bias(np.asarray(na_rpb, f)),
        "pool_w": np.ascontiguousarray(np.asarray(pool_w, f).transpose(0, 2, 1, 3)),
        "pool_sc": _fm(np.asarray(pool_scale, f), 4),
        "pool_edge": _pool_edge(),
        "conv_w": np.ascontiguousarray(np.asarray(conv_w, f).reshape(DEPTH, 3, 4, 128).transpose(0, 3, 2, 1)),
        "ssm_d": _fm(np.asarray(ssm_d, f), 4),
        "glu_w": np.ascontiguousarray(glu_w, f),
        "w_br": np.ascontiguousarray(w_br, f),
        "w_o": np.ascontiguousarray(w_o, f),
        "ident": np.eye(128, dtype=f),
        "iota17": np.tile(np.arange(17, dtype=f)[None], (128, 1)),
        "iota_nt": np.tile(np.arange(NT, dtype=f)[None], (128, 1)),
    }

    def upl(a):
        a = np.asarray(a, f).reshape(DEPTH, 2, 16, 2, 64)
        return np.ascontiguousarray(a.transpose(0, 3, 4, 1, 2).reshape(DEPTH, 128, 32))

    shared["ssm_are"] = upl(ssm_a_re)
    shared["ssm_aim"] = upl(ssm_a_im)
    shared["ssm_ldt"] = upl(np.broadcast_to(np.asarray(ssm_log_dt, f)[..., None], (DEPTH, 2, 32, 64)))
    Bp = np.zeros((DEPTH, 128, 32, 2, 128), f)
    Cp = np.zeros((DEPTH, 128, 32, 2, 128), f)
    bre = np.asarray(ssm_b_re, f); bim = np.asarray(ssm_b_im, f)
    cre = np.asarray(ssm_c_re, f); cim = np.asarray(ssm_c_im, f)
    for d in range(2):
        for j in range(16):
            for gl in range(2):
                g = 2 * j + gl
                u = d * 16 + j
                ch0 = (j % 4) * 32 + gl * 16
                Bp[:, gl * 64:(gl + 1) * 64, u, 0, ch0:ch0 + 16] = bre[:, d, g]
                Bp[:, gl * 64:(gl + 1) * 64, u, 1, ch0:ch0 + 16] = bim[:, d, g]
                Cp[:, gl * 64:(gl + 1) * 64, u, 0, ch0:ch0 + 16] = cre[:, d, g].transpose(0, 2, 1)
                Cp[:, gl * 64:(gl + 1) * 64, u, 1, ch0:ch0 + 16] = cim[:, d, g].transpose(0, 2, 1)
    shared["ssm_B"] = Bp
    shared["ssm_C"] = Cp
    in_maps = []
    for core in range(8):
        b = core % 4
        m = dict(shared)
        m["x0"] = np.ascontiguousarray(np.concatenate([x[b].T, ctx[b].T], axis=1))
        cv = np.stack([c[b].reshape(16, 128).T, c_ctx.reshape(16, 128).T], axis=-1)
        m["cvec"] = np.ascontiguousarray(cv, f)
        in_maps.append(m)
    return in_maps


def kernel(x, c, ctx, c_ctx, w_mod, b_mod, g_pre, g_post, w_in, b_gate, na_rpb, pool_w,
           pool_scale, conv_w, ssm_a_re, ssm_a_im, ssm_log_dt, ssm_b_re, ssm_b_im,
           ssm_c_re, ssm_c_im, ssm_d, glu_w, w_br, w_o):
    in_maps = _prep(x, c, ctx, c_ctx, w_mod, b_mod, g_pre, g_post, w_in, b_gate, na_rpb, pool_w,
                    pool_scale, conv_w, ssm_a_re, ssm_a_im, ssm_log_dt, ssm_b_re, ssm_b_im,
                    ssm_c_re, ssm_c_im, ssm_d, glu_w, w_br, w_o)
    if "nc" not in _NC_CACHE:
        _NC_CACHE["nc"] = build_nc()
    res = run_bass_kernel_spmd(_NC_CACHE["nc"], in_maps, core_ids=list(range(8)))
    out = np.stack([np.ascontiguousarray(res.results[b]["outT"].T) for b in range(4)], axis=0)
    return out.astype(np.float32)
```

```python
import numpy as np
import concourse.bass as bass
import concourse.mybir as mybir
from concourse.bass_utils import run_bass_kernel_spmd

F32 = mybir.dt.float32
BF16 = mybir.dt.bfloat16
I32 = mybir.dt.int32
ALU = mybir.AluOpType
AF = mybir.ActivationFunctionType

DEPTH = 4
D = 2048
NL = 4096
NCX = 256
NT = NL + NCX
KT = 16
TILES = [(i * 512, 512) for i in range(8)] + [(4096, 256)]
TWO_PI = float(2 * np.pi)
PI = float(np.pi)
MAGIC = 12582912.0

COMPUTE_Q = ("pe", "act", "dve", "pool")
EPOCH = 20000
NDMA_SEM = 12


class Buf:
    __slots__ = ("name", "last_w", "readers", "t", "bufs")

    def __init__(self, name, t=None):
        self.name = name
        self.last_w = None
        self.readers = []
        self.t = t
        self.bufs = [self]

    def __getitem__(self, k):
        return self.t[k]


class View:
    __slots__ = ("t", "bufs")

    def __init__(self, t, bufs):
        self.t = t
        self.bufs = bufs

    def __getitem__(self, k):
        return self.t[k]


class Op:
    __slots__ = ("q", "fn", "deps", "dma", "sig", "sem", "val", "idx", "prev_same_sem", "seqno")

    def __init__(self, q, fn, dma):
        self.q = q
        self.fn = fn
        self.deps = []
        self.dma = dma
        self.sig = dma
        self.sem = None
        self.val = None
        self.idx = -1
        self.prev_same_sem = None


def _expand(lst):
    out = []
    for b in lst:
        out.extend(b.bufs)
    return out


class Sched:
    def __init__(self, nc):
        self.nc = nc
        self.qops = {q: [] for q in ("pe", "act", "dve", "pool", "sp")}

    def sb(self, name, shape, dtype):
        return Buf(name, self.nc.alloc_sbuf_tensor(name, list(shape), dtype))

    def _add(self, q, fn, reads, writes, dma):
        op = Op(q, fn, dma)
        self.seq = getattr(self, "seq", 0) + 1
        op.idx = self.seq
        reads = _expand(reads)
        writes = _expand(writes)
        deps = []
        for b in reads:
            if b.last_w is not None:
                deps.append((b.last_w, "raw"))
        for b in writes:
            if b.last_w is not None:
                deps.append((b.last_w, "waw"))
            for r in b.readers:
                deps.append((r, "war"))
        seen = set()
        best = {}
        for d, kind in deps:
            if d is op or id(d) in seen:
                continue
            seen.add(id(d))
            if d.q == q and not d.dma and not dma:
                if q == "pe" or kind == "war":
                    continue
            if d.dma:
                op.deps.append(d)
                d.sig = True
            else:
                cur = best.get(d.q)
                if cur is None or d.seqno > cur.seqno:
                    best[d.q] = d
        for d in best.values():
            op.deps.append(d)
            d.sig = True
        op.seqno = self.seq
        for b in reads:
            if not dma:
                b.readers = [r for r in b.readers if r.dma or r.q != q]
            b.readers.append(op)
        for b in writes:
            b.last_w = op
            b.readers = []
        self.qops[q].append(op)
        return op

    def op(self, q, fn, reads=(), writes=()):
        return self._add(q, fn, reads, writes, False)

    def dma(self, q, out, in_, reads=(), writes=()):
        return self._add(q, lambda e: e.dma_start(out=out, in_=in_), reads, writes, True)

    def emit(self, final_ops=()):
        nc = self.nc
        csem = {}
        for q in COMPUTE_Q:
            n = sum(1 for o in self.qops[q] if o.sig and not o.dma)
            ne = max(1, (n + EPOCH - 1) // EPOCH)
            csem[q] = [nc.alloc_semaphore(name=f"s_{q}{i}") for i in range(ne)]
        dsem = {}
        for q in ("sp", "act", "pool"):
            if any(o.dma for o in self.qops[q]):
                dsem[q] = [nc.alloc_semaphore(name=f"d_{q}{i}") for i in range(NDMA_SEM)]
        for q, lst in self.qops.items():
            cnt = 0
            dcnt = 0
            last_on_sem = {}
            for o in lst:
                if o.dma:
                    k = dcnt % NDMA_SEM
                    o.sem = dsem[q][k]
                    o.val = 16 * (dcnt // NDMA_SEM + 1)
                    o.prev_same_sem = last_on_sem.get(k)
                    last_on_sem[k] = o
                    dcnt += 1
                elif o.sig:
                    o.sem = csem[q][cnt // EPOCH]
                    o.val = cnt % EPOCH + 1
                    o.idx = cnt
                    cnt += 1
        engs = {"pe": "tensor", "act": "scalar", "dve": "vector", "pool": "gpsimd", "sp": "sync"}
        with nc.Block() as block:
            for q, lst in self.qops.items():
                if not lst:
                    continue

                def body(e, q=q, lst=lst):
                    waited = {}
                    dwaited = set()

                    def wait_for(d):
                        if d.dma:
                            if id(d) in dwaited:
                                return
                            dwaited.add(id(d))
                            e.wait_ge(d.sem, d.val)
                        else:
                            if waited.get(d.q, -1) >= d.idx:
                                return
                            waited[d.q] = d.idx
                            e.wait_ge(d.sem, d.val)

                    for o in lst:
                        for d in o.deps:
                            wait_for(d)
                        if o.dma and o.prev_same_sem is not None:
                            wait_for(o.prev_same_sem)
                        ins = o.fn(e)
                        if o.dma:
                            ins.then_inc(o.sem, 16)
                        elif o.sig:
                            ins.then_inc(o.sem, 1)
                    if q == "sp":
                        for d in final_ops:
                            wait_for(d)

                getattr(block, engs[q])(body)


CH = 512
NCHUNK = 164


class Arena:
    def __init__(self, S):
        self.t = S.nc.alloc_sbuf_tensor("arena", [128, CH * NCHUNK], BF16)
        self.chunks = [Buf(f"ch{i}") for i in range(NCHUNK)]

    def view(self, off_b, nbytes, dtype=BF16, pat=None, **kw):
        assert off_b % 4 == 0 and off_b + nbytes <= CH * NCHUNK * 2, (off_b, nbytes)
        e0 = off_b // 2
        e1 = (off_b + nbytes) // 2
        ap = self.t[:, e0:e1]
        if dtype != BF16:
            ap = ap.bitcast(dtype)
        if pat is not None:
            ap = ap.rearrange(pat, **kw)
        c0 = e0 // CH
        c1 = (e1 - 1) // CH
        return View(ap, self.chunks[c0:c1 + 1])


class Alloc:
    def __init__(self, arena):
        self.a = arena
        self.off = 0

    def get(self, nbytes, dtype=BF16, pat=None, **kw):
        self.off = (self.off + 1023) // 1024 * 1024
        v = self.a.view(self.off, nbytes, dtype, pat, **kw)
        self.off += nbytes
        return v


class _Stop(Exception):
    pass


def build_nc(depth=DEPTH, dbg=None, stop=None):
    nc = bass.Bass("TRN2", target_bir_lowering=False)
    S = Sched(nc)
    AR = Arena(S)

    def din(name, shape, dt=F32):
        return nc.dram_tensor(name, list(shape), dt, kind="ExternalInput").ap()

    def dscr(name, shape, dt):
        return nc.dram_tensor(name, list(shape), dt, kind=("ExternalOutput" if dbg else "Internal")).ap()

    x0 = din("x0", [D, NT])
    cvec = din("cvec", [128, KT, 2])
    w_mod = din("w_mod", [DEPTH, D, 3 * D])
    b_mod = din("b_mod", [DEPTH, 128, 48])
    g_pre = din("g_pre", [DEPTH, 128, KT])
    g_post = din("g_post", [DEPTH, 128, KT])
    w_in = din("w_in", [DEPTH, D, 14336])
    b_gate = din("b_gate", [DEPTH, 128, 64])
    na_bias = din("na_bias", [DEPTH, 8, 128, 5, 640])
    pool_w = din("pool_w", [DEPTH, 128, 4, 128])
    pool_sc = din("pool_sc", [DEPTH, 128, 4])
    pool_edge = din("pool_edge", [128, 2, 4, 2, 8])
    conv_w = din("conv_w", [DEPTH, 128, 4, 3])
    ssm_are = din("ssm_are", [DEPTH, 128, 32])
    ssm_aim = din("ssm_aim", [DEPTH, 128, 32])
    ssm_ldt = din("ssm_ldt", [DEPTH, 128, 32])
    ssm_B = din("ssm_B", [DEPTH, 128, 32, 2, 128])
    ssm_C = din("ssm_C", [DEPTH, 128, 32, 2, 128])
    ssm_d = din("ssm_d", [DEPTH, 128, 4])
    glu_w = din("glu_w", [DEPTH, 512, 1024])
    w_br = din("w_br", [DEPTH, D, D])
    w_o = din("w_o", [DEPTH, D, D])
    ident_in = din("ident", [128, 128])
    iota17 = din("iota17", [128, 17])
    iota_nt = din("iota_nt", [128, NT])
    outT = nc.dram_tensor("outT", [D, NL], F32, kind="ExternalOutput").ap()

    xs = dscr("xs", [D, NT], F32)
    hT = dscr("hT", [D, NT], BF16)
    proj = dscr("proj", [14336, NT], BF16)
    vT = dscr("vT", [NT, 512], BF16)
    a_d = dscr("a_d", [D, NT], BF16)
    mg = dscr("mg", [D, NT], BF16)
    yss = dscr("yss", [512, NT], BF16)
    dbufs = {}
    dbg_pob = nc.dram_tensor("dbg_pob", [4, 128, NL], BF16, kind="ExternalOutput").ap() if dbg else None

    def DB(name, rb, ti):
        k = (name, rb, ti)
        if k not in dbufs:
            dbufs[k] = Buf(str(k))
        return dbufs[k]

    def DBrow(name, rb):
        return [DB(name, rb, ti) for ti in range(9)]

    NTI = len(TILES)

    ident = S.sb("identb", [128, 128], BF16)
    ones = S.sb("onesb", [128, 128], BF16)
    epsb = S.sb("epsb", [128, 1], F32)
    cact = S.sb("cact", [128, KT, 2], BF16)
    cv32 = S.sb("cv32", [128, KT, 2], F32)
    modt = S.sb("modt", [128, 48, 2], F32)
    bmod = S.sb("bmod", [128, 48], F32)
    gpre = S.sb("gpre", [128, KT], F32)
    gpost = S.sb("gpost", [128, KT], F32)
    Amod = S.sb("Amod", [128, KT, 2], F32)
    Gmod = S.sb("Gmod", [128, KT, 2], F32)
    bgate = S.sb("bgate", [128, 64], F32)
    psc = S.sb("psc", [128, 4], F32)
    pedge = S.sb("pedge", [128, 2, 4, 2, 8], F32)
    cw = S.sb("cw", [128, 4, 3], F32)
    sdk = S.sb("sdk", [128, 4], F32)
    io17 = S.sb("io17", [128, 17], F32)
    (s_are, s_aim, s_ldt, s_dt, s_x1, s_th, s_rho, s_thr, s_sn1, s_cs1, s_nr, s_ni, s_den, s_fr, s_fi, s_t1, s_t2) = [
        S.sb(f"ss{i}", [128, 32], F32) for i in range(17)]
    s_ti = S.sb("ssti", [128, 32], I32)
    s_thq = S.sb("ssthq", [128, 32], F32)
    halfpi = S.sb("halfpi", [128, 1], F32)
    sB = S.sb("sB", [128, 2, 128], F32)
    sC = S.sb("sC", [128, 2, 128], F32)
    sT = S.sb("sT", [128, 128], F32)
    sBb = S.sb("sBb", [128, 2, 128], BF16)
    sCb = S.sb("sCb", [128, 2, 128], BF16)
    sWT = S.sb("sWT", [128, 2, 128], BF16)
    stg = [S.sb(f"stg{i}", [128, 512], BF16) for i in range(4)]
    stgf = [S.sb(f"stgf{i}", [128, 512], F32) for i in range(3)]
    rstd = S.sb("rstd", [128, 512], F32)
    rtmp = S.sb("rtmp", [128, 512], F32)

    PSB = nc.alloc_psum_tensor("psall", [128, 8, 512], F32)
    BK = [Buf(f"bank{i}") for i in range(8)]

    def bank(i):
        return View(PSB[:, i, :], [BK[i]])

    def bank2(i):
        return View(PSB[:, i:i + 2, :].rearrange("p a b -> p (a b)"), [BK[i], BK[i + 1]])

    def bankbf(i):
        return View(PSB[:, i, :].bitcast(BF16), [BK[i]])

    S.dma("pool", ident[:, :], ident_in, writes=[ident])
    S.op("dve", lambda e: e.memset(ones[:, :], 1.0), writes=[ones])
    S.op("dve", lambda e: e.memset(epsb[:, :], 1e-6), writes=[epsb])
    S.op("dve", lambda e: e.memset(halfpi[:, :], PI / 2), writes=[halfpi])
    S.dma("sp", cv32[:, :, :], cvec, writes=[cv32])
    S.op("act", lambda e: e.activation(out=cact[:, :, :], in_=cv32[:, :, :], func=AF.Silu), reads=[cv32], writes=[cact])
    S.dma("sp", pedge[:, :, :, :, :], pool_edge, writes=[pedge])
    S.dma("sp", io17[:, :], iota17, writes=[io17])

    stg_i = [0]

    def next_stg():
        stg_i[0] += 1
        return stg[stg_i[0] % 4]

    bank_i = [0]

    def next_bank(n=4):
        bank_i[0] += 1
        return bank(bank_i[0] % n)

    def w_panel_view(al):
        return al.get(KT * 1024 * 2, BF16, "p (k n) -> p k n", k=KT)

    def load_w_panel(dst, src_l, col0, ncols=1024):
        src = src_l.rearrange("(k p) n -> p k n", p=128)
        for kk in range(0, KT, 4):
            S.dma("pool", dst[:, kk:kk + 4, 0:ncols], src[:, kk:kk + 4, col0:col0 + ncols], writes=[dst])

    for L in range(depth):
      try:
        xsrc = x0 if L == 0 else xs
        xsn = "x0" if L == 0 else "xs"
        last = (L == depth - 1) and not dbg
        with_ctx = (L < DEPTH - 1) and not last
        al = Alloc(AR)
        WP = [w_panel_view(al), w_panel_view(al)]
        S.dma("sp", bmod[:, :], b_mod[L], writes=[bmod])
        S.dma("sp", gpre[:, :], g_pre[L], writes=[gpre])
        S.dma("sp", gpost[:, :], g_post[L], writes=[gpost])
        S.dma("sp", bgate[:, :], b_gate[L], writes=[bgate])
        S.dma("sp", psc[:, :], pool_sc[L], writes=[psc])
        S.dma("sp", cw[:, :, :], conv_w[L], writes=[cw])
        S.dma("sp", sdk[:, :], ssm_d[L], writes=[sdk])
        for pp in range(6):
            wp = WP[pp % 2]
            load_w_panel(wp, w_mod[L], pp * 1024)
            for bl in range(8):
                j = pp * 8 + bl
                pb = next_bank()
                for k in range(KT):
                    S.op("pe", lambda e, k=k, bl=bl, wp=wp, pb=pb: e.matmul(pb[:, 0:2], lhsT=wp[:, k, bl * 128:(bl + 1) * 128],
                                                                    rhs=cact[:, k, :], start=(k == 0), stop=(k == KT - 1)),
                         reads=[wp, cact], writes=[pb])
                S.op("dve", lambda e, j=j, pb=pb: e.tensor_scalar(out=modt[:, j, :], in0=pb[:, 0:2], scalar1=bmod[:, j:j + 1], scalar2=None, op0=ALU.add),
                     reads=[pb, bmod], writes=[modt])
        S.op("dve", lambda e: e.scalar_tensor_tensor(out=Amod[:, :, :], in0=modt[:, 16:32, :], scalar=1.0,
                                                     in1=gpre[:, :].unsqueeze(2).to_broadcast([128, KT, 2]), op0=ALU.add, op1=ALU.mult),
             reads=[modt, gpre], writes=[Amod])
        S.op("dve", lambda e: e.tensor_tensor(out=Gmod[:, :, :], in0=modt[:, 32:48, :],
                                              in1=gpost[:, :].unsqueeze(2).to_broadcast([128, KT, 2]), op=ALU.mult),
             reads=[modt, gpost], writes=[Gmod])

        if stop == "S1":
            raise _Stop()
        XTS = [al.get(KT * 512 * 4, F32, "p (k n) -> p k n", k=KT) for _ in range(2)]
        HBS = [al.get(KT * 512 * 2, BF16, "p (k n) -> p k n", k=KT) for _ in range(2)]

        def load_x2(ti_):
            c0_, n_ = TILES[ti_]
            S.dma("sp", XTS[ti_ % 2][:, :, 0:n_], xsrc.rearrange("(k p) n -> p k n", p=128)[:, :, c0_:c0_ + n_],
                  reads=[DB(xsn, 0, ti_)], writes=[XTS[ti_ % 2]])

        load_x2(0)
        for ti, (c0, n) in enumerate(TILES):
            s = 0 if ti < 8 else 1
            xt = XTS[ti % 2]
            hb = HBS[ti % 2]
            if ti + 1 < NTI:
                load_x2(ti + 1)
            S.op("act", lambda e, n=n, xt=xt, hb=hb: e.activation(out=hb[:, :, 0:n], in_=xt[:, :, 0:n], func=AF.Square), reads=[xt], writes=[hb])
            pb = next_bank()
            for k in range(KT):
                S.op("pe", lambda e, k=k, n=n, pb=pb, hb=hb: e.matmul(pb[:, 0:n], lhsT=ones[:, :], rhs=hb[:, k, 0:n], start=(k == 0), stop=(k == KT - 1)),
                     reads=[ones, hb], writes=[pb])
            S.op("act", lambda e, n=n, pb=pb: e.activation(out=rtmp[:, 0:n], in_=pb[:, 0:n], func=AF.Sqrt, bias=epsb[:, 0:1], scale=1.0 / D),
                 reads=[pb, epsb], writes=[rtmp])
            S.op("dve", lambda e, n=n: e.reciprocal(out=rstd[:, 0:n], in_=rtmp[:, 0:n]), reads=[rtmp], writes=[rstd])
            S.op("dve", lambda e, n=n, xt=xt: e.tensor_tensor(out=xt[:, :, 0:n], in0=xt[:, :, 0:n],
                                                       in1=rstd[:, 0:n].unsqueeze(1).to_broadcast([128, KT, n]), op=ALU.mult),
                 reads=[xt, rstd], writes=[xt])
            for k in range(KT):
                S.op("act", lambda e, k=k, n=n, s=s, xt=xt, hb=hb: e.activation(out=hb[:, k, 0:n], in_=xt[:, k, 0:n], func=AF.Identity,
                                                                 bias=modt[:, k, s:s + 1], scale=Amod[:, k, s:s + 1]),
                     reads=[xt, modt, Amod], writes=[hb])
            S.dma("sp", hT.rearrange("(k p) n -> p k n", p=128)[:, :, c0:c0 + n], hb[:, :, 0:n], reads=[hb], writes=[DB("hT", 0, ti)])

        if stop == "S2":
            raise _Stop()
        al = Alloc(AR)
        WP = [w_panel_view(al), w_panel_view(al)]
        HB = [al.get(KT * 512 * 2, BF16, "p (k n) -> p k n", k=KT) for _ in range(2)]
        vst = al.get(512 * 2 * 2, BF16, "p (a n) -> p a n", a=2)
        load_w_panel(WP[0], w_in[L], 0)
        hcnt = 0

        def load_h(cnt):
            ti_ = cnt % NTI
            c0_, n_ = TILES[ti_]
            hb_ = HB[cnt % 2]
            S.dma("sp", hb_[:, :, 0:n_], hT.rearrange("(k p) n -> p k n", p=128)[:, :, c0_:c0_ + n_], reads=[DB("hT", 0, ti_)], writes=[hb_])

        load_h(0)
        for pp in range(14):
            wp = WP[pp % 2]
            if pp + 1 < 14:
                load_w_panel(WP[(pp + 1) % 2], w_in[L], (pp + 1) * 1024)
            for ti, (c0, n) in enumerate(TILES):
                hbt = HB[hcnt % 2]
                hcnt += 1
                if hcnt < 14 * NTI:
                    load_h(hcnt)
                for bl in range(8):
                    bi = pp * 8 + bl
                    if 8 <= bi < 12:
                        if bi > 8:
                            continue
                        for tb in range(n // 128):
                            pb = next_bank()
                            for k in range(KT):
                                S.op("pe", lambda e, k=k, tb=tb, pb=pb, hbt=hbt, wp=wp: e.matmul(pb[:, :], lhsT=hbt[:, k, tb * 128:(tb + 1) * 128],
                                                                                         rhs=wp[:, k, 0:512], start=(k == 0), stop=(k == KT - 1)),
                                     reads=[hbt, wp], writes=[pb])
                            st = next_stg()
                            S.op("dve", lambda e, pb=pb, st=st: e.tensor_copy(out=st[:, :], in_=pb[:, :]), reads=[pb], writes=[st])
                            S.dma("sp", vT[c0 + tb * 128:c0 + (tb + 1) * 128, :], st[:, :], reads=[st], writes=[DB("vT", 0, ti)])
                        continue
                    pb = next_bank()
                    for k in range(KT):
                        S.op("pe", lambda e, k=k, bl=bl, pb=pb, hbt=hbt, wp=wp, n=n: e.matmul(pb[:, 0:n], lhsT=wp[:, k, bl * 128:(bl + 1) * 128],
                                                                                      rhs=hbt[:, k, 0:n], start=(k == 0), stop=(k == KT - 1)),
                             reads=[hbt, wp], writes=[pb])
                    st = next_stg()
                    if bi < 4:
                        S.op("dve", lambda e, pb=pb, st=st, n=n: e.tensor_scalar(out=st[:, 0:n], in0=pb[:, 0:n], scalar1=0.125, scalar2=None, op0=ALU.mult),
                             reads=[pb], writes=[st])
                    elif bi >= 48:
                        S.op("act", lambda e, pb=pb, st=st, n=n, bi=bi: e.activation(out=st[:, 0:n], in_=pb[:, 0:n], func=AF.Sigmoid,
                                                                                  bias=bgate[:, bi - 48:bi - 47], scale=1.0),
                             reads=[pb, bgate], writes=[st])
                    elif (12 <= bi < 16) or (20 <= bi < 24) or (36 <= bi < 40) or (44 <= bi < 48):
                        S.op("act", lambda e, pb=pb, st=st, n=n: e.activation(out=st[:, 0:n], in_=pb[:, 0:n], func=AF.Silu), reads=[pb], writes=[st])
                    else:
                        S.op("dve", lambda e, pb=pb, st=st, n=n: e.tensor_copy(out=st[:, 0:n], in_=pb[:, 0:n]), reads=[pb], writes=[st])
                    S.dma("sp", proj[bi * 128:(bi + 1) * 128, c0:c0 + n], st[:, 0:n], reads=[st], writes=[DB("proj", bi, ti)])

        if stop == "S3":
            raise _Stop()
        al = Alloc(AR)
        qs = al.get(NT * 2)
        ks = al.get(NT * 2)
        szs = al.get(NT * 2)
        ao = al.get(NT * 2)
        vs = al.get(34 * 128 * 2, BF16, "p (b c) -> p b c", b=34)
        bt = al.get(2 * 5 * 640 * 2, BF16, "p (h t k) -> p h t k", h=2, t=5)
        PT = [al.get(896 * 2) for _ in range(2)]
        rden = S.sb(f"rden{L}", [128, 128], F32) if L == 0 else rden
        otmp = S.sb(f"otmp{L}", [128, 128], F32) if L == 0 else otmp
        acnt = 0
        for hp in range(4):
            S.dma("sp", qs[:, :], proj[hp * 128:(hp + 1) * 128, :], reads=DBrow("proj", hp), writes=[qs])
            S.dma("sp", ks[:, :], proj[512 + hp * 128:512 + (hp + 1) * 128, :], reads=DBrow("proj", 4 + hp), writes=[ks])
            S.dma("sp", szs[:, :], proj[1536 + hp * 128:1536 + (hp + 1) * 128, :], reads=DBrow("proj", 12 + hp), writes=[szs])
            S.dma("sp", vs[:, :, :], vT.rearrange("(b p) c -> p b c", p=128)[:, :, hp * 128:(hp + 1) * 128], reads=DBrow("vT", 0), writes=[vs])
            S.dma("pool", bt[:, :, :, :], na_bias[L, 2 * hp:2 * hp + 2].rearrange("h p t k -> p h t k"), writes=[bt])
            qblocks = [("band", i) for i in range(32)] + ([("ctx", 0), ("ctx", 1)] if with_ctx else [])
            for kind, i in qblocks:
                for hd in range(2):
                    pbs = 64 * hd
                    sb2 = bank2(4 + 2 * (acnt % 2))
                    oc = bank(acnt % 2 + 2) if False else bank(2 + acnt % 2)
                    pt = PT[acnt % 2]
                    acnt += 1
                    if kind == "band":
                        r0 = 2 * i
                        pat = 0 if i == 0 else 1 if i == 1 else 3 if i == 30 else 4 if i == 31 else 2
                        lo = min(max(r0 - 4, 0), 54)
                        k0 = lo * 64
                        q0 = r0 * 64
                        nkb = 7
                        for b in range(5):
                            S.op("pe", lambda e, b=b, sb2=sb2, pbs=pbs, k0=k0, q0=q0: e.matmul(sb2[:, b * 128:(b + 1) * 128], lhsT=ks[pbs:pbs + 64, k0 + b * 128:k0 + (b + 1) * 128],
                                                                                     rhs=qs[pbs:pbs + 64, q0:q0 + 128], start=True, stop=False),
                                 reads=[ks, qs], writes=[sb2])
                            S.op("pe", lambda e, b=b, sb2=sb2, hd=hd, pat=pat: e.matmul(sb2[:, b * 128:(b + 1) * 128], lhsT=bt[:, hd, pat, b * 128:(b + 1) * 128],
                                                                                    rhs=ident[:, :], start=False, stop=True),
                                 reads=[bt, ident], writes=[sb2])
                        for cb in range(2):
                            S.op("pe", lambda e, cb=cb, sb2=sb2, pbs=pbs, q0=q0: e.matmul(sb2[:, 640 + cb * 128:640 + (cb + 1) * 128],
                                                                                      lhsT=ks[pbs:pbs + 64, NL + cb * 128:NL + (cb + 1) * 128],
                                                                                      rhs=qs[pbs:pbs + 64, q0:q0 + 128], start=True, stop=True),
                                 reads=[ks, qs], writes=[sb2])
                        vblk = [k0 // 128 + b for b in range(5)] + [32, 33]
                    else:
                        q0 = NL + i * 128
                        nkb = 2
                        for cb in range(2):
                            S.op("pe", lambda e, cb=cb, sb2=sb2, pbs=pbs, q0=q0: e.matmul(sb2[:, cb * 128:(cb + 1) * 128],
                                                                                      lhsT=ks[pbs:pbs + 64, NL + cb * 128:NL + (cb + 1) * 128],
                                                                                      rhs=qs[pbs:pbs + 64, q0:q0 + 128], start=True, stop=True),
                                 reads=[ks, qs], writes=[sb2])
                        vblk = [32, 33]
                    nk = nkb * 128
                    S.op("act", lambda e, sb2=sb2, pt=pt, nk=nk: e.activation(out=pt[:, 0:nk], in_=sb2[:, 0:nk], func=AF.Exp), reads=[sb2], writes=[pt])
                    for kb in range(nkb):
                        S.op("pe", lambda e, kb=kb, oc=oc, pt=pt, vb=vblk[kb], nkb=nkb: e.matmul(oc[:, 0:128], lhsT=vs[:, vb, :], rhs=pt[:, kb * 128:(kb + 1) * 128],
                                                                                       start=(kb == 0), stop=(kb == nkb - 1)),
                             reads=[vs, pt], writes=[oc])
                    for kb in range(nkb):
                        S.op("pe", lambda e, kb=kb, oc=oc, pt=pt, nkb=nkb: e.matmul(oc[:, 128:256], lhsT=ones[:, :], rhs=pt[:, kb * 128:(kb + 1) * 128],
                                                                             start=(kb == 0), stop=(kb == nkb - 1)),
                             reads=[ones, pt], writes=[oc])
                    S.op("dve", lambda e, oc=oc, pbs=pbs: e.reciprocal(out=rden[pbs:pbs + 64, :], in_=oc[pbs:pbs + 64, 128:256]), reads=[oc], writes=[rden])
                    S.op("dve", lambda e, oc=oc, pbs=pbs: e.tensor_tensor(out=otmp[pbs:pbs + 64, :], in0=oc[pbs:pbs + 64, 0:128], in1=rden[pbs:pbs + 64, :], op=ALU.mult),
                         reads=[oc, rden], writes=[otmp])
                    S.op("dve", lambda e, pbs=pbs, q0=q0: e.tensor_tensor(out=ao[pbs:pbs + 64, q0:q0 + 128], in0=otmp[pbs:pbs + 64, :], in1=szs[pbs:pbs + 64, q0:q0 + 128], op=ALU.mult),
                         reads=[otmp, szs], writes=[ao])
            ncol = NT if with_ctx else NL
            S.dma("sp", a_d[hp * 128:(hp + 1) * 128, 0:ncol], ao[:, 0:ncol], reads=[ao], writes=DBrow("a", hp))

        if stop == "S4":
            raise _Stop()
        al = Alloc(AR)
        ubp = al.get(NT * 2)
        zbp = al.get(NT * 2)
        pob = al.get(NT * 2)
        U = al.get((NL + 16) * 4, F32)
        T1 = al.get((NL + 16) * 4, F32)
        T2 = al.get((NL + 16) * 4, F32)
        pwb = al.get(4 * 128 * 2, BF16, "p (g d) -> p g d", g=4)
        pab = al.get(NT * 2)
        S.dma("pool", pwb[:, :, :], pool_w[L], writes=[pwb])
        seqs = [(0, NL, 0)] + ([(NL, NCX, 1)] if with_ctx else [])
        for g in range(4):
            S.dma("sp", ubp[:, :], proj[2048 + g * 128:2048 + (g + 1) * 128, :], reads=DBrow("proj", 16 + g), writes=[ubp])
            S.dma("sp", zbp[:, :], proj[2560 + g * 128:2560 + (g + 1) * 128, :], reads=DBrow("proj", 20 + g), writes=[zbp])
            w = (2, 4, 8, 16)[g]
            for (c0, n, sq) in seqs:
                Ln = n + 16
                S.op("dve", lambda e, Ln=Ln: e.memset(U[:, 0:Ln], 0.0), writes=[U])
                S.op("dve", lambda e, c0=c0, n=n: e.tensor_copy(out=U[:, 8:8 + n], in_=ubp[:, c0:c0 + n]), reads=[ubp, U], writes=[U])
                S.op("dve", lambda e, Ln=Ln: e.tensor_tensor(out=T1[:, 1:Ln], in0=U[:, 0:Ln - 1], in1=U[:, 1:Ln], op=ALU.add), reads=[U], writes=[T1])
                cur, oth = T1, T2
                if w >= 4:
                    S.op("dve", lambda e, Ln=Ln: e.tensor_tensor(out=T2[:, 2:Ln - 1], in0=T1[:, 1:Ln - 2], in1=T1[:, 3:Ln], op=ALU.add), reads=[T1], writes=[T2])
                    cur, oth = T2, T1
                if w >= 8:
                    S.op("dve", lambda e, Ln=Ln: e.tensor_tensor(out=T1[:, 4:Ln - 3], in0=T2[:, 2:Ln - 5], in1=T2[:, 6:Ln - 1], op=ALU.add), reads=[T2], writes=[T1])
                    cur, oth = T1, T2
                if w >= 16:
                    S.op("dve", lambda e, Ln=Ln: e.tensor_tensor(out=T2[:, 8:Ln - 7], in0=T1[:, 4:Ln - 11], in1=T1[:, 12:Ln - 3], op=ALU.add), reads=[T1], writes=[T2])
                    cur, oth = T2, T1
                S.op("dve", lambda e, cur=cur, oth=oth, n=n, w=w: e.scalar_tensor_tensor(out=oth[:, 8:8 + n], in0=cur[:, 8:8 + n], scalar=1.0 / w, in1=U[:, 8:8 + n],
                                                                                 op0=ALU.mult, op1=ALU.subtract), reads=[cur, U], writes=[oth])
                for side, e0 in ((0, 8), (1, 8 + n - 8)):
                    S.op("dve", lambda e, cur=cur, sq=sq, g=g, side=side, e0=e0: e.tensor_tensor(out=rtmp[:, 0:8], in0=cur[:, e0:e0 + 8], in1=pedge[:, sq, g, side, :], op=ALU.mult),
                         reads=[cur, pedge], writes=[rtmp])
                    S.op("dve", lambda e, oth=oth, e0=e0: e.tensor_tensor(out=oth[:, e0:e0 + 8], in0=rtmp[:, 0:8], in1=U[:, e0:e0 + 8], op=ALU.subtract),
                         reads=[rtmp, U, oth], writes=[oth])
                S.op("act", lambda e, oth=oth, c0=c0, n=n: e.activation(out=pob[:, c0:c0 + n], in_=oth[:, 8:8 + n], func=AF.Copy), reads=[oth], writes=[pob])
                if dbg and L == 0 and sq == 0:
                    S.dma("sp", dbg_pob[g], pob[:, 0:NL], reads=[pob], writes=[DB("dbgpob", g, 0)])
                for cc in range(0, n, 512):
                    m = min(512, n - cc)
                    pb = next_bank()
                    S.op("pe", lambda e, pb=pb, g=g, c=c0 + cc, m=m: e.matmul(pb[:, 0:m], lhsT=pwb[:, g, :], rhs=pob[:, c:c + m], start=True, stop=True),
                         reads=[pwb, pob], writes=[pb])
                    S.op("dve", lambda e, pb=pb, g=g, c=c0 + cc, m=m: e.scalar_tensor_tensor(out=pab[:, c:c + m], in0=pb[:, 0:m], scalar=psc[:, g:g + 1], in1=zbp[:, c:c + m],
                                                                                    op0=ALU.mult, op1=ALU.mult), reads=[pb, psc, zbp], writes=[pab])
            ncol = NT if with_ctx else NL
            S.dma("sp", a_d[512 + g * 128:512 + (g + 1) * 128, 0:ncol], pab[:, 0:ncol], reads=[pab], writes=DBrow("a", 4 + g))

        if stop == "S5":
            raise _Stop()
        al = Alloc(AR)
        xb_ = al.get(NT * 2)
        bb_ = al.get(NT * 2)
        cb_ = al.get(NT * 2)
        zb = al.get(NT * 2)
        cab = al.get(NT * 2)
        Tt = al.get((NL + 2) * 4, F32)
        Dw = al.get(NL * 4, F32)
        for g in range(4):
            S.dma("sp", xb_[:, :], proj[3072 + g * 128:3072 + (g + 1) * 128, :], reads=DBrow("proj", 24 + g), writes=[xb_])
            S.dma("sp", bb_[:, :], proj[3584 + g * 128:3584 + (g + 1) * 128, :], reads=DBrow("proj", 28 + g), writes=[bb_])
            S.dma("sp", cb_[:, :], proj[4096 + g * 128:4096 + (g + 1) * 128, :], reads=DBrow("proj", 32 + g), writes=[cb_])
            S.dma("sp", zb[:, :], proj[4608 + g * 128:4608 + (g + 1) * 128, :], reads=DBrow("proj", 36 + g), writes=[zb])
            for (c0, n, sq) in seqs:
                S.op("dve", lambda e, n=n: e.memset(Tt[:, 0:n + 2], 0.0), writes=[Tt])
                S.op("dve", lambda e, c0=c0, n=n: e.tensor_tensor(out=Tt[:, 1:n + 1], in0=cb_[:, c0:c0 + n], in1=xb_[:, c0:c0 + n], op=ALU.mult),
                     reads=[cb_, xb_, Tt], writes=[Tt])
                S.op("dve", lambda e, n=n, g=g: e.tensor_scalar(out=Dw[:, 0:n], in0=Tt[:, 0:n], scalar1=cw[:, g, 0:1], scalar2=None, op0=ALU.mult), reads=[Tt, cw], writes=[Dw])
                for j in (1, 2):
                    S.op("dve", lambda e, n=n, g=g, j=j: e.scalar_tensor_tensor(out=Dw[:, 0:n], in0=Tt[:, j:j + n], scalar=cw[:, g, j:j + 1], in1=Dw[:, 0:n],
                                                                           op0=ALU.mult, op1=ALU.add), reads=[Tt, cw, Dw], writes=[Dw])
                S.op("dve", lambda e, c0=c0, n=n: e.tensor_tensor(out=Dw[:, 0:n], in0=Dw[:, 0:n], in1=bb_[:, c0:c0 + n], op=ALU.mult), reads=[Dw, bb_], writes=[Dw])
                S.op("dve", lambda e, c0=c0, n=n: e.tensor_tensor(out=cab[:, c0:c0 + n], in0=Dw[:, 0:n], in1=zb[:, c0:c0 + n], op=ALU.mult), reads=[Dw, zb], writes=[cab])
            ncol = NT if with_ctx else NL
            S.dma("sp", a_d[1024 + g * 128:1024 + (g + 1) * 128, 0:ncol], cab[:, 0:ncol], reads=[cab], writes=DBrow("a", 8 + g))

        if stop == "S6":
            raise _Stop()
        al = Alloc(AR)
        NH = NT // 2
        t1 = al.get(NH * 4, F32)
        t2 = al.get(NH * 4, F32)
        csT = al.get(NT * 4, F32)
        snT = al.get(NT * 4, F32)
        BTr = al.get(NT * 4, F32)
        BTi = al.get(NT * 4, F32)
        Yacc = al.get(NT * 4, F32)
        iot = al.get(NT * 4, F32)
        usb = al.get(NT * 2)
        usbb = al.get(NT * 2)
        sre = al.get(NT * 2)
        sim = al.get(NT * 2)
        tii = View(sre.t.bitcast(I32), sre.bufs)
        S.dma("sp", iot[:, :], iota_nt, writes=[iot])
        S.dma("sp", s_are[:, :], ssm_are[L], writes=[s_are])
        S.dma("sp", s_aim[:, :], ssm_aim[L], writes=[s_aim])
        S.dma("sp", s_ldt[:, :], ssm_ldt[L], writes=[s_ldt])
        S.op("act", lambda e: e.activation(out=s_dt[:, :], in_=s_ldt[:, :], func=AF.Exp), reads=[s_ldt], writes=[s_dt])
        S.op("dve", lambda e: e.tensor_tensor(out=s_x1[:, :], in0=s_are[:, :], in1=s_dt[:, :], op=ALU.mult), reads=[s_are, s_dt], writes=[s_x1])
        S.op("dve", lambda e: e.tensor_tensor(out=s_th[:, :], in0=s_aim[:, :], in1=s_dt[:, :], op=ALU.mult), reads=[s_aim, s_dt], writes=[s_th])
        S.op("act", lambda e: e.activation(out=s_rho[:, :], in_=s_x1[:, :], func=AF.Exp), reads=[s_x1], writes=[s_rho])

        def rr(dst, src, tmp, tint, n, shift=0.0):
            S.op("dve", lambda e: e.tensor_scalar(out=tmp[:, 0:n], in0=src[:, 0:n], scalar1=shift, scalar2=1.0 / TWO_PI, op0=ALU.add, op1=ALU.mult),
                 reads=[src], writes=[tmp])
            S.op("dve", lambda e: e.tensor_copy(out=tint[:, 0:n], in_=tmp[:, 0:n]), reads=[tmp], writes=[tint])
            S.op("dve", lambda e: e.tensor_copy(out=tmp[:, 0:n], in_=tint[:, 0:n]), reads=[tint], writes=[tmp])
            S.op("dve", lambda e: e.scalar_tensor_tensor(out=tmp[:, 0:n], in0=tmp[:, 0:n], scalar=-TWO_PI, in1=src[:, 0:n], op0=ALU.mult, op1=ALU.add),
                 reads=[tmp, src], writes=[tmp])
            if shift != 0.0:
                S.op("dve", lambda e: e.tensor_scalar(out=tmp[:, 0:n], in0=tmp[:, 0:n], scalar1=shift, scalar2=None, op0=ALU.add), reads=[tmp], writes=[tmp])
            S.op("dve", lambda e: e.tensor_scalar(out=dst[:, 0:n], in0=tmp[:, 0:n], scalar1=PI, scalar2=-TWO_PI, op0=ALU.is_gt, op1=ALU.mult), reads=[tmp], writes=[dst])
            S.op("dve", lambda e: e.tensor_tensor(out=tmp[:, 0:n], in0=tmp[:, 0:n], in1=dst[:, 0:n], op=ALU.add), reads=[tmp, dst], writes=[tmp])
            S.op("dve", lambda e: e.tensor_scalar(out=dst[:, 0:n], in0=tmp[:, 0:n], scalar1=-PI, scalar2=TWO_PI, op0=ALU.is_lt, op1=ALU.mult), reads=[tmp], writes=[dst])
            S.op("dve", lambda e: e.tensor_tensor(out=dst[:, 0:n], in0=tmp[:, 0:n], in1=dst[:, 0:n], op=ALU.add), reads=[tmp, dst], writes=[dst])

        rr(s_thr, s_th, s_t1, s_ti, 32)
        S.op("dve", lambda e: e.tensor_scalar(out=s_thq[:, :], in0=s_thr[:, :], scalar1=1.0 / TWO_PI, scalar2=None, op0=ALU.mult), reads=[s_thr], writes=[s_thq])
        S.op("act", lambda e: e.activation(out=s_sn1[:, :], in_=s_thr[:, :], func=AF.Sin), reads=[s_thr], writes=[s_sn1])
        rr(s_t2, s_thr, s_t1, s_ti, 32, shift=PI / 2)
        S.op("act", lambda e: e.activation(out=s_cs1[:, :], in_=s_t2[:, :], func=AF.Sin), reads=[s_t2], writes=[s_cs1])
        S.op("dve", lambda e: e.tensor_tensor(out=s_nr[:, :], in0=s_rho[:, :], in1=s_cs1[:, :], op=ALU.mult), reads=[s_rho, s_cs1], writes=[s_nr])
        S.op("dve", lambda e: e.tensor_scalar(out=s_nr[:, :], in0=s_nr[:, :], scalar1=-1.0, scalar2=None, op0=ALU.add), reads=[s_nr], writes=[s_nr])
        S.op("dve", lambda e: e.tensor_tensor(out=s_ni[:, :], in0=s_rho[:, :], in1=s_sn1[:, :], op=ALU.mult), reads=[s_rho, s_sn1], writes=[s_ni])
        S.op("dve", lambda e: e.tensor_tensor(out=s_t1[:, :], in0=s_are[:, :], in1=s_are[:, :], op=ALU.mult), reads=[s_are], writes=[s_t1])
        S.op("dve", lambda e: e.tensor_tensor(out=s_t2[:, :], in0=s_aim[:, :], in1=s_aim[:, :], op=ALU.mult), reads=[s_aim], writes=[s_t2])
        S.op("dve", lambda e: e.tensor_tensor(out=s_t1[:, :], in0=s_t1[:, :], in1=s_t2[:, :], op=ALU.add), reads=[s_t1, s_t2], writes=[s_t1])
        S.op("dve", lambda e: e.reciprocal(out=s_den[:, :], in_=s_t1[:, :]), reads=[s_t1], writes=[s_den])
        S.op("dve", lambda e: e.tensor_tensor(out=s_t1[:, :], in0=s_nr[:, :], in1=s_are[:, :], op=ALU.mult), reads=[s_nr, s_are], writes=[s_t1])
        S.op("dve", lambda e: e.tensor_tensor(out=s_t2[:, :], in0=s_ni[:, :], in1=s_aim[:, :], op=ALU.mult), reads=[s_ni, s_aim], writes=[s_t2])
        S.op("dve", lambda e: e.tensor_tensor(out=s_t1[:, :], in0=s_t1[:, :], in1=s_t2[:, :], op=ALU.add), reads=[s_t1, s_t2], writes=[s_t1])
        S.op("dve", lambda e: e.tensor_tensor(out=s_fr[:, :], in0=s_t1[:, :], in1=s_den[:, :], op=ALU.mult), reads=[s_t1, s_den], writes=[s_fr])
        S.op("dve", lambda e: e.tensor_tensor(out=s_t1[:, :], in0=s_ni[:, :], in1=s_are[:, :], op=ALU.mult), reads=[s_ni, s_are], writes=[s_t1])
        S.op("dve", lambda e: e.tensor_tensor(out=s_t2[:, :], in0=s_nr[:, :], in1=s_aim[:, :], op=ALU.mult), reads=[s_nr, s_aim], writes=[s_t2])
        S.op("dve", lambda e: e.tensor_tensor(out=s_t1[:, :], in0=s_t1[:, :], in1=s_t2[:, :], op=ALU.subtract), reads=[s_t1, s_t2], writes=[s_t1])
        S.op("dve", lambda e: e.tensor_tensor(out=s_fi[:, :], in0=s_t1[:, :], in1=s_den[:, :], op=ALU.mult), reads=[s_t1, s_den], writes=[s_fi])

        HALVES = [(0, NH), (NH, NH)]
        for o in range(4):
            S.dma("sp", usb[:, 0:NCX], proj[5120 + o * 128:5120 + (o + 1) * 128, NL:NT], reads=[DB("proj", 40 + o, 8)], writes=[usb])
            S.dma("sp", usb[:, NCX:NT], proj[5120 + o * 128:5120 + (o + 1) * 128, 0:NL], reads=DBrow("proj", 40 + o), writes=[usb])
            S.dma("sp", usbb[:, :], proj[5120 + o * 128:5120 + (o + 1) * 128, :], reads=DBrow("proj", 40 + o), writes=[usbb])
            first = True
            for d in range(2):
                for jj in range(4):
                    u = d * 16 + o * 4 + jj
                    S.dma("sp", sB[:, :, :], ssm_B[L][:, u], writes=[sB])
                    S.dma("sp", sC[:, :, :], ssm_C[L][:, u], writes=[sC])
                    S.op("dve", lambda e, u=u: e.tensor_scalar(out=sT[:, :], in0=sB[:, 1, :], scalar1=s_fi[:, u:u + 1], scalar2=None, op0=ALU.mult), reads=[sB, s_fi], writes=[sT])
                    S.op("dve", lambda e, u=u: e.scalar_tensor_tensor(out=sBb[:, 0, :], in0=sB[:, 0, :], scalar=s_fr[:, u:u + 1], in1=sT[:, :], op0=ALU.mult, op1=ALU.subtract),
                         reads=[sB, s_fr, sT], writes=[sBb])
                    S.op("dve", lambda e, u=u: e.tensor_scalar(out=sT[:, :], in0=sB[:, 0, :], scalar1=s_fi[:, u:u + 1], scalar2=None, op0=ALU.mult), reads=[sB, s_fi], writes=[sT])
                    S.op("dve", lambda e, u=u: e.scalar_tensor_tensor(out=sBb[:, 1, :], in0=sB[:, 1, :], scalar=s_fr[:, u:u + 1], in1=sT[:, :], op0=ALU.mult, op1=ALU.add),
                         reads=[sB, s_fr, sT], writes=[sBb])
                    S.op("act", lambda e: e.activation(out=sCb[:, 0, :], in_=sC[:, 0, :], func=AF.Copy), reads=[sC], writes=[sCb])
                    S.op("act", lambda e: e.activation(out=sCb[:, 1, :], in_=sC[:, 1, :], func=AF.Copy, scale=-1.0), reads=[sC], writes=[sCb])
                    pbt = bankbf(0)
                    for ri in range(2):
                        S.op("pe", lambda e, ri=ri, pbt=pbt: e.transpose(out=pbt[:, ri * 128:(ri + 1) * 128], in_=sBb[:, ri, :], identity=ident[:, :]),
                             reads=[sBb, ident], writes=[pbt])
                    S.op("act", lambda e, pbt=pbt: e.activation(out=sWT[:, :, :], in_=pbt[:, 0:256].rearrange("p (a b) -> p a b", a=2), func=AF.Copy), reads=[pbt], writes=[sWT])
                    for (h0, hn) in HALVES:
                        S.op("dve", lambda e, u=u, h0=h0, hn=hn: e.tensor_scalar(out=t1[:, 0:hn], in0=iot[:, h0:h0 + hn], scalar1=s_thq[:, u:u + 1], scalar2=MAGIC,
                                                                            op0=ALU.mult, op1=ALU.add), reads=[iot, s_thq], writes=[t1])
                        S.op("dve", lambda e, hn=hn: e.tensor_scalar(out=t1[:, 0:hn], in0=t1[:, 0:hn], scalar1=MAGIC, scalar2=-TWO_PI, op0=ALU.subtract, op1=ALU.mult),
                             reads=[t1], writes=[t1])
                        S.op("dve", lambda e, u=u, h0=h0, hn=hn: e.scalar_tensor_tensor(out=snT[:, h0:h0 + hn], in0=iot[:, h0:h0 + hn], scalar=s_thr[:, u:u + 1], in1=t1[:, 0:hn],
                                                                                   op0=ALU.mult, op1=ALU.add), reads=[iot, s_thr, t1], writes=[snT])
                        S.op("dve", lambda e, h0=h0, hn=hn: e.tensor_scalar(out=snT[:, h0:h0 + hn], in0=snT[:, h0:h0 + hn], scalar1=-PI, scalar2=PI, op0=ALU.max, op1=ALU.min),
                             reads=[snT], writes=[snT])
                        S.op("dve", lambda e, h0=h0, hn=hn: e.scalar_tensor_tensor(out=csT[:, h0:h0 + hn], in0=snT[:, h0:h0 + hn], scalar=-1.0, in1=snT[:, h0:h0 + hn], op0=ALU.mult, op1=ALU.max),
                             reads=[snT], writes=[csT])
                    S.op("act", lambda e: e.activation(out=csT[:, :], in_=csT[:, :], func=AF.Sin, bias=halfpi[:, 0:1], scale=-1.0), reads=[csT, halfpi], writes=[csT])
                    S.op("act", lambda e: e.activation(out=snT[:, :], in_=snT[:, :], func=AF.Sin), reads=[snT], writes=[snT])
                    def rsl(buf, c0, n, d=d):
                        if d == 0:
                            return buf[:, c0:c0 + n]
                        st_ = NT - 1 - c0
                        sp_ = st_ - n
                        return buf[:, st_::-1] if sp_ < 0 else buf[:, st_:sp_:-1]
                    ub = usb if d == 0 else usbb
                    for c0 in range(0, NT, 512):
                        n = min(512, NT - c0)
                        p0 = bank(1)
                        p1 = bank(2)
                        S.op("pe", lambda e, c0=c0, n=n, p0=p0, ub=ub: e.matmul(p0[:, 0:n], lhsT=sWT[:, 0, :], rhs=ub[:, c0:c0 + n], start=True, stop=True), reads=[sWT, ub], writes=[p0])
                        S.op("pe", lambda e, c0=c0, n=n, p1=p1, ub=ub: e.matmul(p1[:, 0:n], lhsT=sWT[:, 1, :], rhs=ub[:, c0:c0 + n], start=True, stop=True), reads=[sWT, ub], writes=[p1])
                        ta, tb = stgf[0], stgf[1]
                        S.op("dve", lambda e, c0=c0, n=n, p0=p0, v=rsl(csT, c0, min(512, NT - c0)): e.tensor_tensor(out=ta[:, 0:n], in0=p0[:, 0:n], in1=v, op=ALU.mult), reads=[p0, csT], writes=[ta])
                        S.op("dve", lambda e, c0=c0, n=n, p1=p1, v=rsl(snT, c0, min(512, NT - c0)): e.tensor_tensor(out=tb[:, 0:n], in0=p1[:, 0:n], in1=v, op=ALU.mult), reads=[p1, snT], writes=[tb])
                        S.op("dve", lambda e, c0=c0, n=n: e.tensor_tensor(out=BTr[:, c0:c0 + n], in0=ta[:, 0:n], in1=tb[:, 0:n], op=ALU.add), reads=[ta, tb], writes=[BTr])
                        S.op("dve", lambda e, c0=c0, n=n, p1=p1, v=rsl(csT, c0, min(512, NT - c0)): e.tensor_tensor(out=ta[:, 0:n], in0=p1[:, 0:n], in1=v, op=ALU.mult), reads=[p1, csT], writes=[ta])
                        S.op("dve", lambda e, c0=c0, n=n, p0=p0, v=rsl(snT, c0, min(512, NT - c0)): e.tensor_tensor(out=tb[:, 0:n], in0=p0[:, 0:n], in1=v, op=ALU.mult), reads=[p0, snT], writes=[tb])
                        S.op("dve", lambda e, c0=c0, n=n: e.tensor_tensor(out=BTi[:, c0:c0 + n], in0=ta[:, 0:n], in1=tb[:, 0:n], op=ALU.subtract), reads=[ta, tb], writes=[BTi])
                    for BT in (BTr, BTi):
                        bv = BT[:, :] if d == 0 else BT[:, ::-1]
                        S.op("dve", lambda e, bv=bv, u=u: e.tensor_tensor_scan(out=bv, data0=s_rho[:, u:u + 1].to_broadcast([128, NT]), data1=bv, initial=0.0,
                                                                          op0=ALU.mult, op1=ALU.add), reads=[BT, s_rho], writes=[BT])
                    for (h0, hn) in HALVES:
                        cvh = rsl(csT, h0, hn)
                        svh = rsl(snT, h0, hn)
                        S.op("dve", lambda e, h0=h0, hn=hn, cvh=cvh: e.tensor_tensor(out=t1[:, 0:hn], in0=BTr[:, h0:h0 + hn], in1=cvh, op=ALU.mult), reads=[BTr, csT], writes=[t1])
                        S.op("dve", lambda e, h0=h0, hn=hn, svh=svh: e.tensor_tensor(out=t2[:, 0:hn], in0=BTi[:, h0:h0 + hn], in1=svh, op=ALU.mult), reads=[BTi, snT], writes=[t2])
                        S.op("dve", lambda e, h0=h0, hn=hn: e.tensor_tensor(out=sre[:, h0:h0 + hn], in0=t1[:, 0:hn], in1=t2[:, 0:hn], op=ALU.subtract), reads=[t1, t2], writes=[sre])
                        S.op("dve", lambda e, h0=h0, hn=hn, cvh=cvh: e.tensor_tensor(out=t1[:, 0:hn], in0=BTi[:, h0:h0 + hn], in1=cvh, op=ALU.mult), reads=[BTi, csT], writes=[t1])
                        S.op("dve", lambda e, h0=h0, hn=hn, svh=svh: e.tensor_tensor(out=t2[:, 0:hn], in0=BTr[:, h0:h0 + hn], in1=svh, op=ALU.mult), reads=[BTr, snT], writes=[t2])
                        S.op("dve", lambda e, h0=h0, hn=hn: e.tensor_tensor(out=sim[:, h0:h0 + hn], in0=t1[:, 0:hn], in1=t2[:, 0:hn], op=ALU.add), reads=[t1, t2], writes=[sim])
                    for c0 in range(0, NT, 512):
                        n = min(512, NT - c0)
                        p0 = bank(3)
                        yc0 = c0 if d == 0 else (NCX + c0 if c0 < NL else 0)
                        S.op("pe", lambda e, c0=c0, n=n, p0=p0: e.matmul(p0[:, 0:n], lhsT=sCb[:, 0, :], rhs=sre[:, c0:c0 + n], start=True, stop=False), reads=[sCb, sre], writes=[p0])
                        S.op("pe", lambda e, c0=c0, n=n, p0=p0: e.matmul(p0[:, 0:n], lhsT=sCb[:, 1, :], rhs=sim[:, c0:c0 + n], start=False, stop=True), reads=[sCb, sim], writes=[p0])
                        if first:
                            S.op("dve", lambda e, c0=yc0, n=n, p0=p0: e.tensor_copy(out=Yacc[:, c0:c0 + n], in_=p0[:, 0:n]), reads=[p0], writes=[Yacc])
                        else:
                            S.op("dve", lambda e, c0=yc0, n=n, p0=p0: e.tensor_tensor(out=Yacc[:, c0:c0 + n], in0=p0[:, 0:n], in1=Yacc[:, c0:c0 + n], op=ALU.add), reads=[p0, Yacc], writes=[Yacc])
                    first = False
            for (h0, hn) in HALVES:
                S.op("dve", lambda e, o=o, h0=h0, hn=hn: e.scalar_tensor_tensor(out=Yacc[:, h0:h0 + hn], in0=usb[:, h0:h0 + hn], scalar=sdk[:, o:o + 1], in1=Yacc[:, h0:h0 + hn],
                                                                         op0=ALU.mult, op1=ALU.add), reads=[usb, sdk, Yacc], writes=[Yacc])
                S.op("dve", lambda e, h0=h0, hn=hn: e.tensor_tensor(out=t1[:, 0:hn], in0=Yacc[:, h0:h0 + hn], in1=Yacc[:, h0:h0 + hn], op=ALU.mult), reads=[Yacc], writes=[t1])
                S.op("dve", lambda e, hn=hn: e.tensor_scalar(out=t1[:, 0:hn], in0=t1[:, 0:hn], scalar1=0.044715, scalar2=1.0, op0=ALU.mult, op1=ALU.add), reads=[t1], writes=[t1])
                S.op("dve", lambda e, h0=h0, hn=hn: e.tensor_tensor(out=t1[:, 0:hn], in0=t1[:, 0:hn], in1=Yacc[:, h0:h0 + hn], op=ALU.mult), reads=[t1, Yacc], writes=[t1])
                S.op("act", lambda e, hn=hn: e.activation(out=t2[:, 0:hn], in_=t1[:, 0:hn], func=AF.Sigmoid, scale=float(2.0 * np.sqrt(2.0 / np.pi))), reads=[t1], writes=[t2])
                S.op("dve", lambda e, h0=h0, hn=hn: e.tensor_tensor(out=sre[:, h0:h0 + hn], in0=t2[:, 0:hn], in1=Yacc[:, h0:h0 + hn], op=ALU.mult), reads=[t2, Yacc], writes=[sre])
            S.dma("sp", yss[o * 128:(o + 1) * 128, NL:NT], sre[:, 0:NCX], reads=[sre], writes=[DB("yss", o, 1)])
            S.dma("sp", yss[o * 128:(o + 1) * 128, 0:NL], sre[:, NCX:NT], reads=[sre], writes=[DB("yss", o, 0)])
        al = Alloc(AR)
        gw = al.get(4 * 1024 * 2, BF16, "p (k n) -> p k n", k=4)
        YT = [al.get(4 * 512 * 2, BF16, "p (k n) -> p k n", k=4) for _ in range(2)]
        ZT = [al.get(4 * 512 * 2, BF16, "p (k n) -> p k n", k=4) for _ in range(2)]
        S.dma("pool", gw[:, :, :], glu_w[L].rearrange("(k p) n -> p k n", p=128), writes=[gw])
        tiles7 = TILES if with_ctx else TILES[:8]
        for ti, (c0, n) in enumerate(tiles7):
            yt = YT[ti % 2]
            zt = ZT[ti % 2]
            S.dma("sp", yt[:, :, 0:n], yss.rearrange("(k p) n -> p k n", p=128)[:, :, c0:c0 + n], reads=[DB("yss", oo, hh) for oo in range(4) for hh in range(2)], writes=[yt])
            S.dma("sp", zt[:, :, 0:n], proj[5632:6144, :].rearrange("(k p) n -> p k n", p=128)[:, :, c0:c0 + n], reads=[DB("proj", 44 + oo, ti) for oo in range(4)], writes=[zt])
            for ob in range(4):
                pa = next_bank()
                for k in range(4):
                    S.op("pe", lambda e, k=k, ob=ob, pa=pa, yt=yt, n=n: e.matmul(pa[:, 0:n], lhsT=gw[:, k, ob * 128:(ob + 1) * 128], rhs=yt[:, k, 0:n], start=(k == 0), stop=(k == 3)),
                         reads=[gw, yt], writes=[pa])
                pg = next_bank()
                for k in range(4):
                    S.op("pe", lambda e, k=k, ob=ob, pg=pg, yt=yt, n=n: e.matmul(pg[:, 0:n], lhsT=gw[:, k, 512 + ob * 128:512 + (ob + 1) * 128], rhs=yt[:, k, 0:n], start=(k == 0), stop=(k == 3)),
                         reads=[gw, yt], writes=[pg])
                S.op("act", lambda e, pg=pg, n=n: e.activation(out=stgf[2][:, 0:n], in_=pg[:, 0:n], func=AF.Sigmoid), reads=[pg], writes=[stgf[2]])
                S.op("dve", lambda e, pa=pa, n=n: e.tensor_tensor(out=stgf[2][:, 0:n], in0=pa[:, 0:n], in1=stgf[2][:, 0:n], op=ALU.mult), reads=[pa, stgf[2]], writes=[stgf[2]])
                st = next_stg()
                S.op("dve", lambda e, st=st, zt=zt, ob=ob, n=n: e.tensor_tensor(out=st[:, 0:n], in0=stgf[2][:, 0:n], in1=zt[:, ob, 0:n], op=ALU.mult), reads=[stgf[2], zt], writes=[st])
                S.dma("sp", a_d[1536 + ob * 128:1536 + (ob + 1) * 128, c0:c0 + n], st[:, 0:n], reads=[st], writes=[DB("a", 12 + ob, ti)])

        if stop == "S7":
            raise _Stop()
        al = Alloc(AR)
        wbr = al.get(KT * D * 2, BF16, "p (k n) -> p k n", k=KT)
        AT = [al.get(KT * 512 * 2, BF16, "p (k n) -> p k n", k=KT) for _ in range(2)]
        GT = [al.get(4 * 512 * 2, BF16, "p (i n) -> p i n", i=4) for _ in range(2)]
        MT = al.get(KT * 512 * 2, BF16, "p (k n) -> p k n", k=KT)
        for half in range(2):
            src = w_br[L].rearrange("(k p) n -> p k n", p=128)
            for kk in range(0, KT, 4):
                S.dma("pool", wbr[:, kk:kk + 4, half * 1024:(half + 1) * 1024], src[:, kk:kk + 4, half * 1024:(half + 1) * 1024], writes=[wbr])
        tiles8 = TILES if with_ctx else TILES[:8]
        gcnt = 0
        def load_at(ti_):
            c0_, n_ = tiles8[ti_]
            S.dma("sp", AT[ti_ % 2][:, :, 0:n_], a_d.rearrange("(k p) n -> p k n", p=128)[:, :, c0_:c0_ + n_],
                  reads=[DB("a", rb, t2) for rb in range(16) for t2 in range(9)], writes=[AT[ti_ % 2]])

        def load_gt(cnt):
            ti_, j_ = divmod(cnt, 16)
            c0_, n_ = tiles8[ti_]
            S.dma("sp", GT[cnt % 2][:, :, 0:n_], proj[6144:14336, :].rearrange("(i j p) n -> j p i n", i=4, j=16)[j_][:, :, c0_:c0_ + n_],
                  reads=[DB("proj", 48 + i * 16 + j_, ti_) for i in range(4)], writes=[GT[cnt % 2]])

        load_at(0)
        load_gt(0)
        for ti, (c0, n) in enumerate(tiles8):
            at = AT[ti % 2]
            if ti + 1 < len(tiles8):
                load_at(ti + 1)
            for j in range(16):
                gt = GT[gcnt % 2]
                gcnt += 1
                if gcnt < 16 * len(tiles8):
                    load_gt(gcnt)
                acc = stgf[0]
                tmpf = stgf[1]
                for i in range(4):
                    pb = next_bank()
                    for kk in range(4):
                        S.op("pe", lambda e, pb=pb, i=i, kk=kk, j=j, at=at, n=n: e.matmul(pb[:, 0:n], lhsT=wbr[:, i * 4 + kk, j * 128:(j + 1) * 128],
                                                                                  rhs=at[:, i * 4 + kk, 0:n], start=(kk == 0), stop=(kk == 3)),
                             reads=[wbr, at], writes=[pb])
                    if i == 0:
                        S.op("dve", lambda e, pb=pb, gt=gt, n=n: e.tensor_tensor(out=acc[:, 0:n], in0=pb[:, 0:n], in1=gt[:, 0, 0:n], op=ALU.mult),
                             reads=[pb, gt], writes=[acc])
                    else:
                        S.op("dve", lambda e, pb=pb, gt=gt, n=n, i=i: e.tensor_tensor(out=tmpf[:, 0:n], in0=pb[:, 0:n], in1=gt[:, i, 0:n], op=ALU.mult),
                             reads=[pb, gt], writes=[tmpf])
                        if i < 3:
                            S.op("dve", lambda e, n=n: e.tensor_tensor(out=acc[:, 0:n], in0=acc[:, 0:n], in1=tmpf[:, 0:n], op=ALU.add), reads=[acc, tmpf], writes=[acc])
                        else:
                            S.op("dve", lambda e, n=n, j=j: e.tensor_tensor(out=MT[:, j, 0:n], in0=acc[:, 0:n], in1=tmpf[:, 0:n], op=ALU.add), reads=[acc, tmpf], writes=[MT])
            S.dma("sp", mg.rearrange("(k p) n -> p k n", p=128)[:, :, c0:c0 + n], MT[:, :, 0:n], reads=[MT], writes=[DB("mg", 0, ti)])

        if stop == "S8":
            raise _Stop()
        al = Alloc(AR)
        wo = al.get(KT * D * 2, BF16, "p (k n) -> p k n", k=KT)
        MT2 = [al.get(KT * 256 * 2, BF16, "p (k n) -> p k n", k=KT) for _ in range(2)]
        Y = al.get(KT * 256 * 4, F32, "p (k n) -> p k n", k=KT)
        SQ = al.get(KT * 256 * 2, BF16, "p (k n) -> p k n", k=KT)
        XT9 = [al.get(KT * 256 * 4, F32, "p (k n) -> p k n", k=KT) for _ in range(2)]
        for half in range(2):
            src = w_o[L].rearrange("(k p) n -> p k n", p=128)
            for kk in range(0, KT, 4):
                S.dma("pool", wo[:, kk:kk + 4, half * 1024:(half + 1) * 1024], src[:, kk:kk + 4, half * 1024:(half + 1) * 1024], writes=[wo])
        ntok = NT if with_ctx else NL

        def load9(qi_):
            c0_ = qi_ * 256
            ti_ = min(c0_ // 512, 8)
            S.dma("sp", MT2[qi_ % 2][:, :, :], mg.rearrange("(k p) n -> p k n", p=128)[:, :, c0_:c0_ + 256], reads=[DB("mg", 0, ti_)], writes=[MT2[qi_ % 2]])
            S.dma("sp", XT9[qi_ % 2][:, :, :], xsrc.rearrange("(k p) n -> p k n", p=128)[:, :, c0_:c0_ + 256], reads=[DB(xsn, 0, ti_)], writes=[XT9[qi_ % 2]])

        for qi, c0 in enumerate(range(0, ntok, 256)):
            n = 256
            ti = min(c0 // 512, 8)
            s = 0 if c0 < NL else 1
            mt = MT2[qi % 2]
            XT = XT9[qi % 2]
            if qi == 0:
                load9(0)
            if c0 + 256 < ntok:
                load9(qi + 1)
            for j in range(16):
                pb = next_bank()
                for k in range(KT):
                    S.op("pe", lambda e, pb=pb, k=k, j=j, mt=mt: e.matmul(pb[:, 0:256], lhsT=wo[:, k, j * 128:(j + 1) * 128], rhs=mt[:, k, :],
                                                                     start=(k == 0), stop=(k == KT - 1)), reads=[wo, mt], writes=[pb])
                S.op("dve", lambda e, pb=pb, j=j: e.tensor_copy(out=Y[:, j, :], in_=pb[:, 0:256]), reads=[pb], writes=[Y])
                S.op("act", lambda e, j=j: e.activation(out=SQ[:, j, :], in_=Y[:, j, :], func=AF.Square), reads=[Y], writes=[SQ])
            if stop == "S9a":
                raise _Stop()
            pb = next_bank()
            for k in range(KT):
                S.op("pe", lambda e, k=k, pb=pb: e.matmul(pb[:, 0:256], lhsT=ones[:, :], rhs=SQ[:, k, :], start=(k == 0), stop=(k == KT - 1)),
                     reads=[ones, SQ], writes=[pb])
            S.op("act", lambda e, pb=pb: e.activation(out=rtmp[:, 0:256], in_=pb[:, 0:256], func=AF.Sqrt, bias=epsb[:, 0:1], scale=1.0 / D),
                 reads=[pb, epsb], writes=[rtmp])
            S.op("dve", lambda e: e.reciprocal(out=rstd[:, 0:256], in_=rtmp[:, 0:256]), reads=[rtmp], writes=[rstd])
            S.op("dve", lambda e: e.tensor_tensor(out=Y[:, :, :], in0=Y[:, :, :], in1=rstd[:, 0:256].unsqueeze(1).to_broadcast([128, KT, 256]), op=ALU.mult),
                 reads=[Y, rstd], writes=[Y])
            if stop == "S9b":
                raise _Stop()
            for k in range(KT):
                S.op("dve", lambda e, k=k, s=s, XT=XT: e.scalar_tensor_tensor(out=XT[:, k, :], in0=Y[:, k, :], scalar=Gmod[:, k, s:s + 1], in1=XT[:, k, :],
                                                                     op0=ALU.mult, op1=ALU.add), reads=[Y, Gmod, XT], writes=[XT])
            if last:
                S.dma("sp", outT.rearrange("(k p) n -> p k n", p=128)[:, :, c0:c0 + n], XT[:, :, :], reads=[XT], writes=[DB("out", 0, qi)])
            else:
                S.dma("sp", xs.rearrange("(k p) n -> p k n", p=128)[:, :, c0:c0 + n], XT[:, :, :], reads=[XT], writes=[DB("xs", 0, ti)])
            if stop == "S9c":
                raise _Stop()
      except _Stop:
        break

    finals = [o for o in S.qops["sp"] if o.dma][-40:]
    S.emit(final_ops=finals)
    return nc


def _fm(v, nb):
    return np.ascontiguousarray(np.swapaxes(v.reshape(v.shape[:-1] + (nb, 128)), -1, -2))


def _na_bias(rpb):
    pats = [(0, 0), (2, 0), (8, 4), (60, 54), (62, 54)]
    drow = np.zeros((5, 128, 640), np.int64)
    dcol = np.zeros((5, 128, 640), np.int64)
    mask = np.zeros((5, 128, 640), bool)
    qc = np.arange(64)
    kc = np.arange(64)
    cs = np.clip(qc - 8, 0, 48)
    inwin = (kc[None, :] >= cs[:, None]) & (kc[None, :] < cs[:, None] + 16)
    dc = np.clip(kc[None, :] - qc[:, None] + 15, 0, 30)
    for pi, (r0, lo) in enumerate(pats):
        for dr in range(2):
            r = r0 + dr
            bs = min(max(r - 4, 0), 56)
            for ko in range(10):
                kr = lo + ko
                inb = bs <= kr < bs + 8
                drw = min(max(kr - r + 7, 0), 14)
                drow[pi, dr * 64:(dr + 1) * 64, ko * 64:(ko + 1) * 64] = drw
                dcol[pi, dr * 64:(dr + 1) * 64, ko * 64:(ko + 1) * 64] = dc
                mask[pi, dr * 64:(dr + 1) * 64, ko * 64:(ko + 1) * 64] = inwin & inb
    g = rpb[:, :, drow, dcol]
    g = np.where(mask[None, None], g, np.float32(-30000.0)).astype(np.float32)
    return np.ascontiguousarray(g.transpose(0, 1, 3, 2, 4))


def _pool_edge():
    out = np.zeros((128, 2, 4, 2, 8), np.float32)
    for si, n in enumerate((NL, NCX)):
        for g, w in enumerate((2, 4, 8, 16)):
            t = np.arange(n)
            lo = np.clip(t - w // 2, 0, n)
            hi = np.clip(t + w - w // 2, 0, n)
            inv = (1.0 / (hi - lo)).astype(np.float32)
            out[:, si, g, 0, :] = inv[None, 0:8]
            out[:, si, g, 1, :] = inv[None, n - 8:n]
    return out


_NC_CACHE = {}


def _prep(x, c, ctx, c_ctx, w_mod, b_mod, g_pre, g_post, w_in, b_gate, na_rpb, pool_w,
           pool_scale, conv_w, ssm_a_re, ssm_a_im, ssm_log_dt, ssm_b_re, ssm_b_im,
           ssm_c_re, ssm_c_im, ssm_d, glu_w, w_br, w_o):
    f = np.float32
    x = np.asarray(x, f); ctx = np.asarray(ctx, f); c = np.asarray(c, f); c_ctx = np.asarray(c_ctx, f)
    shared = {
        "w_mod": np.ascontiguousarray(w_mod, f),
        "b_mod": _fm(np.asarray(b_mod, f), 48),
        "g_pre": _fm(np.asarray(g_pre, f), 16),
        "g_post": _fm(np.asarray(g_post, f), 16),
        "w_in": np.ascontiguousarray(w_in, f),
        "b_gate": _fm(np.asarray(b_gate, f), 64),
        "na_bias": _na_bias(np.asarray(na_rpb, f)),
        "pool_w": np.ascontiguousarray(np.asarray(pool_w, f).transpose(0, 2, 1, 3)),
        "pool_sc": _fm(np.asarray(pool_scale, f), 4),
        "pool_edge": _pool_edge(),
        "conv_w": np.ascontiguousarray(np.asarray(conv_w, f).reshape(DEPTH, 3, 4, 128).transpose(0, 3, 2, 1)),
        "ssm_d": _fm(np.asarray(ssm_d, f), 4),
        "glu_w": np.ascontiguousarray(glu_w, f),
        "w_br": np.ascontiguousarray(w_br, f),
        "w_o": np.ascontiguousarray(w_o, f),
        "ident": np.eye(128, dtype=f),
        "iota17": np.tile(np.arange(17, dtype=f)[None], (128, 1)),
        "iota_nt": np.tile(np.arange(NT, dtype=f)[None], (128, 1)),
    }

    def upl(a):
        a = np.asarray(a, f).reshape(DEPTH, 2, 16, 2, 64)
        return np.ascontiguousarray(a.transpose(0, 3, 4, 1, 2).reshape(DEPTH, 128, 32))

    shared["ssm_are"] = upl(ssm_a_re)
    shared["ssm_aim"] = upl(ssm_a_im)
    shared["ssm_ldt"] = upl(np.broadcast_to(np.asarray(ssm_log_dt, f)[..., None], (DEPTH, 2, 32, 64)))
    Bp = np.zeros((DEPTH, 128, 32, 2, 128), f)
    Cp = np.zeros((DEPTH, 128, 32, 2, 128), f)
    bre = np.asarray(ssm_b_re, f); bim = np.asarray(ssm_b_im, f)
    cre = np.asarray(ssm_c_re, f); cim = np.asarray(ssm_c_im, f)
    for d in range(2):
        for j in range(16):
            for gl in range(2):
                g = 2 * j + gl
                u = d * 16 + j
                ch0 = (j % 4) * 32 + gl * 16
                Bp[:, gl * 64:(gl + 1) * 64, u, 0, ch0:ch0 + 16] = bre[:, d, g]
                Bp[:, gl * 64:(gl + 1) * 64, u, 1, ch0:ch0 + 16] = bim[:, d, g]
                Cp[:, gl * 64:(gl + 1) * 64, u, 0, ch0:ch0 + 16] = cre[:, d, g].transpose(0, 2, 1)
                Cp[:, gl * 64:(gl + 1) * 64, u, 1, ch0:ch0 + 16] = cim[:, d, g].transpose(0, 2, 1)
    shared["ssm_B"] = Bp
    shared["ssm_C"] = Cp
    in_maps = []
    for core in range(8):
        b = core % 4
        m = dict(shared)
        m["x0"] = np.ascontiguousarray(np.concatenate([x[b].T, ctx[b].T], axis=1))
        cv = np.stack([c[b].reshape(16, 128).T, c_ctx.reshape(16, 128).T], axis=-1)
        m["cvec"] = np.ascontiguousarray(cv, f)
        in_maps.append(m)
    return in_maps


def kernel(x, c, ctx, c_ctx, w_mod, b_mod, g_pre, g_post, w_in, b_gate, na_rpb, pool_w,
           pool_scale, conv_w, ssm_a_re, ssm_a_im, ssm_log_dt, ssm_b_re, ssm_b_im,
           ssm_c_re, ssm_c_im, ssm_d, glu_w, w_br, w_o):
    in_maps = _prep(x, c, ctx, c_ctx, w_mod, b_mod, g_pre, g_post, w_in, b_gate, na_rpb, pool_w,
                    pool_scale, conv_w, ssm_a_re, ssm_a_im, ssm_log_dt, ssm_b_re, ssm_b_im,
                    ssm_c_re, ssm_c_im, ssm_d, glu_w, w_br, w_o)
    if "nc" not in _NC_CACHE:
        _NC_CACHE["nc"] = build_nc()
    res = run_bass_kernel_spmd(_NC_CACHE["nc"], in_maps, core_ids=list(range(8)))
    out = np.stack([np.ascontiguousarray(res.results[b]["outT"].T) for b in range(4)], axis=0)
    return out.astype(np.float32)
```

```python
import numpy as np
import concourse.bass as bass
import concourse.mybir as mybir
from concourse.bass_utils import run_bass_kernel_spmd

F32 = mybir.dt.float32
BF16 = mybir.dt.bfloat16
I32 = mybir.dt.int32
ALU = mybir.AluOpType
AF = mybir.ActivationFunctionType

DEPTH = 4
D = 2048
NL = 4096
NCX = 256
NT = NL + NCX
KT = 16
TILES = [(i * 512, 512) for i in range(8)] + [(4096, 256)]
TWO_PI = float(2 * np.pi)
PI = float(np.pi)
MAGIC = 12582912.0

COMPUTE_Q = ("pe", "act", "dve", "pool")
EPOCH = 20000
NDMA_SEM = 12


class Buf:
    __slots__ = ("name", "last_w", "readers", "t", "bufs")

    def __init__(self, name, t=None):
        self.name = name
        self.last_w = None
        self.readers = []
        self.t = t
        self.bufs = [self]

    def __getitem__(self, k):
        return self.t[k]


class View:
    __slots__ = ("t", "bufs")

    def __init__(self, t, bufs):
        self.t = t
        self.bufs = bufs

    def __getitem__(self, k):
        return self.t[k]


class Op:
    __slots__ = ("q", "fn", "deps", "dma", "sig", "sem", "val", "idx", "prev_same_sem", "seqno")

    def __init__(self, q, fn, dma):
        self.q = q
        self.fn = fn
        self.deps = []
        self.dma = dma
        self.sig = dma
        self.sem = None
        self.val = None
        self.idx = -1
        self.prev_same_sem = None


def _expand(lst):
    out = []
    for b in lst:
        out.extend(b.bufs)
    return out


class Sched:
    def __init__(self, nc):
        self.nc = nc
        self.qops = {q: [] for q in ("pe", "act", "dve", "pool", "sp")}

    def sb(self, name, shape, dtype):
        return Buf(name, self.nc.alloc_sbuf_tensor(name, list(shape), dtype))

    def _add(self, q, fn, reads, writes, dma):
        op = Op(q, fn, dma)
        self.seq = getattr(self, "seq", 0) + 1
        op.idx = self.seq
        reads = _expand(reads)
        writes = _expand(writes)
        deps = []
        for b in reads:
            if b.last_w is not None:
                deps.append((b.last_w, "raw"))
        for b in writes:
            if b.last_w is not None:
                deps.append((b.last_w, "waw"))
            for r in b.readers:
                deps.append((r, "war"))
        seen = set()
        best = {}
        for d, kind in deps:
            if d is op or id(d) in seen:
                continue
            seen.add(id(d))
            if d.q == q and not d.dma and not dma:
                if q == "pe" or kind == "war":
                    continue
            if d.dma:
                op.deps.append(d)
                d.sig = True
            else:
                cur = best.get(d.q)
                if cur is None or d.seqno > cur.seqno:
                    best[d.q] = d
        for d in best.values():
            op.deps.append(d)
            d.sig = True
        op.seqno = self.seq
        for b in reads:
            if not dma:
                b.readers = [r for r in b.readers if r.dma or r.q != q]
            b.readers.append(op)
        for b in writes:
            b.last_w = op
            b.readers = []
        self.qops[q].append(op)
        return op

    def op(self, q, fn, reads=(), writes=()):
        return self._add(q, fn, reads, writes, False)

    def dma(self, q, out, in_, reads=(), writes=()):
        return self._add(q, lambda e: e.dma_start(out=out, in_=in_), reads, writes, True)

    def emit(self, final_ops=()):
        nc = self.nc
        csem = {}
        for q in COMPUTE_Q:
            n = sum(1 for o in self.qops[q] if o.sig and not o.dma)
            ne = max(1, (n + EPOCH - 1) // EPOCH)
            csem[q] = [nc.alloc_semaphore(name=f"s_{q}{i}") for i in range(ne)]
        dsem = {}
        for q in ("sp", "act", "pool"):
            if any(o.dma for o in self.qops[q]):
                dsem[q] = [nc.alloc_semaphore(name=f"d_{q}{i}") for i in range(NDMA_SEM)]
        for q, lst in self.qops.items():
            cnt = 0
            dcnt = 0
            last_on_sem = {}
            for o in lst:
                if o.dma:
                    k = dcnt % NDMA_SEM
                    o.sem = dsem[q][k]
                    o.val = 16 * (dcnt // NDMA_SEM + 1)
                    o.prev_same_sem = last_on_sem.get(k)
                    last_on_sem[k] = o
                    dcnt += 1
                elif o.sig:
                    o.sem = csem[q][cnt // EPOCH]
                    o.val = cnt % EPOCH + 1
                    o.idx = cnt
                    cnt += 1
        engs = {"pe": "tensor", "act": "scalar", "dve": "vector", "pool": "gpsimd", "sp": "sync"}
        with nc.Block() as block:
            for q, lst in self.qops.items():
                if not lst:
                    continue

                def body(e, q=q, lst=lst):
                    waited = {}
                    dwaited = set()

                    def wait_for(d):
                        if d.dma:
                            if id(d) in dwaited:
                                return
                            dwaited.add(id(d))
                            e.wait_ge(d.sem, d.val)
                        else:
                            if waited.get(d.q, -1) >= d.idx:
                                return
                            waited[d.q] = d.idx
                            e.wait_ge(d.sem, d.val)

                    for o in lst:
                        for d in o.deps:
                            wait_for(d)
                        if o.dma and o.prev_same_sem is not None:
                            wait_for(o.prev_same_sem)
                        ins = o.fn(e)
                        if o.dma:
                            ins.then_inc(o.sem, 16)
                        elif o.sig:
                            ins.then_inc(o.sem, 1)
                    if q == "sp":
                        for d in final_ops:
                            wait_for(d)

                getattr(block, engs[q])(body)


CH = 512
NCHUNK = 164


class Arena:
    def __init__(self, S):
        self.t = S.nc.alloc_sbuf_tensor("arena", [128, CH * NCHUNK], BF16)
        self.chunks = [Buf(f"ch{i}") for i in range(NCHUNK)]

    def view(self, off_b, nbytes, dtype=BF16, pat=None, **kw):
        assert off_b % 4 == 0 and off_b + nbytes <= CH * NCHUNK * 2, (off_b, nbytes)
        e0 = off_b // 2
        e1 = (off_b + nbytes) // 2
        ap = self.t[:, e0:e1]
        if dtype != BF16:
            ap = ap.bitcast(dtype)
        if pat is not None:
            ap = ap.rearrange(pat, **kw)
        c0 = e0 // CH
        c1 = (e1 - 1) // CH
        return View(ap, self.chunks[c0:c1 + 1])


def sub(view, c0, n, esz):
    b0 = (c0 * esz) // (CH * 2)
    b1 = ((c0 + n) * esz - 1) // (CH * 2)
    return View(view.t[:, c0:c0 + n], view.bufs[b0:b1 + 1])


class Alloc:
    def __init__(self, arena):
        self.a = arena
        self.off = 0

    def get(self, nbytes, dtype=BF16, pat=None, **kw):
        self.off = (self.off + 1023) // 1024 * 1024
        v = self.a.view(self.off, nbytes, dtype, pat, **kw)
        self.off += nbytes
        return v


class _Stop(Exception):
    pass


def build_nc(depth=DEPTH, dbg=None, stop=None):
    nc = bass.Bass("TRN2", target_bir_lowering=False)
    S = Sched(nc)
    AR = Arena(S)

    def din(name, shape, dt=F32):
        return nc.dram_tensor(name, list(shape), dt, kind="ExternalInput").ap()

    def dscr(name, shape, dt):
        return nc.dram_tensor(name, list(shape), dt, kind=("ExternalOutput" if dbg else "Internal")).ap()

    x0 = din("x0", [D, NT])
    cvec = din("cvec", [128, KT, 2])
    w_mod = din("w_mod", [DEPTH, D, 3 * D])
    b_mod = din("b_mod", [DEPTH, 128, 48])
    g_pre = din("g_pre", [DEPTH, 128, KT])
    g_post = din("g_post", [DEPTH, 128, KT])
    w_in = din("w_in", [DEPTH, D, 14336])
    b_gate = din("b_gate", [DEPTH, 128, 64])
    na_bias = din("na_bias", [DEPTH, 8, 128, 5, 640])
    pool_w = din("pool_w", [DEPTH, 128, 4, 128])
    pool_sc = din("pool_sc", [DEPTH, 128, 4])
    pool_edge = din("pool_edge", [128, 2, 4, 2, 8])
    conv_w = din("conv_w", [DEPTH, 128, 4, 3])
    ssm_are = din("ssm_are", [DEPTH, 128, 32])
    ssm_aim = din("ssm_aim", [DEPTH, 128, 32])
    ssm_ldt = din("ssm_ldt", [DEPTH, 128, 32])
    ssm_B = din("ssm_B", [DEPTH, 128, 32, 2, 128])
    ssm_C = din("ssm_C", [DEPTH, 128, 32, 2, 128])
    ssm_d = din("ssm_d", [DEPTH, 128, 4])
    glu_w = din("glu_w", [DEPTH, 512, 1024])
    w_br = din("w_br", [DEPTH, D, D])
    w_o = din("w_o", [DEPTH, D, D])
    ident_in = din("ident", [128, 128])
    iota17 = din("iota17", [128, 17])
    iota_nt = din("iota_nt", [128, NT])
    outT = nc.dram_tensor("outT", [D, NL], F32, kind="ExternalOutput").ap()

    xs = dscr("xs", [D, NT], F32)
    hT = dscr("hT", [D, NT], BF16)
    proj = dscr("proj", [14336, NT], BF16)
    vT = dscr("vT", [NT, 512], BF16)
    a_d = dscr("a_d", [D, NT], BF16)
    mg = dscr("mg", [D, NT], BF16)
    yss = dscr("yss", [512, NT], BF16)
    dbufs = {}
    dbg_pob = nc.dram_tensor("dbg_pob", [4, 128, NL], BF16, kind="ExternalOutput").ap() if dbg else None

    def DB(name, rb, ti):
        k = (name, rb, ti)
        if k not in dbufs:
            dbufs[k] = Buf(str(k))
        return dbufs[k]

    def DBrow(name, rb):
        return [DB(name, rb, ti) for ti in range(9)]

    NTI = len(TILES)

    ident = S.sb("identb", [128, 128], BF16)
    ones = S.sb("onesb", [128, 128], BF16)
    epsb = S.sb("epsb", [128, 1], F32)
    cact = S.sb("cact", [128, KT, 2], BF16)
    cv32 = S.sb("cv32", [128, KT, 2], F32)
    modt = S.sb("modt", [128, 48, 2], F32)
    bmod = S.sb("bmod", [128, 48], F32)
    gpre = S.sb("gpre", [128, KT], F32)
    gpost = S.sb("gpost", [128, KT], F32)
    Amod = S.sb("Amod", [128, KT, 2], F32)
    Gmod = S.sb("Gmod", [128, KT, 2], F32)
    bgate = S.sb("bgate", [128, 64], F32)
    psc = S.sb("psc", [128, 4], F32)
    pedge = S.sb("pedge", [128, 2, 4, 2, 8], F32)
    cw = S.sb("cw", [128, 4, 3], F32)
    sdk = S.sb("sdk", [128, 4], F32)
    io17 = S.sb("io17", [128, 17], F32)
    (s_are, s_aim, s_ldt, s_dt, s_x1, s_th, s_rho, s_thr, s_sn1, s_cs1, s_nr, s_ni, s_den, s_fr, s_fi, s_t1, s_t2) = [
        S.sb(f"ss{i}", [128, 32], F32) for i in range(17)]
    s_ti = S.sb("ssti", [128, 32], I32)
    s_thq = S.sb("ssthq", [128, 32], F32)
    halfpi = S.sb("halfpi", [128, 1], F32)
    sB = S.sb("sB", [128, 2, 128], F32)
    sC = S.sb("sC", [128, 2, 128], F32)
    sT = S.sb("sT", [128, 128], F32)
    sBb = S.sb("sBb", [128, 2, 128], BF16)
    sCb = S.sb("sCb", [128, 3, 128], BF16)
    sWT = S.sb("sWT", [128, 2, 128], BF16)
    stg = [S.sb(f"stg{i}", [128, 512], BF16) for i in range(4)]
    stgf = [S.sb(f"stgf{i}", [128, 512], F32) for i in range(4)]
    rstd = S.sb("rstd", [128, 512], F32)
    rtmp = S.sb("rtmp", [128, 512], F32)

    PSB = nc.alloc_psum_tensor("psall", [128, 8, 512], F32)
    BK = [Buf(f"bank{i}") for i in range(8)]

    def bank(i):
        return View(PSB[:, i, :], [BK[i]])

    def bank2(i):
        return View(PSB[:, i:i + 2, :].rearrange("p a b -> p (a b)"), [BK[i], BK[i + 1]])

    def bankbf(i):
        return View(PSB[:, i, :].bitcast(BF16), [BK[i]])

    S.dma("pool", ident[:, :], ident_in, writes=[ident])
    S.op("dve", lambda e: e.memset(ones[:, :], 1.0), writes=[ones])
    S.op("dve", lambda e: e.memset(epsb[:, :], 1e-6), writes=[epsb])
    S.op("dve", lambda e: e.memset(halfpi[:, :], PI / 2), writes=[halfpi])
    S.dma("sp", cv32[:, :, :], cvec, writes=[cv32])
    S.op("act", lambda e: e.activation(out=cact[:, :, :], in_=cv32[:, :, :], func=AF.Silu), reads=[cv32], writes=[cact])
    S.dma("sp", pedge[:, :, :, :, :], pool_edge, writes=[pedge])
    S.dma("sp", io17[:, :], iota17, writes=[io17])

    stg_i = [0]

    def next_stg():
        stg_i[0] += 1
        return stg[stg_i[0] % 4]

    bank_i = [0]

    def next_bank(n=4):
        bank_i[0] += 1
        return bank(bank_i[0] % n)

    def w_panel_view(al):
        return al.get(KT * 1024 * 2, BF16, "p (k n) -> p k n", k=KT)

    def load_w_panel(dst, src_l, col0, ncols=1024):
        src = src_l.rearrange("(k p) n -> p k n", p=128)
        for kk in range(0, KT, 4):
            S.dma("pool", dst[:, kk:kk + 4, 0:ncols], src[:, kk:kk + 4, col0:col0 + ncols], writes=[dst])

    for L in range(depth):
      try:
        xsrc = x0 if L == 0 else xs
        xsn = "x0" if L == 0 else "xs"
        last = (L == depth - 1) and not dbg
        with_ctx = (L < DEPTH - 1) and not last
        al = Alloc(AR)
        WP = [w_panel_view(al), w_panel_view(al)]
        S.dma("sp", bmod[:, :], b_mod[L], writes=[bmod])
        S.dma("sp", gpre[:, :], g_pre[L], writes=[gpre])
        S.dma("sp", gpost[:, :], g_post[L], writes=[gpost])
        S.dma("sp", bgate[:, :], b_gate[L], writes=[bgate])
        S.dma("sp", psc[:, :], pool_sc[L], writes=[psc])
        S.dma("sp", cw[:, :, :], conv_w[L], writes=[cw])
        S.dma("sp", sdk[:, :], ssm_d[L], writes=[sdk])
        for pp in range(6):
            wp = WP[pp % 2]
            load_w_panel(wp, w_mod[L], pp * 1024)
            for bl in range(8):
                j = pp * 8 + bl
                pb = next_bank()
                for k in range(KT):
                    S.op("pe", lambda e, k=k, bl=bl, wp=wp, pb=pb: e.matmul(pb[:, 0:2], lhsT=wp[:, k, bl * 128:(bl + 1) * 128],
                                                                    rhs=cact[:, k, :], start=(k == 0), stop=(k == KT - 1)),
                         reads=[wp, cact], writes=[pb])
                S.op("dve", lambda e, j=j, pb=pb: e.tensor_scalar(out=modt[:, j, :], in0=pb[:, 0:2], scalar1=bmod[:, j:j + 1], scalar2=None, op0=ALU.add),
                     reads=[pb, bmod], writes=[modt])
        S.op("dve", lambda e: e.scalar_tensor_tensor(out=Amod[:, :, :], in0=modt[:, 16:32, :], scalar=1.0,
                                                     in1=gpre[:, :].unsqueeze(2).to_broadcast([128, KT, 2]), op0=ALU.add, op1=ALU.mult),
             reads=[modt, gpre], writes=[Amod])
        S.op("dve", lambda e: e.tensor_tensor(out=Gmod[:, :, :], in0=modt[:, 32:48, :],
                                              in1=gpost[:, :].unsqueeze(2).to_broadcast([128, KT, 2]), op=ALU.mult),
             reads=[modt, gpost], writes=[Gmod])

        if stop == "S1":
            raise _Stop()
        XTS = [al.get(KT * 512 * 4, F32, "p (k n) -> p k n", k=KT) for _ in range(2)]
        HBS = [al.get(KT * 512 * 2, BF16, "p (k n) -> p k n", k=KT) for _ in range(2)]

        def load_x2(ti_):
            c0_, n_ = TILES[ti_]
            S.dma("sp", XTS[ti_ % 2][:, :, 0:n_], xsrc.rearrange("(k p) n -> p k n", p=128)[:, :, c0_:c0_ + n_],
                  reads=[DB(xsn, 0, ti_)], writes=[XTS[ti_ % 2]])

        load_x2(0)
        for ti, (c0, n) in enumerate(TILES):
            s = 0 if ti < 8 else 1
            xt = XTS[ti % 2]
            hb = HBS[ti % 2]
            if ti + 1 < NTI:
                load_x2(ti + 1)
            S.op("act", lambda e, n=n, xt=xt, hb=hb: e.activation(out=hb[:, :, 0:n], in_=xt[:, :, 0:n], func=AF.Square), reads=[xt], writes=[hb])
            pb = next_bank()
            for k in range(KT):
                S.op("pe", lambda e, k=k, n=n, pb=pb, hb=hb: e.matmul(pb[:, 0:n], lhsT=ones[:, :], rhs=hb[:, k, 0:n], start=(k == 0), stop=(k == KT - 1)),
                     reads=[ones, hb], writes=[pb])
            S.op("act", lambda e, n=n, pb=pb: e.activation(out=rtmp[:, 0:n], in_=pb[:, 0:n], func=AF.Sqrt, bias=epsb[:, 0:1], scale=1.0 / D),
                 reads=[pb, epsb], writes=[rtmp])
            S.op("dve", lambda e, n=n: e.reciprocal(out=rstd[:, 0:n], in_=rtmp[:, 0:n]), reads=[rtmp], writes=[rstd])
            S.op("dve", lambda e, n=n, xt=xt: e.tensor_tensor(out=xt[:, :, 0:n], in0=xt[:, :, 0:n],
                                                       in1=rstd[:, 0:n].unsqueeze(1).to_broadcast([128, KT, n]), op=ALU.mult),
                 reads=[xt, rstd], writes=[xt])
            for k in range(KT):
                S.op("act", lambda e, k=k, n=n, s=s, xt=xt, hb=hb: e.activation(out=hb[:, k, 0:n], in_=xt[:, k, 0:n], func=AF.Identity,
                                                                 bias=modt[:, k, s:s + 1], scale=Amod[:, k, s:s + 1]),
                     reads=[xt, modt, Amod], writes=[hb])
            S.dma("sp", hT.rearrange("(k p) n -> p k n", p=128)[:, :, c0:c0 + n], hb[:, :, 0:n], reads=[hb], writes=[DB("hT", 0, ti)])

        if stop == "S2":
            raise _Stop()
        al = Alloc(AR)
        WP = [w_panel_view(al), w_panel_view(al)]
        HB = [al.get(KT * 512 * 2, BF16, "p (k n) -> p k n", k=KT) for _ in range(2)]
        vst = al.get(512 * 2 * 2, BF16, "p (a n) -> p a n", a=2)
        load_w_panel(WP[0], w_in[L], 0)
        hcnt = 0

        def load_h(cnt):
            ti_ = cnt % NTI
            c0_, n_ = TILES[ti_]
            hb_ = HB[cnt % 2]
            S.dma("sp", hb_[:, :, 0:n_], hT.rearrange("(k p) n -> p k n", p=128)[:, :, c0_:c0_ + n_], reads=[DB("hT", 0, ti_)], writes=[hb_])

        load_h(0)
        for pp in range(14):
            wp = WP[pp % 2]
            if pp + 1 < 14:
                load_w_panel(WP[(pp + 1) % 2], w_in[L], (pp + 1) * 1024)
            for ti, (c0, n) in enumerate(TILES):
                hbt = HB[hcnt % 2]
                hcnt += 1
                if hcnt < 14 * NTI:
                    load_h(hcnt)
                for bl in range(8):
                    bi = pp * 8 + bl
                    if 8 <= bi < 12:
                        if bi > 8:
                            continue
                        for tb in range(n // 128):
                            pb = next_bank()
                            for k in range(KT):
                                S.op("pe", lambda e, k=k, tb=tb, pb=pb, hbt=hbt, wp=wp: e.matmul(pb[:, :], lhsT=hbt[:, k, tb * 128:(tb + 1) * 128],
                                                                                         rhs=wp[:, k, 0:512], start=(k == 0), stop=(k == KT - 1)),
                                     reads=[hbt, wp], writes=[pb])
                            st = next_stg()
                            S.op("dve", lambda e, pb=pb, st=st: e.tensor_copy(out=st[:, :], in_=pb[:, :]), reads=[pb], writes=[st])
                            S.dma("sp", vT[c0 + tb * 128:c0 + (tb + 1) * 128, :], st[:, :], reads=[st], writes=[DB("vT", 0, ti)])
                        continue
                    pb = next_bank()
                    for k in range(KT):
                        S.op("pe", lambda e, k=k, bl=bl, pb=pb, hbt=hbt, wp=wp, n=n: e.matmul(pb[:, 0:n], lhsT=wp[:, k, bl * 128:(bl + 1) * 128],
                                                                                      rhs=hbt[:, k, 0:n], start=(k == 0), stop=(k == KT - 1)),
                             reads=[hbt, wp], writes=[pb])
                    st = next_stg()
                    if bi < 4:
                        S.op("dve", lambda e, pb=pb, st=st, n=n: e.tensor_scalar(out=st[:, 0:n], in0=pb[:, 0:n], scalar1=0.125, scalar2=None, op0=ALU.mult),
                             reads=[pb], writes=[st])
                    elif bi >= 48:
                        S.op("act", lambda e, pb=pb, st=st, n=n, bi=bi: e.activation(out=st[:, 0:n], in_=pb[:, 0:n], func=AF.Sigmoid,
                                                                                  bias=bgate[:, bi - 48:bi - 47], scale=1.0),
                             reads=[pb, bgate], writes=[st])
                    elif (12 <= bi < 16) or (20 <= bi < 24) or (36 <= bi < 40) or (44 <= bi < 48):
                        S.op("act", lambda e, pb=pb, st=st, n=n: e.activation(out=st[:, 0:n], in_=pb[:, 0:n], func=AF.Silu), reads=[pb], writes=[st])
                    else:
                        S.op("dve", lambda e, pb=pb, st=st, n=n: e.tensor_copy(out=st[:, 0:n], in_=pb[:, 0:n]), reads=[pb], writes=[st])
                    S.dma("sp", proj[bi * 128:(bi + 1) * 128, c0:c0 + n], st[:, 0:n], reads=[st], writes=[DB("proj", bi, ti)])

        if stop == "S3":
            raise _Stop()
        al = Alloc(AR)
        qs = al.get(NT * 2)
        ks = al.get(NT * 2)
        szs = al.get(NT * 2)
        ao = al.get(NT * 2)
        vs = al.get(34 * 128 * 2, BF16, "p (b c) -> p b c", b=34)
        bt = al.get(2 * 5 * 640 * 2, BF16, "p (h t k) -> p h t k", h=2, t=5)
        PT = [al.get(896 * 2) for _ in range(2)]
        rden = S.sb(f"rden{L}", [128, 128], F32) if L == 0 else rden
        otmp = S.sb(f"otmp{L}", [128, 128], F32) if L == 0 else otmp
        acnt = 0
        for hp in range(4):
            S.dma("sp", qs[:, :], proj[hp * 128:(hp + 1) * 128, :], reads=DBrow("proj", hp), writes=[qs])
            S.dma("sp", ks[:, :], proj[512 + hp * 128:512 + (hp + 1) * 128, :], reads=DBrow("proj", 4 + hp), writes=[ks])
            S.dma("sp", szs[:, :], proj[1536 + hp * 128:1536 + (hp + 1) * 128, :], reads=DBrow("proj", 12 + hp), writes=[szs])
            S.dma("sp", vs[:, :, :], vT.rearrange("(b p) c -> p b c", p=128)[:, :, hp * 128:(hp + 1) * 128], reads=DBrow("vT", 0), writes=[vs])
            S.dma("pool", bt[:, :, :, :], na_bias[L, 2 * hp:2 * hp + 2].rearrange("h p t k -> p h t k"), writes=[bt])
            qblocks = [("band", i) for i in range(32)] + ([("ctx", 0), ("ctx", 1)] if with_ctx else [])
            for kind, i in qblocks:
                for hd in range(2):
                    pbs = 64 * hd
                    sb2 = bank2(4 + 2 * (acnt % 2))
                    oc = bank(acnt % 2 + 2) if False else bank(2 + acnt % 2)
                    pt = PT[acnt % 2]
                    acnt += 1
                    if kind == "band":
                        r0 = 2 * i
                        pat = 0 if i == 0 else 1 if i == 1 else 3 if i == 30 else 4 if i == 31 else 2
                        lo = min(max(r0 - 4, 0), 54)
                        k0 = lo * 64
                        q0 = r0 * 64
                        nkb = 7
                        for b in range(5):
                            S.op("pe", lambda e, b=b, sb2=sb2, pbs=pbs, k0=k0, q0=q0: e.matmul(sb2[:, b * 128:(b + 1) * 128], lhsT=ks[pbs:pbs + 64, k0 + b * 128:k0 + (b + 1) * 128],
                                                                                     rhs=qs[pbs:pbs + 64, q0:q0 + 128], start=True, stop=False),
                                 reads=[ks, qs], writes=[sb2])
                            S.op("pe", lambda e, b=b, sb2=sb2, hd=hd, pat=pat: e.matmul(sb2[:, b * 128:(b + 1) * 128], lhsT=bt[:, hd, pat, b * 128:(b + 1) * 128],
                                                                                    rhs=ident[:, :], start=False, stop=True),
                                 reads=[bt, ident], writes=[sb2])
                        for cb in range(2):
                            S.op("pe", lambda e, cb=cb, sb2=sb2, pbs=pbs, q0=q0: e.matmul(sb2[:, 640 + cb * 128:640 + (cb + 1) * 128],
                                                                                      lhsT=ks[pbs:pbs + 64, NL + cb * 128:NL + (cb + 1) * 128],
                                                                                      rhs=qs[pbs:pbs + 64, q0:q0 + 128], start=True, stop=True),
                                 reads=[ks, qs], writes=[sb2])
                        vblk = [k0 // 128 + b for b in range(5)] + [32, 33]
                    else:
                        q0 = NL + i * 128
                        nkb = 2
                        for cb in range(2):
                            S.op("pe", lambda e, cb=cb, sb2=sb2, pbs=pbs, q0=q0: e.matmul(sb2[:, cb * 128:(cb + 1) * 128],
                                                                                      lhsT=ks[pbs:pbs + 64, NL + cb * 128:NL + (cb + 1) * 128],
                                                                                      rhs=qs[pbs:pbs + 64, q0:q0 + 128], start=True, stop=True),
                                 reads=[ks, qs], writes=[sb2])
                        vblk = [32, 33]
                    nk = nkb * 128
                    S.op("act", lambda e, sb2=sb2, pt=pt, nk=nk: e.activation(out=pt[:, 0:nk], in_=sb2[:, 0:nk], func=AF.Exp), reads=[sb2], writes=[pt])
                    for kb in range(nkb):
                        S.op("pe", lambda e, kb=kb, oc=oc, pt=pt, vb=vblk[kb], nkb=nkb: e.matmul(oc[:, 0:128], lhsT=vs[:, vb, :], rhs=pt[:, kb * 128:(kb + 1) * 128],
                                                                                       start=(kb == 0), stop=(kb == nkb - 1)),
                             reads=[vs, pt], writes=[oc])
                    for kb in range(nkb):
                        S.op("pe", lambda e, kb=kb, oc=oc, pt=pt, nkb=nkb: e.matmul(oc[:, 128:256], lhsT=ones[:, :], rhs=pt[:, kb * 128:(kb + 1) * 128],
                                                                             start=(kb == 0), stop=(kb == nkb - 1)),
                             reads=[ones, pt], writes=[oc])
                    S.op("dve", lambda e, oc=oc, pbs=pbs: e.reciprocal(out=rden[pbs:pbs + 64, :], in_=oc[pbs:pbs + 64, 128:256]), reads=[oc], writes=[rden])
                    S.op("dve", lambda e, oc=oc, pbs=pbs: e.tensor_tensor(out=otmp[pbs:pbs + 64, :], in0=oc[pbs:pbs + 64, 0:128], in1=rden[pbs:pbs + 64, :], op=ALU.mult),
                         reads=[oc, rden], writes=[otmp])
                    S.op("dve", lambda e, pbs=pbs, q0=q0: e.tensor_tensor(out=ao[pbs:pbs + 64, q0:q0 + 128], in0=otmp[pbs:pbs + 64, :], in1=szs[pbs:pbs + 64, q0:q0 + 128], op=ALU.mult),
                         reads=[otmp, szs], writes=[ao])
            ncol = NT if with_ctx else NL
            S.dma("sp", a_d[hp * 128:(hp + 1) * 128, 0:ncol], ao[:, 0:ncol], reads=[ao], writes=DBrow("a", hp))

        if stop == "S4":
            raise _Stop()
        al = Alloc(AR)
        ubp = al.get(NT * 2)
        zbp = al.get(NT * 2)
        pob = al.get(NT * 2)
        U = al.get((NL + 16) * 4, F32)
        T1 = al.get((NL + 16) * 4, F32)
        T2 = al.get((NL + 16) * 4, F32)
        pwb = al.get(4 * 128 * 2, BF16, "p (g d) -> p g d", g=4)
        pab = al.get(NT * 2)
        S.dma("pool", pwb[:, :, :], pool_w[L], writes=[pwb])
        seqs = [(0, NL, 0)] + ([(NL, NCX, 1)] if with_ctx else [])
        for g in range(4):
            S.dma("sp", ubp[:, :], proj[2048 + g * 128:2048 + (g + 1) * 128, :], reads=DBrow("proj", 16 + g), writes=[ubp])
            S.dma("sp", zbp[:, :], proj[2560 + g * 128:2560 + (g + 1) * 128, :], reads=DBrow("proj", 20 + g), writes=[zbp])
            w = (2, 4, 8, 16)[g]
            for (c0, n, sq) in seqs:
                Ln = n + 16
                S.op("dve", lambda e, Ln=Ln: e.memset(U[:, 0:Ln], 0.0), writes=[U])
                S.op("dve", lambda e, c0=c0, n=n: e.tensor_copy(out=U[:, 8:8 + n], in_=ubp[:, c0:c0 + n]), reads=[ubp, U], writes=[U])
                S.op("dve", lambda e, Ln=Ln: e.tensor_tensor(out=T1[:, 1:Ln], in0=U[:, 0:Ln - 1], in1=U[:, 1:Ln], op=ALU.add), reads=[U], writes=[T1])
                cur, oth = T1, T2
                if w >= 4:
                    S.op("dve", lambda e, Ln=Ln: e.tensor_tensor(out=T2[:, 2:Ln - 1], in0=T1[:, 1:Ln - 2], in1=T1[:, 3:Ln], op=ALU.add), reads=[T1], writes=[T2])
                    cur, oth = T2, T1
                if w >= 8:
                    S.op("dve", lambda e, Ln=Ln: e.tensor_tensor(out=T1[:, 4:Ln - 3], in0=T2[:, 2:Ln - 5], in1=T2[:, 6:Ln - 1], op=ALU.add), reads=[T2], writes=[T1])
                    cur, oth = T1, T2
                if w >= 16:
                    S.op("dve", lambda e, Ln=Ln: e.tensor_tensor(out=T2[:, 8:Ln - 7], in0=T1[:, 4:Ln - 11], in1=T1[:, 12:Ln - 3], op=ALU.add), reads=[T1], writes=[T2])
                    cur, oth = T2, T1
                S.op("dve", lambda e, cur=cur, oth=oth, n=n, w=w: e.scalar_tensor_tensor(out=oth[:, 8:8 + n], in0=cur[:, 8:8 + n], scalar=1.0 / w, in1=U[:, 8:8 + n],
                                                                                 op0=ALU.mult, op1=ALU.subtract), reads=[cur, U], writes=[oth])
                for side, e0 in ((0, 8), (1, 8 + n - 8)):
                    S.op("dve", lambda e, cur=cur, sq=sq, g=g, side=side, e0=e0: e.tensor_tensor(out=rtmp[:, 0:8], in0=cur[:, e0:e0 + 8], in1=pedge[:, sq, g, side, :], op=ALU.mult),
                         reads=[cur, pedge], writes=[rtmp])
                    S.op("dve", lambda e, oth=oth, e0=e0: e.tensor_tensor(out=oth[:, e0:e0 + 8], in0=rtmp[:, 0:8], in1=U[:, e0:e0 + 8], op=ALU.subtract),
                         reads=[rtmp, U, oth], writes=[oth])
                S.op("act", lambda e, oth=oth, c0=c0, n=n: e.activation(out=pob[:, c0:c0 + n], in_=oth[:, 8:8 + n], func=AF.Copy), reads=[oth], writes=[pob])
                if dbg and L == 0 and sq == 0:
                    S.dma("sp", dbg_pob[g], pob[:, 0:NL], reads=[pob], writes=[DB("dbgpob", g, 0)])
                for cc in range(0, n, 512):
                    m = min(512, n - cc)
                    pb = next_bank()
                    S.op("pe", lambda e, pb=pb, g=g, c=c0 + cc, m=m: e.matmul(pb[:, 0:m], lhsT=pwb[:, g, :], rhs=pob[:, c:c + m], start=True, stop=True),
                         reads=[pwb, pob], writes=[pb])
                    S.op("dve", lambda e, pb=pb, g=g, c=c0 + cc, m=m: e.scalar_tensor_tensor(out=pab[:, c:c + m], in0=pb[:, 0:m], scalar=psc[:, g:g + 1], in1=zbp[:, c:c + m],
                                                                                    op0=ALU.mult, op1=ALU.mult), reads=[pb, psc, zbp], writes=[pab])
            ncol = NT if with_ctx else NL
            S.dma("sp", a_d[512 + g * 128:512 + (g + 1) * 128, 0:ncol], pab[:, 0:ncol], reads=[pab], writes=DBrow("a", 4 + g))

        if stop == "S5":
            raise _Stop()
        al = Alloc(AR)
        xb_ = al.get(NT * 2)
        bb_ = al.get(NT * 2)
        cb_ = al.get(NT * 2)
        zb = al.get(NT * 2)
        cab = al.get(NT * 2)
        Tt = al.get((NL + 2) * 4, F32)
        Dw = al.get(NL * 4, F32)
        for g in range(4):
            S.dma("sp", xb_[:, :], proj[3072 + g * 128:3072 + (g + 1) * 128, :], reads=DBrow("proj", 24 + g), writes=[xb_])
            S.dma("sp", bb_[:, :], proj[3584 + g * 128:3584 + (g + 1) * 128, :], reads=DBrow("proj", 28 + g), writes=[bb_])
            S.dma("sp", cb_[:, :], proj[4096 + g * 128:4096 + (g + 1) * 128, :], reads=DBrow("proj", 32 + g), writes=[cb_])
            S.dma("sp", zb[:, :], proj[4608 + g * 128:4608 + (g + 1) * 128, :], reads=DBrow("proj", 36 + g), writes=[zb])
            for (c0, n, sq) in seqs:
                S.op("dve", lambda e, n=n: e.memset(Tt[:, 0:n + 2], 0.0), writes=[Tt])
                S.op("dve", lambda e, c0=c0, n=n: e.tensor_tensor(out=Tt[:, 1:n + 1], in0=cb_[:, c0:c0 + n], in1=xb_[:, c0:c0 + n], op=ALU.mult),
                     reads=[cb_, xb_, Tt], writes=[Tt])
                S.op("dve", lambda e, n=n, g=g: e.tensor_scalar(out=Dw[:, 0:n], in0=Tt[:, 0:n], scalar1=cw[:, g, 0:1], scalar2=None, op0=ALU.mult), reads=[Tt, cw], writes=[Dw])
                for j in (1, 2):
                    S.op("dve", lambda e, n=n, g=g, j=j: e.scalar_tensor_tensor(out=Dw[:, 0:n], in0=Tt[:, j:j + n], scalar=cw[:, g, j:j + 1], in1=Dw[:, 0:n],
                                                                           op0=ALU.mult, op1=ALU.add), reads=[Tt, cw, Dw], writes=[Dw])
                S.op("dve", lambda e, c0=c0, n=n: e.tensor_tensor(out=Dw[:, 0:n], in0=Dw[:, 0:n], in1=bb_[:, c0:c0 + n], op=ALU.mult), reads=[Dw, bb_], writes=[Dw])
                S.op("dve", lambda e, c0=c0, n=n: e.tensor_tensor(out=cab[:, c0:c0 + n], in0=Dw[:, 0:n], in1=zb[:, c0:c0 + n], op=ALU.mult), reads=[Dw, zb], writes=[cab])
            ncol = NT if with_ctx else NL
            S.dma("sp", a_d[1024 + g * 128:1024 + (g + 1) * 128, 0:ncol], cab[:, 0:ncol], reads=[cab], writes=DBrow("a", 8 + g))

        if stop == "S6":
            raise _Stop()
        al = Alloc(AR)
        NH = NT // 2
        csT = al.get(NT * 4, F32)
        snT = al.get(NT * 4, F32)
        BTr = al.get(NT * 4, F32)
        BTi = al.get(NT * 4, F32)
        Yacc = al.get(NT * 4, F32)
        iot = al.get(NT * 4, F32)
        usb = al.get(NT * 2)
        sre = al.get(NT * 2)
        sim = al.get(NT * 2)
        sP3 = al.get(NT * 2)
        sP4 = al.get(NT * 2)
        S.dma("sp", iot[:, :], iota_nt, writes=[iot])
        S.dma("sp", s_are[:, :], ssm_are[L], writes=[s_are])
        S.dma("sp", s_aim[:, :], ssm_aim[L], writes=[s_aim])
        S.dma("sp", s_ldt[:, :], ssm_ldt[L], writes=[s_ldt])
        S.op("act", lambda e: e.activation(out=s_dt[:, :], in_=s_ldt[:, :], func=AF.Exp), reads=[s_ldt], writes=[s_dt])
        S.op("dve", lambda e: e.tensor_tensor(out=s_x1[:, :], in0=s_are[:, :], in1=s_dt[:, :], op=ALU.mult), reads=[s_are, s_dt], writes=[s_x1])
        S.op("dve", lambda e: e.tensor_tensor(out=s_th[:, :], in0=s_aim[:, :], in1=s_dt[:, :], op=ALU.mult), reads=[s_aim, s_dt], writes=[s_th])
        S.op("act", lambda e: e.activation(out=s_rho[:, :], in_=s_x1[:, :], func=AF.Exp), reads=[s_x1], writes=[s_rho])

        def rr(dst, src, tmp, tint, n, shift=0.0):
            S.op("dve", lambda e: e.tensor_scalar(out=tmp[:, 0:n], in0=src[:, 0:n], scalar1=shift, scalar2=1.0 / TWO_PI, op0=ALU.add, op1=ALU.mult),
                 reads=[src], writes=[tmp])
            S.op("dve", lambda e: e.tensor_copy(out=tint[:, 0:n], in_=tmp[:, 0:n]), reads=[tmp], writes=[tint])
            S.op("dve", lambda e: e.tensor_copy(out=tmp[:, 0:n], in_=tint[:, 0:n]), reads=[tint], writes=[tmp])
            S.op("dve", lambda e: e.scalar_tensor_tensor(out=tmp[:, 0:n], in0=tmp[:, 0:n], scalar=-TWO_PI, in1=src[:, 0:n], op0=ALU.mult, op1=ALU.add),
                 reads=[tmp, src], writes=[tmp])
            if shift != 0.0:
                S.op("dve", lambda e: e.tensor_scalar(out=tmp[:, 0:n], in0=tmp[:, 0:n], scalar1=shift, scalar2=None, op0=ALU.add), reads=[tmp], writes=[tmp])
            S.op("dve", lambda e: e.tensor_scalar(out=dst[:, 0:n], in0=tmp[:, 0:n], scalar1=PI, scalar2=-TWO_PI, op0=ALU.is_gt, op1=ALU.mult), reads=[tmp], writes=[dst])
            S.op("dve", lambda e: e.tensor_tensor(out=tmp[:, 0:n], in0=tmp[:, 0:n], in1=dst[:, 0:n], op=ALU.add), reads=[tmp, dst], writes=[tmp])
            S.op("dve", lambda e: e.tensor_scalar(out=dst[:, 0:n], in0=tmp[:, 0:n], scalar1=-PI, scalar2=TWO_PI, op0=ALU.is_lt, op1=ALU.mult), reads=[tmp], writes=[dst])
            S.op("dve", lambda e: e.tensor_tensor(out=dst[:, 0:n], in0=tmp[:, 0:n], in1=dst[:, 0:n], op=ALU.add), reads=[tmp, dst], writes=[dst])

        rr(s_thr, s_th, s_t1, s_ti, 32)
        S.op("dve", lambda e: e.tensor_scalar(out=s_thq[:, :], in0=s_thr[:, :], scalar1=1.0 / TWO_PI, scalar2=None, op0=ALU.mult), reads=[s_thr], writes=[s_thq])
        S.op("act", lambda e: e.activation(out=s_sn1[:, :], in_=s_thr[:, :], func=AF.Sin), reads=[s_thr], writes=[s_sn1])
        rr(s_t2, s_thr, s_t1, s_ti, 32, shift=PI / 2)
        S.op("act", lambda e: e.activation(out=s_cs1[:, :], in_=s_t2[:, :], func=AF.Sin), reads=[s_t2], writes=[s_cs1])
        S.op("dve", lambda e: e.tensor_tensor(out=s_nr[:, :], in0=s_rho[:, :], in1=s_cs1[:, :], op=ALU.mult), reads=[s_rho, s_cs1], writes=[s_nr])
        S.op("dve", lambda e: e.tensor_scalar(out=s_nr[:, :], in0=s_nr[:, :], scalar1=-1.0, scalar2=None, op0=ALU.add), reads=[s_nr], writes=[s_nr])
        S.op("dve", lambda e: e.tensor_tensor(out=s_ni[:, :], in0=s_rho[:, :], in1=s_sn1[:, :], op=ALU.mult), reads=[s_rho, s_sn1], writes=[s_ni])
        S.op("dve", lambda e: e.tensor_tensor(out=s_t1[:, :], in0=s_are[:, :], in1=s_are[:, :], op=ALU.mult), reads=[s_are], writes=[s_t1])
        S.op("dve", lambda e: e.tensor_tensor(out=s_t2[:, :], in0=s_aim[:, :], in1=s_aim[:, :], op=ALU.mult), reads=[s_aim], writes=[s_t2])
        S.op("dve", lambda e: e.tensor_tensor(out=s_t1[:, :], in0=s_t1[:, :], in1=s_t2[:, :], op=ALU.add), reads=[s_t1, s_t2], writes=[s_t1])
        S.op("dve", lambda e: e.reciprocal(out=s_den[:, :], in_=s_t1[:, :]), reads=[s_t1], writes=[s_den])
        S.op("dve", lambda e: e.tensor_tensor(out=s_t1[:, :], in0=s_nr[:, :], in1=s_are[:, :], op=ALU.mult), reads=[s_nr, s_are], writes=[s_t1])
        S.op("dve", lambda e: e.tensor_tensor(out=s_t2[:, :], in0=s_ni[:, :], in1=s_aim[:, :], op=ALU.mult), reads=[s_ni, s_aim], writes=[s_t2])
        S.op("dve", lambda e: e.tensor_tensor(out=s_t1[:, :], in0=s_t1[:, :], in1=s_t2[:, :], op=ALU.add), reads=[s_t1, s_t2], writes=[s_t1])
        S.op("dve", lambda e: e.tensor_tensor(out=s_fr[:, :], in0=s_t1[:, :], in1=s_den[:, :], op=ALU.mult), reads=[s_t1, s_den], writes=[s_fr])
        S.op("dve", lambda e: e.tensor_tensor(out=s_t1[:, :], in0=s_ni[:, :], in1=s_are[:, :], op=ALU.mult), reads=[s_ni, s_are], writes=[s_t1])
        S.op("dve", lambda e: e.tensor_tensor(out=s_t2[:, :], in0=s_nr[:, :], in1=s_aim[:, :], op=ALU.mult), reads=[s_nr, s_aim], writes=[s_t2])
        S.op("dve", lambda e: e.tensor_tensor(out=s_t1[:, :], in0=s_t1[:, :], in1=s_t2[:, :], op=ALU.subtract), reads=[s_t1, s_t2], writes=[s_t1])
        S.op("dve", lambda e: e.tensor_tensor(out=s_fi[:, :], in0=s_t1[:, :], in1=s_den[:, :], op=ALU.mult), reads=[s_t1, s_den], writes=[s_fi])

        for o in range(4):
            S.dma("sp", usb[:, 0:NCX], proj[5120 + o * 128:5120 + (o + 1) * 128, NL:NT], reads=[DB("proj", 40 + o, 8)], writes=[usb])
            S.dma("sp", usb[:, NCX:NT], proj[5120 + o * 128:5120 + (o + 1) * 128, 0:NL], reads=DBrow("proj", 40 + o), writes=[usb])
            first = True
            for d in range(2):
                for jj in range(4):
                    u = d * 16 + o * 4 + jj
                    S.dma("sp", sB[:, :, :], ssm_B[L][:, u], writes=[sB])
                    S.dma("sp", sC[:, :, :], ssm_C[L][:, u], writes=[sC])
                    S.op("dve", lambda e, u=u: e.tensor_scalar(out=sT[:, :], in0=sB[:, 1, :], scalar1=s_fi[:, u:u + 1], scalar2=None, op0=ALU.mult), reads=[sB, s_fi], writes=[sT])
                    S.op("dve", lambda e, u=u: e.scalar_tensor_tensor(out=sBb[:, 0, :], in0=sB[:, 0, :], scalar=s_fr[:, u:u + 1], in1=sT[:, :], op0=ALU.mult, op1=ALU.subtract),
                         reads=[sB, s_fr, sT], writes=[sBb])
                    S.op("dve", lambda e, u=u: e.tensor_scalar(out=sT[:, :], in0=sB[:, 0, :], scalar1=s_fi[:, u:u + 1], scalar2=None, op0=ALU.mult), reads=[sB, s_fi], writes=[sT])
                    S.op("dve", lambda e, u=u: e.scalar_tensor_tensor(out=sBb[:, 1, :], in0=sB[:, 1, :], scalar=s_fr[:, u:u + 1], in1=sT[:, :], op0=ALU.mult, op1=ALU.add),
                         reads=[sB, s_fr, sT], writes=[sBb])
                    S.op("act", lambda e: e.activation(out=sCb[:, 0, :], in_=sC[:, 0, :], func=AF.Copy), reads=[sC], writes=[sCb])
                    S.op("act", lambda e: e.activation(out=sCb[:, 1, :], in_=sC[:, 1, :], func=AF.Copy, scale=-1.0), reads=[sC], writes=[sCb])
                    S.op("act", lambda e: e.activation(out=sCb[:, 2, :], in_=sC[:, 0, :], func=AF.Copy, scale=-1.0), reads=[sC], writes=[sCb])
                    pbt = bankbf(0)
                    for ri in range(2):
                        S.op("pe", lambda e, ri=ri, pbt=pbt: e.transpose(out=pbt[:, ri * 128:(ri + 1) * 128], in_=sBb[:, ri, :], identity=ident[:, :]),
                             reads=[sBb, ident], writes=[pbt])
                    S.op("act", lambda e, pbt=pbt: e.activation(out=sWT[:, :, :], in_=pbt[:, 0:256].rearrange("p (a b) -> p a b", a=2), func=AF.Copy), reads=[pbt], writes=[sWT])
                    S.op("dve", lambda e, u=u: e.tensor_scalar(out=BTr[:, :], in0=iot[:, :], scalar1=s_thq[:, u:u + 1], scalar2=MAGIC, op0=ALU.mult, op1=ALU.add),
                         reads=[iot, s_thq], writes=[BTr])
                    S.op("dve", lambda e: e.tensor_scalar(out=BTr[:, :], in0=BTr[:, :], scalar1=MAGIC, scalar2=-TWO_PI, op0=ALU.subtract, op1=ALU.mult), reads=[BTr], writes=[BTr])
                    S.op("dve", lambda e, u=u: e.scalar_tensor_tensor(out=snT[:, :], in0=iot[:, :], scalar=s_thr[:, u:u + 1], in1=BTr[:, :], op0=ALU.mult, op1=ALU.add),
                         reads=[iot, s_thr, BTr], writes=[snT])
                    S.op("dve", lambda e: e.tensor_scalar(out=snT[:, :], in0=snT[:, :], scalar1=-PI, scalar2=PI, op0=ALU.max, op1=ALU.min), reads=[snT], writes=[snT])
                    S.op("act", lambda e: e.activation(out=csT[:, :], in_=snT[:, :], func=AF.Abs), reads=[snT], writes=[csT])
                    S.op("act", lambda e: e.activation(out=csT[:, :], in_=csT[:, :], func=AF.Sin, bias=halfpi[:, 0:1], scale=-1.0), reads=[csT, halfpi], writes=[csT])
                    S.op("act", lambda e: e.activation(out=snT[:, :], in_=snT[:, :], func=AF.Sin), reads=[snT], writes=[snT])
                    def rsl(buf, c0, n, d=d):
                        if d == 0:
                            return buf[:, c0:c0 + n]
                        st_ = NT - 1 - c0
                        sp_ = st_ - n
                        return buf[:, st_::-1] if sp_ < 0 else buf[:, st_:sp_:-1]
                    for c0 in range(0, NT, 512):
                        n = min(512, NT - c0)
                        p0 = bank(1)
                        pq = bank(2)
                        uc0 = c0 if d == 0 else (NCX + c0 if c0 < NL else 0)
                        S.op("pe", lambda e, uc0=uc0, n=n, p0=p0: e.matmul(p0[:, 0:n], lhsT=sWT[:, 0, :], rhs=usb[:, uc0:uc0 + n], start=True, stop=True), reads=[sWT, usb], writes=[p0])
                        S.op("pe", lambda e, uc0=uc0, n=n, pq=pq: e.matmul(pq[:, 0:n], lhsT=sWT[:, 1, :], rhs=usb[:, uc0:uc0 + n], start=True, stop=True), reads=[sWT, usb], writes=[pq])
                        ta, tb, tc, td = stgf[0], stgf[1], stgf[2], stgf[3]
                        S.op("dve", lambda e, c0=c0, n=n, p0=p0, v=rsl(csT, c0, min(512, NT - c0)): e.tensor_tensor(out=ta[:, 0:n], in0=p0[:, 0:n], in1=v, op=ALU.mult), reads=[p0, csT], writes=[ta])
                        S.op("dve", lambda e, c0=c0, n=n, pq=pq, v=rsl(snT, c0, min(512, NT - c0)): e.tensor_tensor(out=tb[:, 0:n], in0=pq[:, 0:n], in1=v, op=ALU.mult), reads=[pq, snT], writes=[tb])
                        S.op("dve", lambda e, c0=c0, n=n: e.tensor_tensor(out=BTr[:, c0:c0 + n], in0=ta[:, 0:n], in1=tb[:, 0:n], op=ALU.add), reads=[ta, tb], writes=[sub(BTr, c0, n, 4)])
                        S.op("dve", lambda e, c0=c0, n=n, pq=pq, v=rsl(csT, c0, min(512, NT - c0)): e.tensor_tensor(out=tc[:, 0:n], in0=pq[:, 0:n], in1=v, op=ALU.mult), reads=[pq, csT], writes=[tc])
                        S.op("dve", lambda e, c0=c0, n=n, p0=p0, v=rsl(snT, c0, min(512, NT - c0)): e.tensor_tensor(out=td[:, 0:n], in0=p0[:, 0:n], in1=v, op=ALU.mult), reads=[p0, snT], writes=[td])
                        S.op("dve", lambda e, c0=c0, n=n: e.tensor_tensor(out=BTi[:, c0:c0 + n], in0=tc[:, 0:n], in1=td[:, 0:n], op=ALU.subtract), reads=[tc, td], writes=[sub(BTi, c0, n, 4)])
                    for BT in (BTr, BTi):
                        bv = BT[:, :] if d == 0 else BT[:, ::-1]
                        S.op("dve", lambda e, bv=bv, u=u: e.tensor_tensor_scan(out=bv, data0=s_rho[:, u:u + 1].to_broadcast([128, NT]), data1=bv, initial=0.0,
                                                                          op0=ALU.mult, op1=ALU.add), reads=[BT, s_rho], writes=[BT])
                    cvf = rsl(csT, 0, NT)
                    svf = rsl(snT, 0, NT)
                    S.op("dve", lambda e, cvf=cvf: e.tensor_tensor(out=sre[:, :], in0=BTr[:, :], in1=cvf, op=ALU.mult), reads=[BTr, csT], writes=[sre])
                    S.op("dve", lambda e, svf=svf: e.tensor_tensor(out=sim[:, :], in0=BTi[:, :], in1=svf, op=ALU.mult), reads=[BTi, snT], writes=[sim])
                    S.op("dve", lambda e, cvf=cvf: e.tensor_tensor(out=sP3[:, :], in0=BTi[:, :], in1=cvf, op=ALU.mult), reads=[BTi, csT], writes=[sP3])
                    S.op("dve", lambda e, svf=svf: e.tensor_tensor(out=sP4[:, :], in0=BTr[:, :], in1=svf, op=ALU.mult), reads=[BTr, snT], writes=[sP4])
                    for c0 in range(0, NT, 512):
                        n = min(512, NT - c0)
                        p0 = bank(3)
                        yc0 = c0 if d == 0 else (NCX + c0 if c0 < NL else 0)
                        S.op("pe", lambda e, c0=c0, n=n, p0=p0: e.matmul(p0[:, 0:n], lhsT=sCb[:, 0, :], rhs=sre[:, c0:c0 + n], start=True, stop=False), reads=[sCb, sre], writes=[p0])
                        S.op("pe", lambda e, c0=c0, n=n, p0=p0: e.matmul(p0[:, 0:n], lhsT=sCb[:, 2, :], rhs=sim[:, c0:c0 + n], start=False, stop=False), reads=[sCb, sim], writes=[p0])
                        S.op("pe", lambda e, c0=c0, n=n, p0=p0: e.matmul(p0[:, 0:n], lhsT=sCb[:, 1, :], rhs=sP3[:, c0:c0 + n], start=False, stop=False), reads=[sCb, sP3], writes=[p0])
                        S.op("pe", lambda e, c0=c0, n=n, p0=p0: e.matmul(p0[:, 0:n], lhsT=sCb[:, 1, :], rhs=sP4[:, c0:c0 + n], start=False, stop=True), reads=[sCb, sP4], writes=[p0])
                        if first:
                            S.op("dve", lambda e, c0=yc0, n=n, p0=p0: e.tensor_copy(out=Yacc[:, c0:c0 + n], in_=p0[:, 0:n]), reads=[p0], writes=[Yacc])
                        else:
                            S.op("dve", lambda e, c0=yc0, n=n, p0=p0: e.tensor_tensor(out=Yacc[:, c0:c0 + n], in0=p0[:, 0:n], in1=Yacc[:, c0:c0 + n], op=ALU.add), reads=[p0, Yacc], writes=[Yacc])
                    first = False
            S.op("dve", lambda e, o=o: e.scalar_tensor_tensor(out=Yacc[:, :], in0=usb[:, :], scalar=sdk[:, o:o + 1], in1=Yacc[:, :], op0=ALU.mult, op1=ALU.add),
                 reads=[usb, sdk, Yacc], writes=[Yacc])
            S.op("dve", lambda e: e.tensor_tensor(out=BTr[:, :], in0=Yacc[:, :], in1=Yacc[:, :], op=ALU.mult), reads=[Yacc], writes=[BTr])
            S.op("dve", lambda e: e.tensor_scalar(out=BTr[:, :], in0=BTr[:, :], scalar1=0.044715, scalar2=1.0, op0=ALU.mult, op1=ALU.add), reads=[BTr], writes=[BTr])
            S.op("dve", lambda e: e.tensor_tensor(out=BTr[:, :], in0=BTr[:, :], in1=Yacc[:, :], op=ALU.mult), reads=[BTr, Yacc], writes=[BTr])
            S.op("act", lambda e: e.activation(out=BTi[:, :], in_=BTr[:, :], func=AF.Sigmoid, scale=float(2.0 * np.sqrt(2.0 / np.pi))), reads=[BTr], writes=[BTi])
            S.op("dve", lambda e: e.tensor_tensor(out=sre[:, :], in0=BTi[:, :], in1=Yacc[:, :], op=ALU.mult), reads=[BTi, Yacc], writes=[sre])
            S.dma("sp", yss[o * 128:(o + 1) * 128, NL:NT], sre[:, 0:NCX], reads=[sre], writes=[DB("yss", o, 1)])
            S.dma("sp", yss[o * 128:(o + 1) * 128, 0:NL], sre[:, NCX:NT], reads=[sre], writes=[DB("yss", o, 0)])
        al = Alloc(AR)
        gw = al.get(4 * 1024 * 2, BF16, "p (k n) -> p k n", k=4)
        YT = [al.get(4 * 512 * 2, BF16, "p (k n) -> p k n", k=4) for _ in range(2)]
        ZT = [al.get(4 * 512 * 2, BF16, "p (k n) -> p k n", k=4) for _ in range(2)]
        S.dma("pool", gw[:, :, :], glu_w[L].rearrange("(k p) n -> p k n", p=128), writes=[gw])
        tiles7 = TILES if with_ctx else TILES[:8]
        for ti, (c0, n) in enumerate(tiles7):
            yt = YT[ti % 2]
            zt = ZT[ti % 2]
            S.dma("sp", yt[:, :, 0:n], yss.rearrange("(k p) n -> p k n", p=128)[:, :, c0:c0 + n], reads=[DB("yss", oo, hh) for oo in range(4) for hh in range(2)], writes=[yt])
            S.dma("sp", zt[:, :, 0:n], proj[5632:6144, :].rearrange("(k p) n -> p k n", p=128)[:, :, c0:c0 + n], reads=[DB("proj", 44 + oo, ti) for oo in range(4)], writes=[zt])
            for ob in range(4):
                pa = next_bank()
                for k in range(4):
                    S.op("pe", lambda e, k=k, ob=ob, pa=pa, yt=yt, n=n: e.matmul(pa[:, 0:n], lhsT=gw[:, k, ob * 128:(ob + 1) * 128], rhs=yt[:, k, 0:n], start=(k == 0), stop=(k == 3)),
                         reads=[gw, yt], writes=[pa])
                pg = next_bank()
                for k in range(4):
                    S.op("pe", lambda e, k=k, ob=ob, pg=pg, yt=yt, n=n: e.matmul(pg[:, 0:n], lhsT=gw[:, k, 512 + ob * 128:512 + (ob + 1) * 128], rhs=yt[:, k, 0:n], start=(k == 0), stop=(k == 3)),
                         reads=[gw, yt], writes=[pg])
                S.op("act", lambda e, pg=pg, n=n: e.activation(out=stgf[2][:, 0:n], in_=pg[:, 0:n], func=AF.Sigmoid), reads=[pg], writes=[stgf[2]])
                S.op("dve", lambda e, pa=pa, n=n: e.tensor_tensor(out=stgf[2][:, 0:n], in0=pa[:, 0:n], in1=stgf[2][:, 0:n], op=ALU.mult), reads=[pa, stgf[2]], writes=[stgf[2]])
                st = next_stg()
                S.op("dve", lambda e, st=st, zt=zt, ob=ob, n=n: e.tensor_tensor(out=st[:, 0:n], in0=stgf[2][:, 0:n], in1=zt[:, ob, 0:n], op=ALU.mult), reads=[stgf[2], zt], writes=[st])
                S.dma("sp", a_d[1536 + ob * 128:1536 + (ob + 1) * 128, c0:c0 + n], st[:, 0:n], reads=[st], writes=[DB("a", 12 + ob, ti)])

        if stop == "S7":
            raise _Stop()
        al = Alloc(AR)
        wbr = al.get(KT * D * 2, BF16, "p (k n) -> p k n", k=KT)
        AT = [al.get(KT * 512 * 2, BF16, "p (k n) -> p k n", k=KT) for _ in range(2)]
        GT = [al.get(4 * 512 * 2, BF16, "p (i n) -> p i n", i=4) for _ in range(2)]
        MT = al.get(KT * 512 * 2, BF16, "p (k n) -> p k n", k=KT)
        for half in range(2):
            src = w_br[L].rearrange("(k p) n -> p k n", p=128)
            for kk in range(0, KT, 4):
                S.dma("pool", wbr[:, kk:kk + 4, half * 1024:(half + 1) * 1024], src[:, kk:kk + 4, half * 1024:(half + 1) * 1024], writes=[wbr])
        tiles8 = TILES if with_ctx else TILES[:8]
        gcnt = 0
        def load_at(ti_):
            c0_, n_ = tiles8[ti_]
            S.dma("sp", AT[ti_ % 2][:, :, 0:n_], a_d.rearrange("(k p) n -> p k n", p=128)[:, :, c0_:c0_ + n_],
                  reads=[DB("a", rb, t2) for rb in range(16) for t2 in range(9)], writes=[AT[ti_ % 2]])

        def load_gt(cnt):
            ti_, j_ = divmod(cnt, 16)
            c0_, n_ = tiles8[ti_]
            S.dma("sp", GT[cnt % 2][:, :, 0:n_], proj[6144:14336, :].rearrange("(i j p) n -> j p i n", i=4, j=16)[j_][:, :, c0_:c0_ + n_],
                  reads=[DB("proj", 48 + i * 16 + j_, ti_) for i in range(4)], writes=[GT[cnt % 2]])

        load_at(0)
        load_gt(0)
        for ti, (c0, n) in enumerate(tiles8):
            at = AT[ti % 2]
            if ti + 1 < len(tiles8):
                load_at(ti + 1)
            for j in range(16):
                gt = GT[gcnt % 2]
                gcnt += 1
                if gcnt < 16 * len(tiles8):
                    load_gt(gcnt)
                acc = stgf[0]
                tmpf = stgf[1]
                for i in range(4):
                    pb = next_bank()
                    for kk in range(4):
                        S.op("pe", lambda e, pb=pb, i=i, kk=kk, j=j, at=at, n=n: e.matmul(pb[:, 0:n], lhsT=wbr[:, i * 4 + kk, j * 128:(j + 1) * 128],
                                                                                  rhs=at[:, i * 4 + kk, 0:n], start=(kk == 0), stop=(kk == 3)),
                             reads=[wbr, at], writes=[pb])
                    if i == 0:
                        S.op("dve", lambda e, pb=pb, gt=gt, n=n: e.tensor_tensor(out=acc[:, 0:n], in0=pb[:, 0:n], in1=gt[:, 0, 0:n], op=ALU.mult),
                             reads=[pb, gt], writes=[acc])
                    else:
                        S.op("dve", lambda e, pb=pb, gt=gt, n=n, i=i: e.tensor_tensor(out=tmpf[:, 0:n], in0=pb[:, 0:n], in1=gt[:, i, 0:n], op=ALU.mult),
                             reads=[pb, gt], writes=[tmpf])
                        if i < 3:
                            S.op("dve", lambda e, n=n: e.tensor_tensor(out=acc[:, 0:n], in0=acc[:, 0:n], in1=tmpf[:, 0:n], op=ALU.add), reads=[acc, tmpf], writes=[acc])
                        else:
                            S.op("dve", lambda e, n=n, j=j: e.tensor_tensor(out=MT[:, j, 0:n], in0=acc[:, 0:n], in1=tmpf[:, 0:n], op=ALU.add), reads=[acc, tmpf], writes=[MT])
            S.dma("sp", mg.rearrange("(k p) n -> p k n", p=128)[:, :, c0:c0 + n], MT[:, :, 0:n], reads=[MT], writes=[DB("mg", 0, ti)])

        if stop == "S8":
            raise _Stop()
        al = Alloc(AR)
        wo = al.get(KT * D * 2, BF16, "p (k n) -> p k n", k=KT)
        MT2 = [al.get(KT * 256 * 2, BF16, "p (k n) -> p k n", k=KT) for _ in range(2)]
        Y = al.get(KT * 256 * 4, F32, "p (k n) -> p k n", k=KT)
        SQ = al.get(KT * 256 * 2, BF16, "p (k n) -> p k n", k=KT)
        XT9 = [al.get(KT * 256 * 4, F32, "p (k n) -> p k n", k=KT) for _ in range(2)]
        for half in range(2):
            src = w_o[L].rearrange("(k p) n -> p k n", p=128)
            for kk in range(0, KT, 4):
                S.dma("pool", wo[:, kk:kk + 4, half * 1024:(half + 1) * 1024], src[:, kk:kk + 4, half * 1024:(half + 1) * 1024], writes=[wo])
        ntok = NT if with_ctx else NL

        def load9(qi_):
            c0_ = qi_ * 256
            ti_ = min(c0_ // 512, 8)
            S.dma("sp", MT2[qi_ % 2][:, :, :], mg.rearrange("(k p) n -> p k n", p=128)[:, :, c0_:c0_ + 256], reads=[DB("mg", 0, ti_)], writes=[MT2[qi_ % 2]])
            S.dma("sp", XT9[qi_ % 2][:, :, :], xsrc.rearrange("(k p) n -> p k n", p=128)[:, :, c0_:c0_ + 256], reads=[DB(xsn, 0, ti_)], writes=[XT9[qi_ % 2]])

        for qi, c0 in enumerate(range(0, ntok, 256)):
            n = 256
            ti = min(c0 // 512, 8)
            s = 0 if c0 < NL else 1
            mt = MT2[qi % 2]
            XT = XT9[qi % 2]
            if qi == 0:
                load9(0)
            if c0 + 256 < ntok:
                load9(qi + 1)
            for j in range(16):
                pb = next_bank()
                for k in range(KT):
                    S.op("pe", lambda e, pb=pb, k=k, j=j, mt=mt: e.matmul(pb[:, 0:256], lhsT=wo[:, k, j * 128:(j + 1) * 128], rhs=mt[:, k, :],
                                                                     start=(k == 0), stop=(k == KT - 1)), reads=[wo, mt], writes=[pb])
                S.op("dve", lambda e, pb=pb, j=j: e.tensor_copy(out=Y[:, j, :], in_=pb[:, 0:256]), reads=[pb], writes=[Y])
                S.op("act", lambda e, j=j: e.activation(out=SQ[:, j, :], in_=Y[:, j, :], func=AF.Square), reads=[Y], writes=[SQ])
            if stop == "S9a":
                raise _Stop()
            pb = next_bank()
            for k in range(KT):
                S.op("pe", lambda e, k=k, pb=pb: e.matmul(pb[:, 0:256], lhsT=ones[:, :], rhs=SQ[:, k, :], start=(k == 0), stop=(k == KT - 1)),
                     reads=[ones, SQ], writes=[pb])
            S.op("act", lambda e, pb=pb: e.activation(out=rtmp[:, 0:256], in_=pb[:, 0:256], func=AF.Sqrt, bias=epsb[:, 0:1], scale=1.0 / D),
                 reads=[pb, epsb], writes=[rtmp])
            S.op("dve", lambda e: e.reciprocal(out=rstd[:, 0:256], in_=rtmp[:, 0:256]), reads=[rtmp], writes=[rstd])
            S.op("dve", lambda e: e.tensor_tensor(out=Y[:, :, :], in0=Y[:, :, :], in1=rstd[:, 0:256].unsqueeze(1).to_broadcast([128, KT, 256]), op=ALU.mult),
                 reads=[Y, rstd], writes=[Y])
            if stop == "S9b":
                raise _Stop()
            for k in range(KT):
                S.op("dve", lambda e, k=k, s=s, XT=XT: e.scalar_tensor_tensor(out=XT[:, k, :], in0=Y[:, k, :], scalar=Gmod[:, k, s:s + 1], in1=XT[:, k, :],
                                                                     op0=ALU.mult, op1=ALU.add), reads=[Y, Gmod, XT], writes=[XT])
            if last:
                S.dma("sp", outT.rearrange("(k p) n -> p k n", p=128)[:, :, c0:c0 + n], XT[:, :, :], reads=[XT], writes=[DB("out", 0, qi)])
            else:
                S.dma("sp", xs.rearrange("(k p) n -> p k n", p=128)[:, :, c0:c0 + n], XT[:, :, :], reads=[XT], writes=[DB("xs", 0, ti)])
            if stop == "S9c":
                raise _Stop()
      except _Stop:
        break

    finals = [o for o in S.qops["sp"] if o.dma][-40:]
    S.emit(final_ops=finals)
    return nc


def _fm(v, nb):
    return np.ascontiguousarray(np.swapaxes(v.reshape(v.shape[:-1] + (nb, 128)), -1, -2))


def _na_bias(rpb):
    pats = [(0, 0), (2, 0), (8, 4), (60, 54), (62, 54)]
    drow = np.zeros((5, 128, 640), np.int64)
    dcol = np.zeros((5, 128, 640), np.int64)
    mask = np.zeros((5, 128, 640), bool)
    qc = np.arange(64)
    kc = np.arange(64)
    cs = np.clip(qc - 8, 0, 48)
    inwin = (kc[None, :] >= cs[:, None]) & (kc[None, :] < cs[:, None] + 16)
    dc = np.clip(kc[None, :] - qc[:, None] + 15, 0, 30)
    for pi, (r0, lo) in enumerate(pats):
        for dr in range(2):
            r = r0 + dr
            bs = min(max(r - 4, 0), 56)
            for ko in range(10):
                kr = lo + ko
                inb = bs <= kr < bs + 8
                drw = min(max(kr - r + 7, 0), 14)
                drow[pi, dr * 64:(dr + 1) * 64, ko * 64:(ko + 1) * 64] = drw
                dcol[pi, dr * 64:(dr + 1) * 64, ko * 64:(ko + 1) * 64] = dc
                mask[pi, dr * 64:(dr + 1) * 64, ko * 64:(ko + 1) * 64] = inwin & inb
    g = rpb[:, :, drow, dcol]
    g = np.where(mask[None, None], g, np.float32(-30000.0)).astype(np.float32)
    return np.ascontiguousarray(g.transpose(0, 1, 3, 2, 4))


def _pool_edge():
    out = np.zeros((128, 2, 4, 2, 8), np.float32)
    for si, n in enumerate((NL, NCX)):
        for g, w in enumerate((2, 4, 8, 16)):
            t = np.arange(n)
            lo = np.clip(t - w // 2, 0, n)
            hi = np.clip(t + w - w // 2, 0, n)
            inv = (1.0 / (hi - lo)).astype(np.float32)
            out[:, si, g, 0, :] = inv[None, 0:8]
            out[:, si, g, 1, :] = inv[None, n - 8:n]
    return out


_NC_CACHE = {}


def _prep(x, c, ctx, c_ctx, w_mod, b_mod, g_pre, g_post, w_in, b_gate, na_rpb, pool_w,
           pool_scale, conv_w, ssm_a_re, ssm_a_im, ssm_log_dt, ssm_b_re, ssm_b_im,
           ssm_c_re, ssm_c_im, ssm_d, glu_w, w_br, w_o):
    f = np.float32
    x = np.asarray(x, f); ctx = np.asarray(ctx, f); c = np.asarray(c, f); c_ctx = np.asarray(c_ctx, f)
    shared = {
        "w_mod": np.ascontiguousarray(w_mod, f),
        "b_mod": _fm(np.asarray(b_mod, f), 48),
        "g_pre": _fm(np.asarray(g_pre, f), 16),
        "g_post": _fm(np.asarray(g_post, f), 16),
        "w_in": np.ascontiguousarray(w_in, f),
        "b_gate": _fm(np.asarray(b_gate, f), 64),
        "na_bias": _na_bias(np.asarray(na_rpb, f)),
        "pool_w": np.ascontiguousarray(np.asarray(pool_w, f).transpose(0, 2, 1, 3)),
        "pool_sc": _fm(np.asarray(pool_scale, f), 4),
        "pool_edge": _pool_edge(),
        "conv_w": np.ascontiguousarray(np.asarray(conv_w, f).reshape(DEPTH, 3, 4, 128).transpose(0, 3, 2, 1)),
        "ssm_d": _fm(np.asarray(ssm_d, f), 4),
        "glu_w": np.ascontiguousarray(glu_w, f),
        "w_br": np.ascontiguousarray(w_br, f),
        "w_o": np.ascontiguousarray(w_o, f),
        "ident": np.eye(128, dtype=f),
        "iota17": np.tile(np.arange(17, dtype=f)[None], (128, 1)),
        "iota_nt": np.tile(np.arange(NT, dtype=f)[None], (128, 1)),
    }

    def upl(a):
        a = np.asarray(a, f).reshape(DEPTH, 2, 16, 2, 64)
        return np.ascontiguousarray(a.transpose(0, 3, 4, 1, 2).reshape(DEPTH, 128, 32))

    shared["ssm_are"] = upl(ssm_a_re)
    shared["ssm_aim"] = upl(ssm_a_im)
    shared["ssm_ldt"] = upl(np.broadcast_to(np.asarray(ssm_log_dt, f)[..., None], (DEPTH, 2, 32, 64)))
    Bp = np.zeros((DEPTH, 128, 32, 2, 128), f)
    Cp = np.zeros((DEPTH, 128, 32, 2, 128), f)
    bre = np.asarray(ssm_b_re, f); bim = np.asarray(ssm_b_im, f)
    cre = np.asarray(ssm_c_re, f); cim = np.asarray(ssm_c_im, f)
    for d in range(2):
        for j in range(16):
            for gl in range(2):
                g = 2 * j + gl
                u = d * 16 + j
                ch0 = (j % 4) * 32 + gl * 16
                Bp[:, gl * 64:(gl + 1) * 64, u, 0, ch0:ch0 + 16] = bre[:, d, g]
                Bp[:, gl * 64:(gl + 1) * 64, u, 1, ch0:ch0 + 16] = bim[:, d, g]
                Cp[:, gl * 64:(gl + 1) * 64, u, 0, ch0:ch0 + 16] = cre[:, d, g].transpose(0, 2, 1)
                Cp[:, gl * 64:(gl + 1) * 64, u, 1, ch0:ch0 + 16] = cim[:, d, g].transpose(0, 2, 1)
    shared["ssm_B"] = Bp
    shared["ssm_C"] = Cp
    in_maps = []
    for core in range(8):
        b = core % 4
        m = dict(shared)
        m["x0"] = np.ascontiguousarray(np.concatenate([x[b].T, ctx[b].T], axis=1))
        cv = np.stack([c[b].reshape(16, 128).T, c_ctx.reshape(16, 128).T], axis=-1)
        m["cvec"] = np.ascontiguousarray(cv, f)
        in_maps.append(m)
    return in_maps


def kernel(x, c, ctx, c_ctx, w_mod, b_mod, g_pre, g_post, w_in, b_gate, na_rpb, pool_w,
           pool_scale, conv_w, ssm_a_re, ssm_a_im, ssm_log_dt, ssm_b_re, ssm_b_im,
           ssm_c_re, ssm_c_im, ssm_d, glu_w, w_br, w_o):
    in_maps = _prep(x, c, ctx, c_ctx, w_mod, b_mod, g_pre, g_post, w_in, b_gate, na_rpb, pool_w,
                    pool_scale, conv_w, ssm_a_re, ssm_a_im, ssm_log_dt, ssm_b_re, ssm_b_im,
                    ssm_c_re, ssm_c_im, ssm_d, glu_w, w_br, w_o)
    if "nc" not in _NC_CACHE:
        _NC_CACHE["nc"] = build_nc()
    res = run_bass_kernel_spmd(_NC_CACHE["nc"], in_maps, core_ids=list(range(8)))
    out = np.stack([np.ascontiguousarray(res.results[b]["outT"].T) for b in range(4)], axis=0)
    return out.astype(np.float32)
```

```python
import numpy as np
import concourse.bass as bass
import concourse.mybir as mybir
from concourse.bass_utils import run_bass_kernel_spmd

F32 = mybir.dt.float32
BF16 = mybir.dt.bfloat16
I32 = mybir.dt.int32
ALU = mybir.AluOpType
AF = mybir.ActivationFunctionType

DEPTH = 4
D = 2048
NL = 4096
NCX = 256
NT = NL + NCX
KT = 16
TILES = [(i * 512, 512) for i in range(8)] + [(4096, 256)]
TWO_PI = float(2 * np.pi)
PI = float(np.pi)
MAGIC = 12582912.0

COMPUTE_Q = ("pe", "act", "dve", "pool")
EPOCH = 20000
NDMA_SEM = 12


class Buf:
    __slots__ = ("name", "last_w", "readers", "t", "bufs")

    def __init__(self, name, t=None):
        self.name = name
        self.last_w = None
        self.readers = []
        self.t = t
        self.bufs = [self]

    def __getitem__(self, k):
        return self.t[k]


class View:
    __slots__ = ("t", "bufs")

    def __init__(self, t, bufs):
        self.t = t
        self.bufs = bufs

    def __getitem__(self, k):
        return self.t[k]


class Op:
    __slots__ = ("q", "fn", "deps", "dma", "sig", "sem", "val", "idx", "prev_same_sem", "seqno")

    def __init__(self, q, fn, dma):
        self.q = q
        self.fn = fn
        self.deps = []
        self.dma = dma
        self.sig = dma
        self.sem = None
        self.val = None
        self.idx = -1
        self.prev_same_sem = None


def _expand(lst):
    out = []
    for b in lst:
        out.extend(b.bufs)
    return out


class Sched:
    def __init__(self, nc):
        self.nc = nc
        self.qops = {q: [] for q in ("pe", "act", "dve", "pool", "sp")}

    def sb(self, name, shape, dtype):
        return Buf(name, self.nc.alloc_sbuf_tensor(name, list(shape), dtype))

    def _add(self, q, fn, reads, writes, dma):
        op = Op(q, fn, dma)
        self.seq = getattr(self, "seq", 0) + 1
        op.idx = self.seq
        reads = _expand(reads)
        writes = _expand(writes)
        deps = []
        for b in reads:
            if b.last_w is not None:
                deps.append((b.last_w, "raw"))
        for b in writes:
            if b.last_w is not None:
                deps.append((b.last_w, "waw"))
            for r in b.readers:
                deps.append((r, "war"))
        seen = set()
        best = {}
        for d, kind in deps:
            if d is op or id(d) in seen:
                continue
            seen.add(id(d))
            if d.q == q and not d.dma and not dma:
                if q == "pe" or kind == "war":
                    continue
            if d.dma:
                op.deps.append(d)
                d.sig = True
            else:
                cur = best.get(d.q)
                if cur is None or d.seqno > cur.seqno:
                    best[d.q] = d
        for d in best.values():
            op.deps.append(d)
            d.sig = True
        op.seqno = self.seq
        for b in reads:
            if not dma:
                b.readers = [r for r in b.readers if r.dma or r.q != q]
            b.readers.append(op)
        for b in writes:
            b.last_w = op
            b.readers = []
        self.qops[q].append(op)
        return op

    def op(self, q, fn, reads=(), writes=()):
        return self._add(q, fn, reads, writes, False)

    def dma(self, q, out, in_, reads=(), writes=()):
        return self._add(q, lambda e: e.dma_start(out=out, in_=in_), reads, writes, True)

    def emit(self, final_ops=()):
        nc = self.nc
        csem = {}
        for q in COMPUTE_Q:
            n = sum(1 for o in self.qops[q] if o.sig and not o.dma)
            ne = max(1, (n + EPOCH - 1) // EPOCH)
            csem[q] = [nc.alloc_semaphore(name=f"s_{q}{i}") for i in range(ne)]
        dsem = {}
        for q in ("sp", "act", "pool"):
            if any(o.dma for o in self.qops[q]):
                dsem[q] = [nc.alloc_semaphore(name=f"d_{q}{i}") for i in range(NDMA_SEM)]
        for q, lst in self.qops.items():
            cnt = 0
            dcnt = 0
            last_on_sem = {}
            for o in lst:
                if o.dma:
                    k = dcnt % NDMA_SEM
                    o.sem = dsem[q][k]
                    o.val = 16 * (dcnt // NDMA_SEM + 1)
                    o.prev_same_sem = last_on_sem.get(k)
                    last_on_sem[k] = o
                    dcnt += 1
                elif o.sig:
                    o.sem = csem[q][cnt // EPOCH]
                    o.val = cnt % EPOCH + 1
                    o.idx = cnt
                    cnt += 1
        engs = {"pe": "tensor", "act": "scalar", "dve": "vector", "pool": "gpsimd", "sp": "sync"}
        with nc.Block() as block:
            for q, lst in self.qops.items():
                if not lst:
                    continue

                def body(e, q=q, lst=lst):
                    waited = {}
                    dwaited = set()

                    def wait_for(d):
                        if d.dma:
                            if id(d) in dwaited:
                                return
                            dwaited.add(id(d))
                            e.wait_ge(d.sem, d.val)
                        else:
                            if waited.get(d.q, -1) >= d.idx:
                                return
                            waited[d.q] = d.idx
                            e.wait_ge(d.sem, d.val)

                    for o in lst:
                        for d in o.deps:
                            wait_for(d)
                        if o.dma and o.prev_same_sem is not None:
                            wait_for(o.prev_same_sem)
                        ins = o.fn(e)
                        if o.dma:
                            ins.then_inc(o.sem, 16)
                        elif o.sig:
                            ins.then_inc(o.sem, 1)
                    if q == "sp":
                        for d in final_ops:
                            wait_for(d)

                getattr(block, engs[q])(body)


CH = 512
NCHUNK = 181


class Arena:
    def __init__(self, S):
        self.t = S.nc.alloc_sbuf_tensor("arena", [128, CH * NCHUNK], BF16)
        self.chunks = [Buf(f"ch{i}") for i in range(NCHUNK)]

    def view(self, off_b, nbytes, dtype=BF16, pat=None, **kw):
        assert off_b % 4 == 0 and off_b + nbytes <= CH * NCHUNK * 2, (off_b, nbytes)
        e0 = off_b // 2
        e1 = (off_b + nbytes) // 2
        ap = self.t[:, e0:e1]
        if dtype != BF16:
            ap = ap.bitcast(dtype)
        if pat is not None:
            ap = ap.rearrange(pat, **kw)
        c0 = e0 // CH
        c1 = (e1 - 1) // CH
        return View(ap, self.chunks[c0:c1 + 1])


def sub(view, c0, n, esz):
    b0 = (c0 * esz) // (CH * 2)
    b1 = ((c0 + n) * esz - 1) // (CH * 2)
    return View(view.t[:, c0:c0 + n], view.bufs[b0:b1 + 1])


class Alloc:
    def __init__(self, arena):
        self.a = arena
        self.off = 0

    def get(self, nbytes, dtype=BF16, pat=None, **kw):
        self.off = (self.off + 1023) // 1024 * 1024
        v = self.a.view(self.off, nbytes, dtype, pat, **kw)
        self.off += nbytes
        return v


class _Stop(Exception):
    pass


def build_nc(depth=DEPTH, dbg=None, stop=None):
    nc = bass.Bass("TRN2", target_bir_lowering=False)
    S = Sched(nc)
    AR = Arena(S)

    def din(name, shape, dt=F32):
        return nc.dram_tensor(name, list(shape), dt, kind="ExternalInput").ap()

    def dscr(name, shape, dt):
        return nc.dram_tensor(name, list(shape), dt, kind=("ExternalOutput" if dbg else "Internal")).ap()

    x0 = din("x0", [D, NT])
    cvec = din("cvec", [128, KT, 2])
    w_mod = din("w_mod", [DEPTH, D, 3 * D])
    b_mod = din("b_mod", [DEPTH, 128, 48])
    g_pre = din("g_pre", [DEPTH, 128, KT])
    g_post = din("g_post", [DEPTH, 128, KT])
    w_in = din("w_in", [DEPTH, D, 14336])
    b_gate = din("b_gate", [DEPTH, 128, 64])
    na_bias = din("na_bias", [DEPTH, 8, 128, 5, 640])
    pool_w = din("pool_w", [DEPTH, 128, 4, 128])
    pool_sc = din("pool_sc", [DEPTH, 128, 4])
    pool_edge = din("pool_edge", [128, 2, 4, 2, 8])
    conv_w = din("conv_w", [DEPTH, 128, 4, 3])
    ssm_are = din("ssm_are", [DEPTH, 128, 32])
    ssm_aim = din("ssm_aim", [DEPTH, 128, 32])
    ssm_ldt = din("ssm_ldt", [DEPTH, 128, 32])
    ssm_B = din("ssm_B", [DEPTH, 128, 32, 2, 128])
    ssm_C = din("ssm_C", [DEPTH, 128, 32, 2, 128])
    ssm_d = din("ssm_d", [DEPTH, 128, 4])
    glu_w = din("glu_w", [DEPTH, 512, 1024])
    w_br = din("w_br", [DEPTH, D, D])
    w_o = din("w_o", [DEPTH, D, D])
    ident_in = din("ident", [128, 128])
    iota17 = din("iota17", [128, 17])
    iota_nt = din("iota_nt", [128, NT])
    outT = nc.dram_tensor("outT", [D, NL], F32, kind="ExternalOutput").ap()

    xs = dscr("xs", [D, NT], F32)
    hT = dscr("hT", [D, NT], BF16)
    proj = dscr("proj", [14336, NT], BF16)
    vT = dscr("vT", [NT, 512], BF16)
    a_d = dscr("a_d", [D, NT], BF16)
    mg = dscr("mg", [D, NT], BF16)
    yss = dscr("yss", [512, NT], BF16)
    dbufs = {}
    dbg_pob = nc.dram_tensor("dbg_pob", [4, 128, NL], BF16, kind="ExternalOutput").ap() if dbg else None

    def DB(name, rb, ti):
        k = (name, rb, ti)
        if k not in dbufs:
            dbufs[k] = Buf(str(k))
        return dbufs[k]

    def DBrow(name, rb):
        return [DB(name, rb, ti) for ti in range(9)]

    NTI = len(TILES)

    ident = S.sb("identb", [128, 128], BF16)
    ones = S.sb("onesb", [128, 128], BF16)
    epsb = S.sb("epsb", [128, 1], F32)
    cact = S.sb("cact", [128, KT, 2], BF16)
    cv32 = S.sb("cv32", [128, KT, 2], F32)
    modt = S.sb("modt", [128, 48, 2], F32)
    bmod = S.sb("bmod", [128, 48], F32)
    gpre = S.sb("gpre", [128, KT], F32)
    gpost = S.sb("gpost", [128, KT], F32)
    Amod = S.sb("Amod", [128, KT, 2], F32)
    Gmod = S.sb("Gmod", [128, KT, 2], F32)
    bgate = S.sb("bgate", [128, 64], F32)
    psc = S.sb("psc", [128, 4], F32)
    pedge = S.sb("pedge", [128, 2, 4, 2, 8], F32)
    cw = S.sb("cw", [128, 4, 3], F32)
    sdk = S.sb("sdk", [128, 4], F32)
    (s_are, s_aim, s_ldt, s_dt, s_x1, s_th, s_rho, s_thr, s_sn1, s_cs1, s_nr, s_ni, s_den, s_fr, s_fi, s_t1, s_t2) = [
        S.sb(f"ss{i}", [128, 32], F32) for i in range(17)]
    s_ti = S.sb("ssti", [128, 32], I32)
    s_thq = S.sb("ssthq", [128, 32], F32)
    halfpi = S.sb("halfpi", [128, 1], F32)
    sB = S.sb("sB", [128, 2, 128], F32)
    sC = S.sb("sC", [128, 2, 128], F32)
    sT = S.sb("sT", [128, 128], F32)
    sBb = S.sb("sBb", [128, 2, 128], BF16)
    sCb = S.sb("sCb", [128, 3, 128], BF16)
    sWT = S.sb("sWT", [128, 2, 128], BF16)
    stg = [S.sb(f"stg{i}", [128, 512], BF16) for i in range(4)]
    stgf = [S.sb(f"stgf{i}", [128, 512], F32) for i in range(4)]
    rstd = S.sb("rstd", [128, 512], F32)
    rtmp = S.sb("rtmp", [128, 512], F32)

    PSB = nc.alloc_psum_tensor("psall", [128, 8, 512], F32)
    BK = [Buf(f"bank{i}") for i in range(8)]

    def bank(i):
        return View(PSB[:, i, :], [BK[i]])

    def bank2(i):
        return View(PSB[:, i:i + 2, :].rearrange("p a b -> p (a b)"), [BK[i], BK[i + 1]])

    def bankbf(i):
        return View(PSB[:, i, :].bitcast(BF16), [BK[i]])

    S.dma("pool", ident[:, :], ident_in, writes=[ident])
    S.op("dve", lambda e: e.memset(ones[:, :], 1.0), writes=[ones])
    S.op("dve", lambda e: e.memset(epsb[:, :], 1e-6), writes=[epsb])
    S.op("dve", lambda e: e.memset(halfpi[:, :], PI / 2), writes=[halfpi])
    S.dma("sp", cv32[:, :, :], cvec, writes=[cv32])
    S.op("act", lambda e: e.activation(out=cact[:, :, :], in_=cv32[:, :, :], func=AF.Silu), reads=[cv32], writes=[cact])
    S.dma("sp", pedge[:, :, :, :, :], pool_edge, writes=[pedge])

    stg_i = [0]

    def next_stg():
        stg_i[0] += 1
        return stg[stg_i[0] % 4]

    bank_i = [0]

    def next_bank(n=4):
        bank_i[0] += 1
        return bank(bank_i[0] % n)

    def w_panel_view(al):
        return al.get(KT * 1024 * 2, BF16, "p (k n) -> p k n", k=KT)

    def load_w_panel(dst, src_l, col0, ncols=1024):
        src = src_l.rearrange("(k p) n -> p k n", p=128)
        for kk in range(0, KT, 4):
            S.dma("pool", dst[:, kk:kk + 4, 0:ncols], src[:, kk:kk + 4, col0:col0 + ncols], writes=[dst])

    WPM = [AR.view(116 * 1024, KT * 1024 * 2, BF16, "p (k n) -> p k n", k=KT), AR.view(148 * 1024, KT * 1024 * 2, BF16, "p (k n) -> p k n", k=KT)]
    Gm = [Gmod, S.sb("Gmod2", [128, KT, 2], F32)]

    def s1_mod(Lx):
        Gx = Gm[Lx % 2]
        S.dma("sp", bmod[:, :], b_mod[Lx], writes=[bmod])
        S.dma("sp", gpre[:, :], g_pre[Lx], writes=[gpre])
        S.dma("sp", gpost[:, :], g_post[Lx], writes=[gpost])
        for pp in range(6):
            wp = WPM[pp % 2]
            load_w_panel(wp, w_mod[Lx], pp * 1024)
            for bl in range(8):
                j = pp * 8 + bl
                pb = next_bank()
                for k in range(KT):
                    S.op("pe", lambda e, k=k, bl=bl, wp=wp, pb=pb: e.matmul(pb[:, 0:2], lhsT=wp[:, k, bl * 128:(bl + 1) * 128],
                                                                    rhs=cact[:, k, :], start=(k == 0), stop=(k == KT - 1)),
                         reads=[wp, cact], writes=[pb])
                S.op("dve", lambda e, j=j, pb=pb: e.tensor_scalar(out=modt[:, j, :], in0=pb[:, 0:2], scalar1=bmod[:, j:j + 1], scalar2=None, op0=ALU.add),
                     reads=[pb, bmod], writes=[modt])
        S.op("dve", lambda e: e.scalar_tensor_tensor(out=Amod[:, :, :], in0=modt[:, 16:32, :], scalar=1.0,
                                                     in1=gpre[:, :].unsqueeze(2).to_broadcast([128, KT, 2]), op0=ALU.add, op1=ALU.mult),
             reads=[modt, gpre], writes=[Amod])
        S.op("dve", lambda e, Gx=Gx: e.tensor_tensor(out=Gx[:, :, :], in0=modt[:, 32:48, :],
                                                     in1=gpost[:, :].unsqueeze(2).to_broadcast([128, KT, 2]), op=ALU.mult),
             reads=[modt, gpost], writes=[Gx])

    s1_mod(0)
    for L in range(depth):
      try:
        xsrc = x0 if L == 0 else xs
        xsn = "x0" if L == 0 else "xs"
        last = (L == depth - 1) and not dbg
        with_ctx = (L < DEPTH - 1) and not last
        al = Alloc(AR)
        S.dma("sp", bgate[:, :], b_gate[L], writes=[bgate])
        S.dma("sp", psc[:, :], pool_sc[L], writes=[psc])
        S.dma("sp", cw[:, :, :], conv_w[L], writes=[cw])
        S.dma("sp", sdk[:, :], ssm_d[L], writes=[sdk])
        Gmod = Gm[L % 2]

        if stop == "S1":
            raise _Stop()
        XTS = [al.get(KT * 512 * 4, F32, "p (k n) -> p k n", k=KT) for _ in range(2)]
        HBS = [al.get(KT * 512 * 2, BF16, "p (k n) -> p k n", k=KT) for _ in range(2)]

        def load_x2(ti_):
            c0_, n_ = TILES[ti_]
            S.dma("sp", XTS[ti_ % 2][:, :, 0:n_], xsrc.rearrange("(k p) n -> p k n", p=128)[:, :, c0_:c0_ + n_],
                  reads=[DB(xsn, 0, ti_)], writes=[XTS[ti_ % 2]])

        load_x2(0)
        for ti, (c0, n) in enumerate(TILES):
            s = 0 if ti < 8 else 1
            xt = XTS[ti % 2]
            hb = HBS[ti % 2]
            if ti + 1 < NTI:
                load_x2(ti + 1)
            S.op("act", lambda e, n=n, xt=xt, hb=hb: e.activation(out=hb[:, :, 0:n], in_=xt[:, :, 0:n], func=AF.Square), reads=[xt], writes=[hb])
            pb = next_bank()
            for k in range(KT):
                S.op("pe", lambda e, k=k, n=n, pb=pb, hb=hb: e.matmul(pb[:, 0:n], lhsT=ones[:, :], rhs=hb[:, k, 0:n], start=(k == 0), stop=(k == KT - 1)),
                     reads=[ones, hb], writes=[pb])
            S.op("act", lambda e, n=n, pb=pb: e.activation(out=rtmp[:, 0:n], in_=pb[:, 0:n], func=AF.Sqrt, bias=epsb[:, 0:1], scale=1.0 / D),
                 reads=[pb, epsb], writes=[rtmp])
            S.op("dve", lambda e, n=n: e.reciprocal(out=rstd[:, 0:n], in_=rtmp[:, 0:n]), reads=[rtmp], writes=[rstd])
            S.op("dve", lambda e, n=n, xt=xt: e.tensor_tensor(out=xt[:, :, 0:n], in0=xt[:, :, 0:n],
                                                       in1=rstd[:, 0:n].unsqueeze(1).to_broadcast([128, KT, n]), op=ALU.mult),
                 reads=[xt, rstd], writes=[xt])
            for k in range(KT):
                S.op("act", lambda e, k=k, n=n, s=s, xt=xt, hb=hb: e.activation(out=hb[:, k, 0:n], in_=xt[:, k, 0:n], func=AF.Identity,
                                                                 bias=modt[:, k, s:s + 1], scale=Amod[:, k, s:s + 1]),
                     reads=[xt, modt, Amod], writes=[hb])
            S.dma("sp", hT.rearrange("(k p) n -> p k n", p=128)[:, :, c0:c0 + n], hb[:, :, 0:n], reads=[hb], writes=[DB("hT", 0, ti)])

        if stop == "S2":
            raise _Stop()
        al = Alloc(AR)
        WP = [w_panel_view(al), w_panel_view(al)]
        HB = [al.get(KT * 512 * 2, BF16, "p (k n) -> p k n", k=KT) for _ in range(2)]
        vst = al.get(512 * 2 * 2, BF16, "p (a n) -> p a n", a=2)
        load_w_panel(WP[0], w_in[L], 0)
        hcnt = 0

        def load_h(cnt):
            ti_ = cnt % NTI
            c0_, n_ = TILES[ti_]
            hb_ = HB[cnt % 2]
            S.dma("sp", hb_[:, :, 0:n_], hT.rearrange("(k p) n -> p k n", p=128)[:, :, c0_:c0_ + n_], reads=[DB("hT", 0, ti_)], writes=[hb_])

        load_h(0)
        for pp in range(14):
            wp = WP[pp % 2]
            if pp + 1 < 14:
                load_w_panel(WP[(pp + 1) % 2], w_in[L], (pp + 1) * 1024)
            for ti, (c0, n) in enumerate(TILES):
                hbt = HB[hcnt % 2]
                hcnt += 1
                if hcnt < 14 * NTI:
                    load_h(hcnt)
                for bl in range(8):
                    bi = pp * 8 + bl
                    if 8 <= bi < 12:
                        if bi > 8:
                            continue
                        for tb in range(n // 128):
                            pb = next_bank()
                            for k in range(KT):
                                S.op("pe", lambda e, k=k, tb=tb, pb=pb, hbt=hbt, wp=wp: e.matmul(pb[:, :], lhsT=hbt[:, k, tb * 128:(tb + 1) * 128],
                                                                                         rhs=wp[:, k, 0:512], start=(k == 0), stop=(k == KT - 1)),
                                     reads=[hbt, wp], writes=[pb])
                            st = next_stg()
                            S.op("dve", lambda e, pb=pb, st=st: e.tensor_copy(out=st[:, :], in_=pb[:, :]), reads=[pb], writes=[st])
                            S.dma("sp", vT[c0 + tb * 128:c0 + (tb + 1) * 128, :], st[:, :], reads=[st], writes=[DB("vT", 0, ti)])
                        continue
                    pb = next_bank()
                    for k in range(KT):
                        S.op("pe", lambda e, k=k, bl=bl, pb=pb, hbt=hbt, wp=wp, n=n: e.matmul(pb[:, 0:n], lhsT=wp[:, k, bl * 128:(bl + 1) * 128],
                                                                                      rhs=hbt[:, k, 0:n], start=(k == 0), stop=(k == KT - 1)),
                             reads=[hbt, wp], writes=[pb])
                    st = next_stg()
                    if bi < 4:
                        S.op("dve", lambda e, pb=pb, st=st, n=n: e.tensor_scalar(out=st[:, 0:n], in0=pb[:, 0:n], scalar1=0.125, scalar2=None, op0=ALU.mult),
                             reads=[pb], writes=[st])
                    elif bi >= 48:
                        S.op("act", lambda e, pb=pb, st=st, n=n, bi=bi: e.activation(out=st[:, 0:n], in_=pb[:, 0:n], func=AF.Sigmoid,
                                                                                  bias=bgate[:, bi - 48:bi - 47], scale=1.0),
                             reads=[pb, bgate], writes=[st])
                    elif (12 <= bi < 16) or (20 <= bi < 24) or (36 <= bi < 40) or (44 <= bi < 48):
                        S.op("act", lambda e, pb=pb, st=st, n=n: e.activation(out=st[:, 0:n], in_=pb[:, 0:n], func=AF.Silu), reads=[pb], writes=[st])
                    else:
                        S.op("dve", lambda e, pb=pb, st=st, n=n: e.tensor_copy(out=st[:, 0:n], in_=pb[:, 0:n]), reads=[pb], writes=[st])
                    S.dma("sp", proj[bi * 128:(bi + 1) * 128, c0:c0 + n], st[:, 0:n], reads=[st], writes=[DB("proj", bi, ti)])

        if stop == "S3":
            raise _Stop()
        al = Alloc(AR)
        qs = al.get(NT * 2)
        ks = al.get(NT * 2)
        szs = al.get(NT * 2)
        ao = al.get(NT * 2)
        vs = al.get(34 * 128 * 2, BF16, "p (b c) -> p b c", b=34)
        bt = al.get(2 * 5 * 640 * 2, BF16, "p (h t k) -> p h t k", h=2, t=5)
        PT = [al.get(896 * 2) for _ in range(2)]
        rden = S.sb(f"rden{L}", [128, 128], F32) if L == 0 else rden
        otmp = S.sb(f"otmp{L}", [128, 128], F32) if L == 0 else otmp
        acnt = 0
        for hp in range(4):
            S.dma("sp", qs[:, :], proj[hp * 128:(hp + 1) * 128, :], reads=DBrow("proj", hp), writes=[qs])
            S.dma("sp", ks[:, :], proj[512 + hp * 128:512 + (hp + 1) * 128, :], reads=DBrow("proj", 4 + hp), writes=[ks])
            S.dma("sp", szs[:, :], proj[1536 + hp * 128:1536 + (hp + 1) * 128, :], reads=DBrow("proj", 12 + hp), writes=[szs])
            S.dma("sp", vs[:, :, :], vT.rearrange("(b p) c -> p b c", p=128)[:, :, hp * 128:(hp + 1) * 128], reads=DBrow("vT", 0), writes=[vs])
            S.dma("pool", bt[:, :, :, :], na_bias[L, 2 * hp:2 * hp + 2].rearrange("h p t k -> p h t k"), writes=[bt])
            qblocks = [("band", i) for i in range(32)] + ([("ctx", 0), ("ctx", 1)] if with_ctx else [])
            for kind, i in qblocks:
                for hd in range(2):
                    pbs = 64 * hd
                    sb2 = bank2(4 + 2 * (acnt % 2))
                    oc = bank(acnt % 2 + 2) if False else bank(2 + acnt % 2)
                    pt = PT[acnt % 2]
                    acnt += 1
                    if kind == "band":
                        r0 = 2 * i
                        pat = 0 if i == 0 else 1 if i == 1 else 3 if i == 30 else 4 if i == 31 else 2
                        lo = min(max(r0 - 4, 0), 54)
                        k0 = lo * 64
                        q0 = r0 * 64
                        nkb = 7
                        for b in range(5):
                            S.op("pe", lambda e, b=b, sb2=sb2, pbs=pbs, k0=k0, q0=q0: e.matmul(sb2[:, b * 128:(b + 1) * 128], lhsT=ks[pbs:pbs + 64, k0 + b * 128:k0 + (b + 1) * 128],
                                                                                     rhs=qs[pbs:pbs + 64, q0:q0 + 128], start=True, stop=False),
                                 reads=[ks, qs], writes=[sb2])
                            S.op("pe", lambda e, b=b, sb2=sb2, hd=hd, pat=pat: e.matmul(sb2[:, b * 128:(b + 1) * 128], lhsT=bt[:, hd, pat, b * 128:(b + 1) * 128],
                                                                                    rhs=ident[:, :], start=False, stop=True),
                                 reads=[bt, ident], writes=[sb2])
                        for cb in range(2):
                            S.op("pe", lambda e, cb=cb, sb2=sb2, pbs=pbs, q0=q0: e.matmul(sb2[:, 640 + cb * 128:640 + (cb + 1) * 128],
                                                                                      lhsT=ks[pbs:pbs + 64, NL + cb * 128:NL + (cb + 1) * 128],
                                                                                      rhs=qs[pbs:pbs + 64, q0:q0 + 128], start=True, stop=True),
                                 reads=[ks, qs], writes=[sb2])
                        vblk = [k0 // 128 + b for b in range(5)] + [32, 33]
                    else:
                        q0 = NL + i * 128
                        nkb = 2
                        for cb in range(2):
                            S.op("pe", lambda e, cb=cb, sb2=sb2, pbs=pbs, q0=q0: e.matmul(sb2[:, cb * 128:(cb + 1) * 128],
                                                                                      lhsT=ks[pbs:pbs + 64, NL + cb * 128:NL + (cb + 1) * 128],
                                                                                      rhs=qs[pbs:pbs + 64, q0:q0 + 128], start=True, stop=True),
                                 reads=[ks, qs], writes=[sb2])
                        vblk = [32, 33]
                    nk = nkb * 128
                    S.op("act", lambda e, sb2=sb2, pt=pt, nk=nk: e.activation(out=pt[:, 0:nk], in_=sb2[:, 0:nk], func=AF.Exp), reads=[sb2], writes=[pt])
                    for kb in range(nkb):
                        S.op("pe", lambda e, kb=kb, oc=oc, pt=pt, vb=vblk[kb], nkb=nkb: e.matmul(oc[:, 0:128], lhsT=vs[:, vb, :], rhs=pt[:, kb * 128:(kb + 1) * 128],
                                                                                       start=(kb == 0), stop=(kb == nkb - 1)),
                             reads=[vs, pt], writes=[oc])
                    for kb in range(nkb):
                        S.op("pe", lambda e, kb=kb, oc=oc, pt=pt, nkb=nkb: e.matmul(oc[:, 128:256], lhsT=ones[:, :], rhs=pt[:, kb * 128:(kb + 1) * 128],
                                                                             start=(kb == 0), stop=(kb == nkb - 1)),
                             reads=[ones, pt], writes=[oc])
                    S.op("dve", lambda e, oc=oc, pbs=pbs: e.reciprocal(out=rden[pbs:pbs + 64, :], in_=oc[pbs:pbs + 64, 128:256]), reads=[oc], writes=[rden])
                    S.op("dve", lambda e, oc=oc, pbs=pbs: e.tensor_tensor(out=otmp[pbs:pbs + 64, :], in0=oc[pbs:pbs + 64, 0:128], in1=rden[pbs:pbs + 64, :], op=ALU.mult),
                         reads=[oc, rden], writes=[otmp])
                    S.op("dve", lambda e, pbs=pbs, q0=q0: e.tensor_tensor(out=ao[pbs:pbs + 64, q0:q0 + 128], in0=otmp[pbs:pbs + 64, :], in1=szs[pbs:pbs + 64, q0:q0 + 128], op=ALU.mult),
                         reads=[otmp, szs], writes=[ao])
            ncol = NT if with_ctx else NL
            S.dma("sp", a_d[hp * 128:(hp + 1) * 128, 0:ncol], ao[:, 0:ncol], reads=[ao], writes=DBrow("a", hp))

        if stop == "S4":
            raise _Stop()
        if L + 1 < depth:
            s1_mod(L + 1)
        al = Alloc(AR)
        ubp = al.get(NT * 2)
        zbp = al.get(NT * 2)
        pob = al.get(NT * 2)
        U = al.get((NL + 16) * 4, F32)
        T1 = al.get((NL + 16) * 4, F32)
        T2 = al.get((NL + 16) * 4, F32)
        pwb = al.get(4 * 128 * 2, BF16, "p (g d) -> p g d", g=4)
        pab = al.get(NT * 2)
        S.dma("pool", pwb[:, :, :], pool_w[L], writes=[pwb])
        seqs = [(0, NL, 0)] + ([(NL, NCX, 1)] if with_ctx else [])
        for g in range(4):
            S.dma("sp", ubp[:, :], proj[2048 + g * 128:2048 + (g + 1) * 128, :], reads=DBrow("proj", 16 + g), writes=[ubp])
            S.dma("sp", zbp[:, :], proj[2560 + g * 128:2560 + (g + 1) * 128, :], reads=DBrow("proj", 20 + g), writes=[zbp])
            w = (2, 4, 8, 16)[g]
            for (c0, n, sq) in seqs:
                Ln = n + 16
                S.op("dve", lambda e, Ln=Ln: e.memset(U[:, 0:Ln], 0.0), writes=[U])
                S.op("dve", lambda e, c0=c0, n=n: e.tensor_copy(out=U[:, 8:8 + n], in_=ubp[:, c0:c0 + n]), reads=[ubp, U], writes=[U])
                S.op("dve", lambda e, Ln=Ln: e.tensor_tensor(out=T1[:, 1:Ln], in0=U[:, 0:Ln - 1], in1=U[:, 1:Ln], op=ALU.add), reads=[U], writes=[T1])
                cur, oth = T1, T2
                if w >= 4:
                    S.op("dve", lambda e, Ln=Ln: e.tensor_tensor(out=T2[:, 2:Ln - 1], in0=T1[:, 1:Ln - 2], in1=T1[:, 3:Ln], op=ALU.add), reads=[T1], writes=[T2])
                    cur, oth = T2, T1
                if w >= 8:
                    S.op("dve", lambda e, Ln=Ln: e.tensor_tensor(out=T1[:, 4:Ln - 3], in0=T2[:, 2:Ln - 5], in1=T2[:, 6:Ln - 1], op=ALU.add), reads=[T2], writes=[T1])
                    cur, oth = T1, T2
                if w >= 16:
                    S.op("dve", lambda e, Ln=Ln: e.tensor_tensor(out=T2[:, 8:Ln - 7], in0=T1[:, 4:Ln - 11], in1=T1[:, 12:Ln - 3], op=ALU.add), reads=[T1], writes=[T2])
                    cur, oth = T2, T1
                S.op("dve", lambda e, cur=cur, oth=oth, n=n, w=w: e.scalar_tensor_tensor(out=oth[:, 8:8 + n], in0=cur[:, 8:8 + n], scalar=1.0 / w, in1=U[:, 8:8 + n],
                                                                                 op0=ALU.mult, op1=ALU.subtract), reads=[cur, U], writes=[oth])
                for side, e0 in ((0, 8), (1, 8 + n - 8)):
                    S.op("dve", lambda e, cur=cur, sq=sq, g=g, side=side, e0=e0: e.tensor_tensor(out=rtmp[:, 0:8], in0=cur[:, e0:e0 + 8], in1=pedge[:, sq, g, side, :], op=ALU.mult),
                         reads=[cur, pedge], writes=[rtmp])
                    S.op("dve", lambda e, oth=oth, e0=e0: e.tensor_tensor(out=oth[:, e0:e0 + 8], in0=rtmp[:, 0:8], in1=U[:, e0:e0 + 8], op=ALU.subtract),
                         reads=[rtmp, U, oth], writes=[oth])
                S.op("act", lambda e, oth=oth, c0=c0, n=n: e.activation(out=pob[:, c0:c0 + n], in_=oth[:, 8:8 + n], func=AF.Copy), reads=[oth], writes=[pob])
                if dbg and L == 0 and sq == 0:
                    S.dma("sp", dbg_pob[g], pob[:, 0:NL], reads=[pob], writes=[DB("dbgpob", g, 0)])
                for cc in range(0, n, 512):
                    m = min(512, n - cc)
                    pb = next_bank()
                    S.op("pe", lambda e, pb=pb, g=g, c=c0 + cc, m=m: e.matmul(pb[:, 0:m], lhsT=pwb[:, g, :], rhs=pob[:, c:c + m], start=True, stop=True),
                         reads=[pwb, pob], writes=[pb])
                    S.op("dve", lambda e, pb=pb, g=g, c=c0 + cc, m=m: e.scalar_tensor_tensor(out=pab[:, c:c + m], in0=pb[:, 0:m], scalar=psc[:, g:g + 1], in1=zbp[:, c:c + m],
                                                                                    op0=ALU.mult, op1=ALU.mult), reads=[pb, psc, zbp], writes=[pab])
            ncol = NT if with_ctx else NL
            S.dma("sp", a_d[512 + g * 128:512 + (g + 1) * 128, 0:ncol], pab[:, 0:ncol], reads=[pab], writes=DBrow("a", 4 + g))

        if stop == "S5":
            raise _Stop()
        al = Alloc(AR)
        xb_ = al.get(NT * 2)
        bb_ = al.get(NT * 2)
        cb_ = al.get(NT * 2)
        zb = al.get(NT * 2)
        cab = al.get(NT * 2)
        Tt = al.get((NL + 2) * 4, F32)
        Dw = al.get(NL * 4, F32)
        for g in range(4):
            S.dma("sp", xb_[:, :], proj[3072 + g * 128:3072 + (g + 1) * 128, :], reads=DBrow("proj", 24 + g), writes=[xb_])
            S.dma("sp", bb_[:, :], proj[3584 + g * 128:3584 + (g + 1) * 128, :], reads=DBrow("proj", 28 + g), writes=[bb_])
            S.dma("sp", cb_[:, :], proj[4096 + g * 128:4096 + (g + 1) * 128, :], reads=DBrow("proj", 32 + g), writes=[cb_])
            S.dma("sp", zb[:, :], proj[4608 + g * 128:4608 + (g + 1) * 128, :], reads=DBrow("proj", 36 + g), writes=[zb])
            for (c0, n, sq) in seqs:
                S.op("dve", lambda e, n=n: e.memset(Tt[:, 0:n + 2], 0.0), writes=[Tt])
                S.op("dve", lambda e, c0=c0, n=n: e.tensor_tensor(out=Tt[:, 1:n + 1], in0=cb_[:, c0:c0 + n], in1=xb_[:, c0:c0 + n], op=ALU.mult),
                     reads=[cb_, xb_, Tt], writes=[Tt])
                S.op("dve", lambda e, n=n, g=g: e.tensor_scalar(out=Dw[:, 0:n], in0=Tt[:, 0:n], scalar1=cw[:, g, 0:1], scalar2=None, op0=ALU.mult), reads=[Tt, cw], writes=[Dw])
                for j in (1, 2):
                    S.op("dve", lambda e, n=n, g=g, j=j: e.scalar_tensor_tensor(out=Dw[:, 0:n], in0=Tt[:, j:j + n], scalar=cw[:, g, j:j + 1], in1=Dw[:, 0:n],
                                                                           op0=ALU.mult, op1=ALU.add), reads=[Tt, cw, Dw], writes=[Dw])
                S.op("dve", lambda e, c0=c0, n=n: e.tensor_tensor(out=Dw[:, 0:n], in0=Dw[:, 0:n], in1=bb_[:, c0:c0 + n], op=ALU.mult), reads=[Dw, bb_], writes=[Dw])
                S.op("dve", lambda e, c0=c0, n=n: e.tensor_tensor(out=cab[:, c0:c0 + n], in0=Dw[:, 0:n], in1=zb[:, c0:c0 + n], op=ALU.mult), reads=[Dw, zb], writes=[cab])
            ncol = NT if with_ctx else NL
            S.dma("sp", a_d[1024 + g * 128:1024 + (g + 1) * 128, 0:ncol], cab[:, 0:ncol], reads=[cab], writes=DBrow("a", 8 + g))

        if stop == "S6":
            raise _Stop()
        al = Alloc(AR)
        NH = NT // 2
        csT = al.get(NT * 4, F32)
        snT = al.get(NT * 4, F32)
        csT2 = al.get(NT * 4, F32)
        snT2 = al.get(NT * 4, F32)
        TAB = [(csT, snT), (csT2, snT2)]
        BTr = al.get(NT * 4, F32)
        BTi = al.get(NT * 4, F32)
        Yacc = al.get(NT * 4, F32)
        iot = al.get(NT * 4, F32)
        usb = al.get(NT * 2)
        sre = al.get(NT * 2)
        sim = al.get(NT * 2)
        sP3 = al.get(NT * 2)
        sP4 = al.get(NT * 2)
        S.dma("sp", iot[:, :], iota_nt, writes=[iot])
        S.dma("sp", s_are[:, :], ssm_are[L], writes=[s_are])
        S.dma("sp", s_aim[:, :], ssm_aim[L], writes=[s_aim])
        S.dma("sp", s_ldt[:, :], ssm_ldt[L], writes=[s_ldt])
        S.op("act", lambda e: e.activation(out=s_dt[:, :], in_=s_ldt[:, :], func=AF.Exp), reads=[s_ldt], writes=[s_dt])
        S.op("dve", lambda e: e.tensor_tensor(out=s_x1[:, :], in0=s_are[:, :], in1=s_dt[:, :], op=ALU.mult), reads=[s_are, s_dt], writes=[s_x1])
        S.op("dve", lambda e: e.tensor_tensor(out=s_th[:, :], in0=s_aim[:, :], in1=s_dt[:, :], op=ALU.mult), reads=[s_aim, s_dt], writes=[s_th])
        S.op("act", lambda e: e.activation(out=s_rho[:, :], in_=s_x1[:, :], func=AF.Exp), reads=[s_x1], writes=[s_rho])

        def rr(dst, src, tmp, tint, n, shift=0.0):
            S.op("dve", lambda e: e.tensor_scalar(out=tmp[:, 0:n], in0=src[:, 0:n], scalar1=shift, scalar2=1.0 / TWO_PI, op0=ALU.add, op1=ALU.mult),
                 reads=[src], writes=[tmp])
            S.op("dve", lambda e: e.tensor_copy(out=tint[:, 0:n], in_=tmp[:, 0:n]), reads=[tmp], writes=[tint])
            S.op("dve", lambda e: e.tensor_copy(out=tmp[:, 0:n], in_=tint[:, 0:n]), reads=[tint], writes=[tmp])
            S.op("dve", lambda e: e.scalar_tensor_tensor(out=tmp[:, 0:n], in0=tmp[:, 0:n], scalar=-TWO_PI, in1=src[:, 0:n], op0=ALU.mult, op1=ALU.add),
                 reads=[tmp, src], writes=[tmp])
            if shift != 0.0:
                S.op("dve", lambda e: e.tensor_scalar(out=tmp[:, 0:n], in0=tmp[:, 0:n], scalar1=shift, scalar2=None, op0=ALU.add), reads=[tmp], writes=[tmp])
            S.op("dve", lambda e: e.tensor_scalar(out=dst[:, 0:n], in0=tmp[:, 0:n], scalar1=PI, scalar2=-TWO_PI, op0=ALU.is_gt, op1=ALU.mult), reads=[tmp], writes=[dst])
            S.op("dve", lambda e: e.tensor_tensor(out=tmp[:, 0:n], in0=tmp[:, 0:n], in1=dst[:, 0:n], op=ALU.add), reads=[tmp, dst], writes=[tmp])
            S.op("dve", lambda e: e.tensor_scalar(out=dst[:, 0:n], in0=tmp[:, 0:n], scalar1=-PI, scalar2=TWO_PI, op0=ALU.is_lt, op1=ALU.mult), reads=[tmp], writes=[dst])
            S.op("dve", lambda e: e.tensor_tensor(out=dst[:, 0:n], in0=tmp[:, 0:n], in1=dst[:, 0:n], op=ALU.add), reads=[tmp, dst], writes=[dst])

        rr(s_thr, s_th, s_t1, s_ti, 32)
        S.op("dve", lambda e: e.tensor_scalar(out=s_thq[:, :], in0=s_thr[:, :], scalar1=1.0 / TWO_PI, scalar2=None, op0=ALU.mult), reads=[s_thr], writes=[s_thq])
        S.op("act", lambda e: e.activation(out=s_sn1[:, :], in_=s_thr[:, :], func=AF.Sin), reads=[s_thr], writes=[s_sn1])
        rr(s_t2, s_thr, s_t1, s_ti, 32, shift=PI / 2)
        S.op("act", lambda e: e.activation(out=s_cs1[:, :], in_=s_t2[:, :], func=AF.Sin), reads=[s_t2], writes=[s_cs1])
        S.op("dve", lambda e: e.tensor_tensor(out=s_nr[:, :], in0=s_rho[:, :], in1=s_cs1[:, :], op=ALU.mult), reads=[s_rho, s_cs1], writes=[s_nr])
        S.op("dve", lambda e: e.tensor_scalar(out=s_nr[:, :], in0=s_nr[:, :], scalar1=-1.0, scalar2=None, op0=ALU.add), reads=[s_nr], writes=[s_nr])
        S.op("dve", lambda e: e.tensor_tensor(out=s_ni[:, :], in0=s_rho[:, :], in1=s_sn1[:, :], op=ALU.mult), reads=[s_rho, s_sn1], writes=[s_ni])
        S.op("dve", lambda e: e.tensor_tensor(out=s_t1[:, :], in0=s_are[:, :], in1=s_are[:, :], op=ALU.mult), reads=[s_are], writes=[s_t1])
        S.op("dve", lambda e: e.tensor_tensor(out=s_t2[:, :], in0=s_aim[:, :], in1=s_aim[:, :], op=ALU.mult), reads=[s_aim], writes=[s_t2])
        S.op("dve", lambda e: e.tensor_tensor(out=s_t1[:, :], in0=s_t1[:, :], in1=s_t2[:, :], op=ALU.add), reads=[s_t1, s_t2], writes=[s_t1])
        S.op("dve", lambda e: e.reciprocal(out=s_den[:, :], in_=s_t1[:, :]), reads=[s_t1], writes=[s_den])
        S.op("dve", lambda e: e.tensor_tensor(out=s_t1[:, :], in0=s_nr[:, :], in1=s_are[:, :], op=ALU.mult), reads=[s_nr, s_are], writes=[s_t1])
        S.op("dve", lambda e: e.tensor_tensor(out=s_t2[:, :], in0=s_ni[:, :], in1=s_aim[:, :], op=ALU.mult), reads=[s_ni, s_aim], writes=[s_t2])
        S.op("dve", lambda e: e.tensor_tensor(out=s_t1[:, :], in0=s_t1[:, :], in1=s_t2[:, :], op=ALU.add), reads=[s_t1, s_t2], writes=[s_t1])
        S.op("dve", lambda e: e.tensor_tensor(out=s_fr[:, :], in0=s_t1[:, :], in1=s_den[:, :], op=ALU.mult), reads=[s_t1, s_den], writes=[s_fr])
        S.op("dve", lambda e: e.tensor_tensor(out=s_t1[:, :], in0=s_ni[:, :], in1=s_are[:, :], op=ALU.mult), reads=[s_ni, s_are], writes=[s_t1])
        S.op("dve", lambda e: e.tensor_tensor(out=s_t2[:, :], in0=s_nr[:, :], in1=s_aim[:, :], op=ALU.mult), reads=[s_nr, s_aim], writes=[s_t2])
        S.op("dve", lambda e: e.tensor_tensor(out=s_t1[:, :], in0=s_t1[:, :], in1=s_t2[:, :], op=ALU.subtract), reads=[s_t1, s_t2], writes=[s_t1])
        S.op("dve", lambda e: e.tensor_tensor(out=s_fi[:, :], in0=s_t1[:, :], in1=s_den[:, :], op=ALU.mult), reads=[s_t1, s_den], writes=[s_fi])

        def gen_tables(u, cs_, sn_):
            S.op("dve", lambda e: e.tensor_scalar(out=BTr[:, :], in0=iot[:, :], scalar1=s_thq[:, u:u + 1], scalar2=MAGIC, op0=ALU.mult, op1=ALU.add),
                 reads=[iot, s_thq], writes=[BTr])
            S.op("dve", lambda e: e.tensor_scalar(out=BTr[:, :], in0=BTr[:, :], scalar1=MAGIC, scalar2=-TWO_PI, op0=ALU.subtract, op1=ALU.mult), reads=[BTr], writes=[BTr])
            S.op("dve", lambda e: e.scalar_tensor_tensor(out=sn_[:, :], in0=iot[:, :], scalar=s_thr[:, u:u + 1], in1=BTr[:, :], op0=ALU.mult, op1=ALU.add),
                 reads=[iot, s_thr, BTr], writes=[sn_])
            S.op("dve", lambda e: e.tensor_scalar(out=sn_[:, :], in0=sn_[:, :], scalar1=-PI, scalar2=PI, op0=ALU.max, op1=ALU.min), reads=[sn_], writes=[sn_])
            S.op("act", lambda e: e.activation(out=cs_[:, :], in_=sn_[:, :], func=AF.Abs), reads=[sn_], writes=[cs_])
            S.op("act", lambda e: e.activation(out=cs_[:, :], in_=cs_[:, :], func=AF.Sin, bias=halfpi[:, 0:1], scale=-1.0), reads=[cs_, halfpi], writes=[cs_])
            S.op("act", lambda e: e.activation(out=sn_[:, :], in_=sn_[:, :], func=AF.Sin), reads=[sn_], writes=[sn_])

        for o in range(4):
            S.dma("sp", usb[:, 0:NCX], proj[5120 + o * 128:5120 + (o + 1) * 128, NL:NT], reads=[DB("proj", 40 + o, 8)], writes=[usb])
            S.dma("sp", usb[:, NCX:NT], proj[5120 + o * 128:5120 + (o + 1) * 128, 0:NL], reads=DBrow("proj", 40 + o), writes=[usb])
            units = [(d_, j_) for d_ in range(2) for j_ in range(4)]
            gen_tables(units[0][0] * 16 + o * 4 + units[0][1], *TAB[0])
            for k, (d, jj) in enumerate(units):
                if True:
                    first = (k == 0)
                    u = d * 16 + o * 4 + jj
                    cs_, sn_ = TAB[k % 2]
                    S.dma("sp", sB[:, :, :], ssm_B[L][:, u], writes=[sB])
                    S.dma("sp", sC[:, :, :], ssm_C[L][:, u], writes=[sC])
                    S.op("dve", lambda e, u=u: e.tensor_scalar(out=sT[:, :], in0=sB[:, 1, :], scalar1=s_fi[:, u:u + 1], scalar2=None, op0=ALU.mult), reads=[sB, s_fi], writes=[sT])
                    S.op("dve", lambda e, u=u: e.scalar_tensor_tensor(out=sBb[:, 0, :], in0=sB[:, 0, :], scalar=s_fr[:, u:u + 1], in1=sT[:, :], op0=ALU.mult, op1=ALU.subtract),
                         reads=[sB, s_fr, sT], writes=[sBb])
                    S.op("dve", lambda e, u=u: e.tensor_scalar(out=sT[:, :], in0=sB[:, 0, :], scalar1=s_fi[:, u:u + 1], scalar2=None, op0=ALU.mult), reads=[sB, s_fi], writes=[sT])
                    S.op("dve", lambda e, u=u: e.scalar_tensor_tensor(out=sBb[:, 1, :], in0=sB[:, 1, :], scalar=s_fr[:, u:u + 1], in1=sT[:, :], op0=ALU.mult, op1=ALU.add),
                         reads=[sB, s_fr, sT], writes=[sBb])
                    S.op("act", lambda e: e.activation(out=sCb[:, 0, :], in_=sC[:, 0, :], func=AF.Copy), reads=[sC], writes=[sCb])
                    S.op("act", lambda e: e.activation(out=sCb[:, 1, :], in_=sC[:, 1, :], func=AF.Copy, scale=-1.0), reads=[sC], writes=[sCb])
                    S.op("act", lambda e: e.activation(out=sCb[:, 2, :], in_=sC[:, 0, :], func=AF.Copy, scale=-1.0), reads=[sC], writes=[sCb])
                    pbt = bankbf(0)
                    for ri in range(2):
                        S.op("pe", lambda e, ri=ri, pbt=pbt: e.transpose(out=pbt[:, ri * 128:(ri + 1) * 128], in_=sBb[:, ri, :], identity=ident[:, :]),
                             reads=[sBb, ident], writes=[pbt])
                    S.op("act", lambda e, pbt=pbt: e.activation(out=sWT[:, :, :], in_=pbt[:, 0:256].rearrange("p (a b) -> p a b", a=2), func=AF.Copy), reads=[pbt], writes=[sWT])
                    if k + 1 < 8:
                        dn_, jn_ = units[k + 1]
                        gen_tables(dn_ * 16 + o * 4 + jn_, *TAB[(k + 1) % 2])
                    def rsl(buf, c0, n, d=d):
                        if d == 0:
                            return buf[:, c0:c0 + n]
                        st_ = NT - 1 - c0
                        sp_ = st_ - n
                        return buf[:, st_::-1] if sp_ < 0 else buf[:, st_:sp_:-1]
                    for c0 in range(0, NT, 512):
                        n = min(512, NT - c0)
                        p0 = bank(1 + 3 * ((c0 // 512) % 2))
                        pq = bank(2 + 3 * ((c0 // 512) % 2))
                        uc0 = c0 if d == 0 else (NCX + c0 if c0 < NL else 0)
                        S.op("pe", lambda e, uc0=uc0, n=n, p0=p0: e.matmul(p0[:, 0:n], lhsT=sWT[:, 0, :], rhs=usb[:, uc0:uc0 + n], start=True, stop=True), reads=[sWT, usb], writes=[p0])
                        S.op("pe", lambda e, uc0=uc0, n=n, pq=pq: e.matmul(pq[:, 0:n], lhsT=sWT[:, 1, :], rhs=usb[:, uc0:uc0 + n], start=True, stop=True), reads=[sWT, usb], writes=[pq])
                        ta, tb, tc, td = stgf[0], stgf[1], stgf[2], stgf[3]
                        S.op("dve", lambda e, c0=c0, n=n, p0=p0, v=rsl(cs_, c0, min(512, NT - c0)): e.tensor_tensor(out=ta[:, 0:n], in0=p0[:, 0:n], in1=v, op=ALU.mult), reads=[p0, cs_], writes=[ta])
                        S.op("dve", lambda e, c0=c0, n=n, pq=pq, v=rsl(sn_, c0, min(512, NT - c0)): e.tensor_tensor(out=tb[:, 0:n], in0=pq[:, 0:n], in1=v, op=ALU.mult), reads=[pq, sn_], writes=[tb])
                        S.op("dve", lambda e, c0=c0, n=n: e.tensor_tensor(out=BTr[:, c0:c0 + n], in0=ta[:, 0:n], in1=tb[:, 0:n], op=ALU.add), reads=[ta, tb], writes=[sub(BTr, c0, n, 4)])
                        S.op("dve", lambda e, c0=c0, n=n, pq=pq, v=rsl(cs_, c0, min(512, NT - c0)): e.tensor_tensor(out=tc[:, 0:n], in0=pq[:, 0:n], in1=v, op=ALU.mult), reads=[pq, cs_], writes=[tc])
                        S.op("dve", lambda e, c0=c0, n=n, p0=p0, v=rsl(sn_, c0, min(512, NT - c0)): e.tensor_tensor(out=td[:, 0:n], in0=p0[:, 0:n], in1=v, op=ALU.mult), reads=[p0, sn_], writes=[td])
                        S.op("dve", lambda e, c0=c0, n=n: e.tensor_tensor(out=BTi[:, c0:c0 + n], in0=tc[:, 0:n], in1=td[:, 0:n], op=ALU.subtract), reads=[tc, td], writes=[sub(BTi, c0, n, 4)])
                    for BT in (BTr, BTi):
                        bv = BT[:, :] if d == 0 else BT[:, ::-1]
                        S.op("dve", lambda e, bv=bv, u=u: e.tensor_tensor_scan(out=bv, data0=s_rho[:, u:u + 1].to_broadcast([128, NT]), data1=bv, initial=0.0,
                                                                          op0=ALU.mult, op1=ALU.add), reads=[BT, s_rho], writes=[BT])
                    cvf = rsl(cs_, 0, NT)
                    svf = rsl(sn_, 0, NT)
                    S.op("dve", lambda e, cvf=cvf: e.tensor_tensor(out=sre[:, :], in0=BTr[:, :], in1=cvf, op=ALU.mult), reads=[BTr, cs_], writes=[sre])
                    S.op("dve", lambda e, svf=svf: e.tensor_tensor(out=sim[:, :], in0=BTi[:, :], in1=svf, op=ALU.mult), reads=[BTi, sn_], writes=[sim])
                    S.op("dve", lambda e, cvf=cvf: e.tensor_tensor(out=sP3[:, :], in0=BTi[:, :], in1=cvf, op=ALU.mult), reads=[BTi, cs_], writes=[sP3])
                    S.op("dve", lambda e, svf=svf: e.tensor_tensor(out=sP4[:, :], in0=BTr[:, :], in1=svf, op=ALU.mult), reads=[BTr, sn_], writes=[sP4])
                    for c0 in range(0, NT, 512):
                        n = min(512, NT - c0)
                        p0 = bank(3 if (c0 // 512) % 2 == 0 else 6)
                        yc0 = c0 if d == 0 else (NCX + c0 if c0 < NL else 0)
                        S.op("pe", lambda e, c0=c0, n=n, p0=p0: e.matmul(p0[:, 0:n], lhsT=sCb[:, 0, :], rhs=sre[:, c0:c0 + n], start=True, stop=False), reads=[sCb, sre], writes=[p0])
                        S.op("pe", lambda e, c0=c0, n=n, p0=p0: e.matmul(p0[:, 0:n], lhsT=sCb[:, 2, :], rhs=sim[:, c0:c0 + n], start=False, stop=False), reads=[sCb, sim], writes=[p0])
                        S.op("pe", lambda e, c0=c0, n=n, p0=p0: e.matmul(p0[:, 0:n], lhsT=sCb[:, 1, :], rhs=sP3[:, c0:c0 + n], start=False, stop=False), reads=[sCb, sP3], writes=[p0])
                        S.op("pe", lambda e, c0=c0, n=n, p0=p0: e.matmul(p0[:, 0:n], lhsT=sCb[:, 1, :], rhs=sP4[:, c0:c0 + n], start=False, stop=True), reads=[sCb, sP4], writes=[p0])
                        if first:
                            S.op("dve", lambda e, c0=yc0, n=n, p0=p0: e.tensor_copy(out=Yacc[:, c0:c0 + n], in_=p0[:, 0:n]), reads=[p0], writes=[Yacc])
                        else:
                            S.op("dve", lambda e, c0=yc0, n=n, p0=p0: e.tensor_tensor(out=Yacc[:, c0:c0 + n], in0=p0[:, 0:n], in1=Yacc[:, c0:c0 + n], op=ALU.add), reads=[p0, Yacc], writes=[Yacc])
            S.op("dve", lambda e, o=o: e.scalar_tensor_tensor(out=Yacc[:, :], in0=usb[:, :], scalar=sdk[:, o:o + 1], in1=Yacc[:, :], op0=ALU.mult, op1=ALU.add),
                 reads=[usb, sdk, Yacc], writes=[Yacc])
            S.op("dve", lambda e: e.tensor_tensor(out=BTr[:, :], in0=Yacc[:, :], in1=Yacc[:, :], op=ALU.mult), reads=[Yacc], writes=[BTr])
            S.op("dve", lambda e: e.tensor_scalar(out=BTr[:, :], in0=BTr[:, :], scalar1=0.044715, scalar2=1.0, op0=ALU.mult, op1=ALU.add), reads=[BTr], writes=[BTr])
            S.op("dve", lambda e: e.tensor_tensor(out=BTr[:, :], in0=BTr[:, :], in1=Yacc[:, :], op=ALU.mult), reads=[BTr, Yacc], writes=[BTr])
            S.op("act", lambda e: e.activation(out=BTi[:, :], in_=BTr[:, :], func=AF.Sigmoid, scale=float(2.0 * np.sqrt(2.0 / np.pi))), reads=[BTr], writes=[BTi])
            S.op("dve", lambda e: e.tensor_tensor(out=sre[:, :], in0=BTi[:, :], in1=Yacc[:, :], op=ALU.mult), reads=[BTi, Yacc], writes=[sre])
            S.dma("sp", yss[o * 128:(o + 1) * 128, NL:NT], sre[:, 0:NCX], reads=[sre], writes=[DB("yss", o, 1)])
            S.dma("sp", yss[o * 128:(o + 1) * 128, 0:NL], sre[:, NCX:NT], reads=[sre], writes=[DB("yss", o, 0)])
        al = Alloc(AR)
        gw = al.get(4 * 1024 * 2, BF16, "p (k n) -> p k n", k=4)
        YT = [al.get(4 * 512 * 2, BF16, "p (k n) -> p k n", k=4) for _ in range(2)]
        ZT = [al.get(4 * 512 * 2, BF16, "p (k n) -> p k n", k=4) for _ in range(2)]
        S.dma("pool", gw[:, :, :], glu_w[L].rearrange("(k p) n -> p k n", p=128), writes=[gw])
        tiles7 = TILES if with_ctx else TILES[:8]
        for ti, (c0, n) in enumerate(tiles7):
            yt = YT[ti % 2]
            zt = ZT[ti % 2]
            S.dma("sp", yt[:, :, 0:n], yss.rearrange("(k p) n -> p k n", p=128)[:, :, c0:c0 + n], reads=[DB("yss", oo, hh) for oo in range(4) for hh in range(2)], writes=[yt])
            S.dma("sp", zt[:, :, 0:n], proj[5632:6144, :].rearrange("(k p) n -> p k n", p=128)[:, :, c0:c0 + n], reads=[DB("proj", 44 + oo, ti) for oo in range(4)], writes=[zt])
            for ob in range(4):
                pa = next_bank()
                for k in range(4):
                    S.op("pe", lambda e, k=k, ob=ob, pa=pa, yt=yt, n=n: e.matmul(pa[:, 0:n], lhsT=gw[:, k, ob * 128:(ob + 1) * 128], rhs=yt[:, k, 0:n], start=(k == 0), stop=(k == 3)),
                         reads=[gw, yt], writes=[pa])
                pg = next_bank()
                for k in range(4):
                    S.op("pe", lambda e, k=k, ob=ob, pg=pg, yt=yt, n=n: e.matmul(pg[:, 0:n], lhsT=gw[:, k, 512 + ob * 128:512 + (ob + 1) * 128], rhs=yt[:, k, 0:n], start=(k == 0), stop=(k == 3)),
                         reads=[gw, yt], writes=[pg])
                S.op("act", lambda e, pg=pg, n=n: e.activation(out=stgf[2][:, 0:n], in_=pg[:, 0:n], func=AF.Sigmoid), reads=[pg], writes=[stgf[2]])
                S.op("dve", lambda e, pa=pa, n=n: e.tensor_tensor(out=stgf[2][:, 0:n], in0=pa[:, 0:n], in1=stgf[2][:, 0:n], op=ALU.mult), reads=[pa, stgf[2]], writes=[stgf[2]])
                st = next_stg()
                S.op("dve", lambda e, st=st, zt=zt, ob=ob, n=n: e.tensor_tensor(out=st[:, 0:n], in0=stgf[2][:, 0:n], in1=zt[:, ob, 0:n], op=ALU.mult), reads=[stgf[2], zt], writes=[st])
                S.dma("sp", a_d[1536 + ob * 128:1536 + (ob + 1) * 128, c0:c0 + n], st[:, 0:n], reads=[st], writes=[DB("a", 12 + ob, ti)])

        if stop == "S7":
            raise _Stop()
        al = Alloc(AR)
        GT = [al.get(4 * 512 * 2, BF16, "p (i n) -> p i n", i=4) for _ in range(2)]
        al.off = 16 * 1024
        wbr = al.get(KT * D * 2, BF16, "p (k n) -> p k n", k=KT)
        AT = [al.get(KT * 512 * 2, BF16, "p (k n) -> p k n", k=KT) for _ in range(2)]
        wo = AR.view(112 * 1024, KT * D * 2, BF16, "p (k n) -> p k n", k=KT)
        for half in range(2):
            src = w_br[L].rearrange("(k p) n -> p k n", p=128)
            for kk in range(0, KT, 4):
                S.dma("pool", wbr[:, kk:kk + 4, half * 1024:(half + 1) * 1024], src[:, kk:kk + 4, half * 1024:(half + 1) * 1024], writes=[wbr])
        for half in range(2):
            src = w_o[L].rearrange("(k p) n -> p k n", p=128)
            for kk in range(0, KT, 4):
                S.dma("pool", wo[:, kk:kk + 4, half * 1024:(half + 1) * 1024], src[:, kk:kk + 4, half * 1024:(half + 1) * 1024], writes=[wo])
        tiles8 = TILES if with_ctx else TILES[:8]
        gcnt = 0
        def load_at(ti_):
            c0_, n_ = tiles8[ti_]
            S.dma("sp", AT[ti_ % 2][:, :, 0:n_], a_d.rearrange("(k p) n -> p k n", p=128)[:, :, c0_:c0_ + n_],
                  reads=[DB("a", rb, t2) for rb in range(16) for t2 in range(9)], writes=[AT[ti_ % 2]])

        def load_gt(cnt):
            ti_, j_ = divmod(cnt, 16)
            c0_, n_ = tiles8[ti_]
            S.dma("sp", GT[cnt % 2][:, :, 0:n_], proj[6144:14336, :].rearrange("(i j p) n -> j p i n", i=4, j=16)[j_][:, :, c0_:c0_ + n_],
                  reads=[DB("proj", 48 + i * 16 + j_, ti_) for i in range(4)], writes=[GT[cnt % 2]])

        load_at(0)
        load_gt(0)
        for ti, (c0, n) in enumerate(tiles8):
            at = AT[ti % 2]
            if ti + 1 < len(tiles8):
                load_at(ti + 1)
            for j in range(16):
                gt = GT[gcnt % 2]
                gcnt += 1
                if gcnt < 16 * len(tiles8):
                    load_gt(gcnt)
                acc = stgf[0]
                tmpf = stgf[1]
                for i in range(4):
                    pb = next_bank()
                    for kk in range(4):
                        S.op("pe", lambda e, pb=pb, i=i, kk=kk, j=j, at=at, n=n: e.matmul(pb[:, 0:n], lhsT=wbr[:, i * 4 + kk, j * 128:(j + 1) * 128],
                                                                                  rhs=at[:, i * 4 + kk, 0:n], start=(kk == 0), stop=(kk == 3)),
                             reads=[wbr, at], writes=[pb])
                    if i == 0:
                        S.op("dve", lambda e, pb=pb, gt=gt, n=n: e.tensor_tensor(out=acc[:, 0:n], in0=pb[:, 0:n], in1=gt[:, 0, 0:n], op=ALU.mult),
                             reads=[pb, gt], writes=[acc])
                    else:
                        S.op("dve", lambda e, pb=pb, gt=gt, n=n, i=i: e.tensor_tensor(out=tmpf[:, 0:n], in0=pb[:, 0:n], in1=gt[:, i, 0:n], op=ALU.mult),
                             reads=[pb, gt], writes=[tmpf])
                        if i < 3:
                            S.op("dve", lambda e, n=n: e.tensor_tensor(out=acc[:, 0:n], in0=acc[:, 0:n], in1=tmpf[:, 0:n], op=ALU.add), reads=[acc, tmpf], writes=[acc])
                        else:
                            mst = next_stg()
                            S.op("dve", lambda e, n=n, mst=mst: e.tensor_tensor(out=mst[:, 0:n], in0=acc[:, 0:n], in1=tmpf[:, 0:n], op=ALU.add), reads=[acc, tmpf], writes=[mst])
                            S.dma("sp", mg[j * 128:(j + 1) * 128, c0:c0 + n], mst[:, 0:n], reads=[mst], writes=[DB("mg", j, ti)])

        if stop == "S8":
            raise _Stop()
        al = Alloc(AR)
        MT2 = [al.get(KT * 256 * 2, BF16, "p (k n) -> p k n", k=KT) for _ in range(2)]
        Y = al.get(KT * 256 * 4, F32, "p (k n) -> p k n", k=KT)
        SQ = al.get(KT * 256 * 2, BF16, "p (k n) -> p k n", k=KT)
        XT9 = [al.get(KT * 256 * 4, F32, "p (k n) -> p k n", k=KT) for _ in range(2)]
        ntok = NT if with_ctx else NL

        def load9(qi_):
            c0_ = qi_ * 256
            ti_ = min(c0_ // 512, 8)
            S.dma("sp", MT2[qi_ % 2][:, :, :], mg.rearrange("(k p) n -> p k n", p=128)[:, :, c0_:c0_ + 256], reads=[DB("mg", j_, ti_) for j_ in range(16)], writes=[MT2[qi_ % 2]])
            S.dma("sp", XT9[qi_ % 2][:, :, :], xsrc.rearrange("(k p) n -> p k n", p=128)[:, :, c0_:c0_ + 256], reads=[DB(xsn, 0, ti_)], writes=[XT9[qi_ % 2]])

        for qi, c0 in enumerate(range(0, ntok, 256)):
            n = 256
            ti = min(c0 // 512, 8)
            s = 0 if c0 < NL else 1
            mt = MT2[qi % 2]
            XT = XT9[qi % 2]
            if qi == 0:
                load9(0)
            if c0 + 256 < ntok:
                load9(qi + 1)
            for j in range(16):
                pb = next_bank()
                for k in range(KT):
                    S.op("pe", lambda e, pb=pb, k=k, j=j, mt=mt, wo=wo: e.matmul(pb[:, 0:256], lhsT=wo[:, k, j * 128:(j + 1) * 128], rhs=mt[:, k, :],
                                                                     start=(k == 0), stop=(k == KT - 1)), reads=[wo, mt], writes=[pb])
                S.op("dve", lambda e, pb=pb, j=j: e.tensor_copy(out=Y[:, j, :], in_=pb[:, 0:256]), reads=[pb], writes=[Y])
                S.op("act", lambda e, j=j: e.activation(out=SQ[:, j, :], in_=Y[:, j, :], func=AF.Square), reads=[Y], writes=[SQ])
            if stop == "S9a":
                raise _Stop()
            pb = next_bank()
            for k in range(KT):
                S.op("pe", lambda e, k=k, pb=pb: e.matmul(pb[:, 0:256], lhsT=ones[:, :], rhs=SQ[:, k, :], start=(k == 0), stop=(k == KT - 1)),
                     reads=[ones, SQ], writes=[pb])
            S.op("act", lambda e, pb=pb: e.activation(out=rtmp[:, 0:256], in_=pb[:, 0:256], func=AF.Sqrt, bias=epsb[:, 0:1], scale=1.0 / D),
                 reads=[pb, epsb], writes=[rtmp])
            S.op("dve", lambda e: e.reciprocal(out=rstd[:, 0:256], in_=rtmp[:, 0:256]), reads=[rtmp], writes=[rstd])
            S.op("dve", lambda e: e.tensor_tensor(out=Y[:, :, :], in0=Y[:, :, :], in1=rstd[:, 0:256].unsqueeze(1).to_broadcast([128, KT, 256]), op=ALU.mult),
                 reads=[Y, rstd], writes=[Y])
            if stop == "S9b":
                raise _Stop()
            for k in range(KT):
                S.op("dve", lambda e, k=k, s=s, XT=XT, Gmod=Gmod: e.scalar_tensor_tensor(out=XT[:, k, :], in0=Y[:, k, :], scalar=Gmod[:, k, s:s + 1], in1=XT[:, k, :],
                                                                     op0=ALU.mult, op1=ALU.add), reads=[Y, Gmod, XT], writes=[XT])
            if last:
                S.dma("sp", outT.rearrange("(k p) n -> p k n", p=128)[:, :, c0:c0 + n], XT[:, :, :], reads=[XT], writes=[DB("out", 0, qi)])
            else:
                S.dma("sp", xs.rearrange("(k p) n -> p k n", p=128)[:, :, c0:c0 + n], XT[:, :, :], reads=[XT], writes=[DB("xs", 0, ti)])
            if stop == "S9c":
                raise _Stop()
      except _Stop:
        break

    finals = [o for o in S.qops["sp"] if o.dma][-40:]
    S.emit(final_ops=finals)
    return nc


def _fm(v, nb):
    return np.ascontiguousarray(np.swapaxes(v.reshape(v.shape[:-1] + (nb, 128)), -1, -2))


def _na_bias(rpb):
    pats = [(0, 0), (2, 0), (8, 4), (60, 54), (62, 54)]
    drow = np.zeros((5, 128, 640), np.int64)
    dcol = np.zeros((5, 128, 640), np.int64)
    mask = np.zeros((5, 128, 640), bool)
    qc = np.arange(64)
    kc = np.arange(64)
    cs = np.clip(qc - 8, 0, 48)
    inwin = (kc[None, :] >= cs[:, None]) & (kc[None, :] < cs[:, None] + 16)
    dc = np.clip(kc[None, :] - qc[:, None] + 15, 0, 30)
    for pi, (r0, lo) in enumerate(pats):
        for dr in range(2):
            r = r0 + dr
            bs = min(max(r - 4, 0), 56)
            for ko in range(10):
                kr = lo + ko
                inb = bs <= kr < bs + 8
                drw = min(max(kr - r + 7, 0), 14)
                drow[pi, dr * 64:(dr + 1) * 64, ko * 64:(ko + 1) * 64] = drw
                dcol[pi, dr * 64:(dr + 1) * 64, ko * 64:(ko + 1) * 64] = dc
                mask[pi, dr * 64:(dr + 1) * 64, ko * 64:(ko + 1) * 64] = inwin & inb
    g = rpb[:, :, drow, dcol]
    g = np.where(mask[None, None], g, np.float32(-30000.0)).astype(np.float32)
    return np.ascontiguousarray(g.transpose(0, 1, 3, 2, 4))


def _pool_edge():
    out = np.zeros((128, 2, 4, 2, 8), np.float32)
    for si, n in enumerate((NL, NCX)):
        for g, w in enumerate((2, 4, 8, 16)):
            t = np.arange(n)
            lo = np.clip(t - w // 2, 0, n)
            hi = np.clip(t + w - w // 2, 0, n)
            inv = (1.0 / (hi - lo)).astype(np.float32)
            out[:, si, g, 0, :] = inv[None, 0:8]
            out[:, si, g, 1, :] = inv[None, n - 8:n]
    return out


_NC_CACHE = {}


def _prep(x, c, ctx, c_ctx, w_mod, b_mod, g_pre, g_post, w_in, b_gate, na_rpb, pool_w,
           pool_scale, conv_w, ssm_a_re, ssm_a_im, ssm_log_dt, ssm_b_re, ssm_b_im,
           ssm_c_re, ssm_c_im, ssm_d, glu_w, w_br, w_o):
    f = np.float32
    x = np.asarray(x, f); ctx = np.asarray(ctx, f); c = np.asarray(c, f); c_ctx = np.asarray(c_ctx, f)
    shared = {
        "w_mod": np.ascontiguousarray(w_mod, f),
        "b_mod": _fm(np.asarray(b_mod, f), 48),
        "g_pre": _fm(np.asarray(g_pre, f), 16),
        "g_post": _fm(np.asarray(g_post, f), 16),
        "w_in": np.ascontiguousarray(w_in, f),
        "b_gate": _fm(np.asarray(b_gate, f), 64),
        "na_bias": _na_bias(np.asarray(na_rpb, f)),
        "pool_w": np.ascontiguousarray(np.asarray(pool_w, f).transpose(0, 2, 1, 3)),
        "pool_sc": _fm(np.asarray(pool_scale, f), 4),
        "pool_edge": _pool_edge(),
        "conv_w": np.ascontiguousarray(np.asarray(conv_w, f).reshape(DEPTH, 3, 4, 128).transpose(0, 3, 2, 1)),
        "ssm_d": _fm(np.asarray(ssm_d, f), 4),
        "glu_w": np.ascontiguousarray(glu_w, f),
        "w_br": np.ascontiguousarray(w_br, f),
        "w_o": np.ascontiguousarray(w_o, f),
        "ident": np.eye(128, dtype=f),
        "iota17": np.tile(np.arange(17, dtype=f)[None], (128, 1)),
        "iota_nt": np.tile(np.arange(NT, dtype=f)[None], (128, 1)),
    }

    def upl(a):
        a = np.asarray(a, f).reshape(DEPTH, 2, 16, 2, 64)
        return np.ascontiguousarray(a.transpose(0, 3, 4, 1, 2).reshape(DEPTH, 128, 32))

    shared["ssm_are"] = upl(ssm_a_re)
    shared["ssm_aim"] = upl(ssm_a_im)
    shared["ssm_ldt"] = upl(np.broadcast_to(np.asarray(ssm_log_dt, f)[..., None], (DEPTH, 2, 32, 64)))
    Bp = np.zeros((DEPTH, 128, 32, 2, 128), f)
    Cp = np.zeros((DEPTH, 128, 32, 2, 128), f)
    bre = np.asarray(ssm_b_re, f); bim = np.asarray(ssm_b_im, f)
    cre = np.asarray(ssm_c_re, f); cim = np.asarray(ssm_c_im, f)
    for d in range(2):
        for j in range(16):
            for gl in range(2):
                g = 2 * j + gl
                u = d * 16 + j
                ch0 = (j % 4) * 32 + gl * 16
                Bp[:, gl * 64:(gl + 1) * 64, u, 0, ch0:ch0 + 16] = bre[:, d, g]
                Bp[:, gl * 64:(gl + 1) * 64, u, 1, ch0:ch0 + 16] = bim[:, d, g]
                Cp[:, gl * 64:(gl + 1) * 64, u, 0, ch0:ch0 + 16] = cre[:, d, g].transpose(0, 2, 1)
                Cp[:, gl * 64:(gl + 1) * 64, u, 1, ch0:ch0 + 16] = cim[:, d, g].transpose(0, 2, 1)
    shared["ssm_B"] = Bp
    shared["ssm_C"] = Cp
    in_maps = []
    for core in range(8):
        b = core % 4
        m = dict(shared)
        m["x0"] = np.ascontiguousarray(np.concatenate([x[b].T, ctx[b].T], axis=1))
        cv = np.stack([c[b].reshape(16, 128).T, c_ctx.reshape(16, 128).T], axis=-1)
        m["cvec"] = np.ascontiguousarray(cv, f)
        in_maps.append(m)
    return in_maps


def kernel(x, c, ctx, c_ctx, w_mod, b_mod, g_pre, g_post, w_in, b_gate, na_rpb, pool_w,
           pool_scale, conv_w, ssm_a_re, ssm_a_im, ssm_log_dt, ssm_b_re, ssm_b_im,
           ssm_c_re, ssm_c_im, ssm_d, glu_w, w_br, w_o):
    in_maps = _prep(x, c, ctx, c_ctx, w_mod, b_mod, g_pre, g_post, w_in, b_gate, na_rpb, pool_w,
                    pool_scale, conv_w, ssm_a_re, ssm_a_im, ssm_log_dt, ssm_b_re, ssm_b_im,
                    ssm_c_re, ssm_c_im, ssm_d, glu_w, w_br, w_o)
    if "nc" not in _NC_CACHE:
        _NC_CACHE["nc"] = build_nc()
    res = run_bass_kernel_spmd(_NC_CACHE["nc"], in_maps, core_ids=list(range(8)))
    out = np.stack([np.ascontiguousarray(res.results[b]["outT"].T) for b in range(4)], axis=0)
    return out.astype(np.float32)
```

```python
import numpy as np
import concourse.bass as bass
import concourse.mybir as mybir
from concourse.bass_utils import run_bass_kernel_spmd

F32 = mybir.dt.float32
BF16 = mybir.dt.bfloat16
I32 = mybir.dt.int32
ALU = mybir.AluOpType
AF = mybir.ActivationFunctionType

DEPTH = 4
D = 2048
NL = 4096
NCX = 256
NT = NL + NCX
KT = 16
TILES = [(i * 512, 512) for i in range(8)] + [(4096, 256)]
TWO_PI = float(2 * np.pi)
PI = float(np.pi)
MAGIC = 12582912.0

COMPUTE_Q = ("pe", "act", "dve", "pool")
EPOCH = 20000
NDMA_SEM = 12


class Buf:
    __slots__ = ("name", "last_w", "readers", "t", "bufs")

    def __init__(self, name, t=None):
        self.name = name
        self.last_w = None
        self.readers = []
        self.t = t
        self.bufs = [self]

    def __getitem__(self, k):
        return self.t[k]


class View:
    __slots__ = ("t", "bufs")

    def __init__(self, t, bufs):
        self.t = t
        self.bufs = bufs

    def __getitem__(self, k):
        return self.t[k]


class Op:
    __slots__ = ("q", "fn", "deps", "dma", "sig", "sem", "val", "idx", "prev_same_sem", "seqno")

    def __init__(self, q, fn, dma):
        self.q = q
        self.fn = fn
        self.deps = []
        self.dma = dma
        self.sig = dma
        self.sem = None
        self.val = None
        self.idx = -1
        self.prev_same_sem = None


def _expand(lst):
    out = []
    for b in lst:
        out.extend(b.bufs)
    return out


class Sched:
    def __init__(self, nc):
        self.nc = nc
        self.qops = {q: [] for q in ("pe", "act", "dve", "pool", "sp")}

    def sb(self, name, shape, dtype):
        return Buf(name, self.nc.alloc_sbuf_tensor(name, list(shape), dtype))

    def _add(self, q, fn, reads, writes, dma):
        op = Op(q, fn, dma)
        self.seq = getattr(self, "seq", 0) + 1
        op.idx = self.seq
        reads = _expand(reads)
        writes = _expand(writes)
        deps = []
        for b in reads:
            if b.last_w is not None:
                deps.append((b.last_w, "raw"))
        for b in writes:
            if b.last_w is not None:
                deps.append((b.last_w, "waw"))
            for r in b.readers:
                deps.append((r, "war"))
        seen = set()
        best = {}
        for d, kind in deps:
            if d is op or id(d) in seen:
                continue
            seen.add(id(d))
            if d.q == q and not d.dma and not dma:
                if q == "pe" or kind == "war":
                    continue
            if d.dma:
                op.deps.append(d)
                d.sig = True
            else:
                cur = best.get(d.q)
                if cur is None or d.seqno > cur.seqno:
                    best[d.q] = d
        for d in best.values():
            op.deps.append(d)
            d.sig = True
        op.seqno = self.seq
        for b in reads:
            if not dma:
                b.readers = [r for r in b.readers if r.dma or r.q != q]
            b.readers.append(op)
        for b in writes:
            b.last_w = op
            b.readers = []
        self.qops[q].append(op)
        return op

    def op(self, q, fn, reads=(), writes=()):
        return self._add(q, fn, reads, writes, False)

    def dma(self, q, out, in_, reads=(), writes=()):
        return self._add(q, lambda e: e.dma_start(out=out, in_=in_), reads, writes, True)

    def emit(self, final_ops=()):
        nc = self.nc
        csem = {}
        for q in COMPUTE_Q:
            n = sum(1 for o in self.qops[q] if o.sig and not o.dma)
            ne = max(1, (n + EPOCH - 1) // EPOCH)
            csem[q] = [nc.alloc_semaphore(name=f"s_{q}{i}") for i in range(ne)]
        dsem = {}
        for q in ("sp", "act", "pool"):
            if any(o.dma for o in self.qops[q]):
                dsem[q] = [nc.alloc_semaphore(name=f"d_{q}{i}") for i in range(NDMA_SEM)]
        for q, lst in self.qops.items():
            cnt = 0
            dcnt = 0
            last_on_sem = {}
            for o in lst:
                if o.dma:
                    k = dcnt % NDMA_SEM
                    o.sem = dsem[q][k]
                    o.val = 16 * (dcnt // NDMA_SEM + 1)
                    o.prev_same_sem = last_on_sem.get(k)
                    last_on_sem[k] = o
                    dcnt += 1
                elif o.sig:
                    o.sem = csem[q][cnt // EPOCH]
                    o.val = cnt % EPOCH + 1
                    o.idx = cnt
                    cnt += 1
        engs = {"pe": "tensor", "act": "scalar", "dve": "vector", "pool": "gpsimd", "sp": "sync"}
        with nc.Block() as block:
            for q, lst in self.qops.items():
                if not lst:
                    continue

                def body(e, q=q, lst=lst):
                    waited = {}
                    dwaited = set()

                    def wait_for(d):
                        if d.dma:
                            if id(d) in dwaited:
                                return
                            dwaited.add(id(d))
                            e.wait_ge(d.sem, d.val)
                        else:
                            if waited.get(d.q, -1) >= d.idx:
                                return
                            waited[d.q] = d.idx
                            e.wait_ge(d.sem, d.val)

                    for o in lst:
                        for d in o.deps:
                            wait_for(d)
                        if o.dma and o.prev_same_sem is not None:
                            wait_for(o.prev_same_sem)
                        ins = o.fn(e)
                        if o.dma:
                            ins.then_inc(o.sem, 16)
                        elif o.sig:
                            ins.then_inc(o.sem, 1)
                    if q == "sp":
                        for d in final_ops:
                            wait_for(d)

                getattr(block, engs[q])(body)


CH = 512
NCHUNK = 181


class Arena:
    def __init__(self, S):
        self.t = S.nc.alloc_sbuf_tensor("arena", [128, CH * NCHUNK], BF16)
        self.chunks = [Buf(f"ch{i}") for i in range(NCHUNK)]

    def view(self, off_b, nbytes, dtype=BF16, pat=None, **kw):
        assert off_b % 4 == 0 and off_b + nbytes <= CH * NCHUNK * 2, (off_b, nbytes)
        e0 = off_b // 2
        e1 = (off_b + nbytes) // 2
        ap = self.t[:, e0:e1]
        if dtype != BF16:
            ap = ap.bitcast(dtype)
        if pat is not None:
            ap = ap.rearrange(pat, **kw)
        c0 = e0 // CH
        c1 = (e1 - 1) // CH
        return View(ap, self.chunks[c0:c1 + 1])


def sub(view, c0, n, esz):
    b0 = (c0 * esz) // (CH * 2)
    b1 = ((c0 + n) * esz - 1) // (CH * 2)
    return View(view.t[:, c0:c0 + n], view.bufs[b0:b1 + 1])


class Alloc:
    def __init__(self, arena):
        self.a = arena
        self.off = 0

    def get(self, nbytes, dtype=BF16, pat=None, **kw):
        self.off = (self.off + 1023) // 1024 * 1024
        v = self.a.view(self.off, nbytes, dtype, pat, **kw)
        self.off += nbytes
        return v


class _Stop(Exception):
    pass


def build_nc(depth=DEPTH, dbg=None, stop=None):
    nc = bass.Bass("TRN2", target_bir_lowering=False)
    S = Sched(nc)
    AR = Arena(S)

    def din(name, shape, dt=F32):
        return nc.dram_tensor(name, list(shape), dt, kind="ExternalInput").ap()

    def dscr(name, shape, dt):
        return nc.dram_tensor(name, list(shape), dt, kind=("ExternalOutput" if dbg else "Internal")).ap()

    x0 = din("x0", [D, NT])
    cvec = din("cvec", [128, KT, 2])
    w_mod = din("w_mod", [DEPTH, D, 3 * D])
    b_mod = din("b_mod", [DEPTH, 128, 48])
    g_pre = din("g_pre", [DEPTH, 128, KT])
    g_post = din("g_post", [DEPTH, 128, KT])
    w_in = din("w_in", [DEPTH, D, 14336])
    b_gate = din("b_gate", [DEPTH, 128, 64])
    na_bias = din("na_bias", [DEPTH, 8, 128, 5, 640])
    pool_w = din("pool_w", [DEPTH, 128, 4, 128])
    pool_sc = din("pool_sc", [DEPTH, 128, 4])
    pool_edge = din("pool_edge", [128, 2, 4, 2, 8])
    conv_w = din("conv_w", [DEPTH, 128, 4, 3])
    ssm_are = din("ssm_are", [DEPTH, 128, 32])
    ssm_aim = din("ssm_aim", [DEPTH, 128, 32])
    ssm_ldt = din("ssm_ldt", [DEPTH, 128, 32])
    ssm_B = din("ssm_B", [DEPTH, 128, 32, 2, 128])
    ssm_C = din("ssm_C", [DEPTH, 128, 32, 2, 128])
    ssm_d = din("ssm_d", [DEPTH, 128, 4])
    glu_w = din("glu_w", [DEPTH, 512, 1024])
    w_br = din("w_br", [DEPTH, D, D])
    w_o = din("w_o", [DEPTH, D, D])
    ident_in = din("ident", [128, 128])
    iota17 = din("iota17", [128, 17])
    iota_nt = din("iota_nt", [128, NT])
    outT = nc.dram_tensor("outT", [D, NL], F32, kind="ExternalOutput").ap()

    xs = dscr("xs", [D, NT], F32)
    hT = dscr("hT", [D, NT], BF16)
    proj = dscr("proj", [14336, NT], BF16)
    vT = dscr("vT", [NT, 512], BF16)
    a_d = dscr("a_d", [D, NT], BF16)
    mg = dscr("mg", [D, NT], BF16)
    yss = dscr("yss", [512, NT], BF16)
    dbufs = {}
    dbg_pob = nc.dram_tensor("dbg_pob", [4, 128, NL], BF16, kind="ExternalOutput").ap() if dbg else None

    def DB(name, rb, ti):
        k = (name, rb, ti)
        if k not in dbufs:
            dbufs[k] = Buf(str(k))
        return dbufs[k]

    def DBrow(name, rb):
        return [DB(name, rb, ti) for ti in range(9)]

    NTI = len(TILES)

    ident = S.sb("identb", [128, 128], BF16)
    ones = S.sb("onesb", [128, 128], BF16)
    epsb = S.sb("epsb", [128, 1], F32)
    cact = S.sb("cact", [128, KT, 2], BF16)
    cv32 = S.sb("cv32", [128, KT, 2], F32)
    modt = S.sb("modt", [128, 48, 2], F32)
    bmod = S.sb("bmod", [128, 48], F32)
    gpre = S.sb("gpre", [128, KT], F32)
    gpost = S.sb("gpost", [128, KT], F32)
    Amod = S.sb("Amod", [128, KT, 2], F32)
    Gmod = S.sb("Gmod", [128, KT, 2], F32)
    bgate = S.sb("bgate", [128, 64], F32)
    psc = S.sb("psc", [128, 4], F32)
    pedge = S.sb("pedge", [128, 2, 4, 2, 8], F32)
    cw = S.sb("cw", [128, 4, 3], F32)
    sdk = S.sb("sdk", [128, 4], F32)
    (s_are, s_aim, s_ldt, s_dt, s_x1, s_th, s_rho, s_thr, s_sn1, s_cs1, s_nr, s_ni, s_den, s_fr, s_fi, s_t1, s_t2) = [
        S.sb(f"ss{i}", [128, 32], F32) for i in range(17)]
    s_ti = S.sb("ssti", [128, 32], I32)
    s_thq = S.sb("ssthq", [128, 32], F32)
    halfpi = S.sb("halfpi", [128, 1], F32)
    sB = S.sb("sB", [128, 2, 128], F32)
    sC = S.sb("sC", [128, 2, 128], F32)
    sT = S.sb("sT", [128, 128], F32)
    sBb = S.sb("sBb", [128, 2, 128], BF16)
    sCb = S.sb("sCb", [128, 3, 128], BF16)
    sWT = S.sb("sWT", [128, 2, 128], BF16)
    stg = [S.sb(f"stg{i}", [128, 512], BF16) for i in range(4)]
    stgf = [S.sb(f"stgf{i}", [128, 512], F32) for i in range(4)]
    rstd = S.sb("rstd", [128, 512], F32)
    rtmp = S.sb("rtmp", [128, 512], F32)

    PSB = nc.alloc_psum_tensor("psall", [128, 8, 512], F32)
    BK = [Buf(f"bank{i}") for i in range(8)]

    def bank(i):
        return View(PSB[:, i, :], [BK[i]])

    def bank2(i):
        return View(PSB[:, i:i + 2, :].rearrange("p a b -> p (a b)"), [BK[i], BK[i + 1]])

    def bankbf(i):
        return View(PSB[:, i, :].bitcast(BF16), [BK[i]])

    S.dma("pool", ident[:, :], ident_in, writes=[ident])
    S.op("dve", lambda e: e.memset(ones[:, :], 1.0), writes=[ones])
    S.op("dve", lambda e: e.memset(epsb[:, :], 1e-6), writes=[epsb])
    S.op("dve", lambda e: e.memset(halfpi[:, :], PI / 2), writes=[halfpi])
    S.dma("sp", cv32[:, :, :], cvec, writes=[cv32])
    S.op("act", lambda e: e.activation(out=cact[:, :, :], in_=cv32[:, :, :], func=AF.Silu), reads=[cv32], writes=[cact])
    S.dma("sp", pedge[:, :, :, :, :], pool_edge, writes=[pedge])

    stg_i = [0]

    def next_stg():
        stg_i[0] += 1
        return stg[stg_i[0] % 4]

    bank_i = [0]

    def next_bank(n=4):
        bank_i[0] += 1
        return bank(bank_i[0] % n)

    def w_panel_view(al):
        return al.get(KT * 1024 * 2, BF16, "p (k n) -> p k n", k=KT)

    def load_w_panel(dst, src_l, col0, ncols=1024):
        src = src_l.rearrange("(k p) n -> p k n", p=128)
        for kk in range(0, KT, 4):
            S.dma("pool", dst[:, kk:kk + 4, 0:ncols], src[:, kk:kk + 4, col0:col0 + ncols], writes=[dst])

    WPM = [AR.view(116 * 1024, KT * 1024 * 2, BF16, "p (k n) -> p k n", k=KT), AR.view(148 * 1024, KT * 1024 * 2, BF16, "p (k n) -> p k n", k=KT)]
    Gm = [Gmod, S.sb("Gmod2", [128, KT, 2], F32)]

    def s1_mod(Lx):
        Gx = Gm[Lx % 2]
        S.dma("sp", bmod[:, :], b_mod[Lx], writes=[bmod])
        S.dma("sp", gpre[:, :], g_pre[Lx], writes=[gpre])
        S.dma("sp", gpost[:, :], g_post[Lx], writes=[gpost])
        for pp in range(6):
            wp = WPM[pp % 2]
            load_w_panel(wp, w_mod[Lx], pp * 1024)
            for bl in range(8):
                j = pp * 8 + bl
                pb = next_bank()
                for k in range(KT):
                    S.op("pe", lambda e, k=k, bl=bl, wp=wp, pb=pb: e.matmul(pb[:, 0:2], lhsT=wp[:, k, bl * 128:(bl + 1) * 128],
                                                                    rhs=cact[:, k, :], start=(k == 0), stop=(k == KT - 1)),
                         reads=[wp, cact], writes=[pb])
                S.op("dve", lambda e, j=j, pb=pb: e.tensor_scalar(out=modt[:, j, :], in0=pb[:, 0:2], scalar1=bmod[:, j:j + 1], scalar2=None, op0=ALU.add),
                     reads=[pb, bmod], writes=[modt])
        S.op("dve", lambda e: e.scalar_tensor_tensor(out=Amod[:, :, :], in0=modt[:, 16:32, :], scalar=1.0,
                                                     in1=gpre[:, :].unsqueeze(2).to_broadcast([128, KT, 2]), op0=ALU.add, op1=ALU.mult),
             reads=[modt, gpre], writes=[Amod])
        S.op("dve", lambda e, Gx=Gx: e.tensor_tensor(out=Gx[:, :, :], in0=modt[:, 32:48, :],
                                                     in1=gpost[:, :].unsqueeze(2).to_broadcast([128, KT, 2]), op=ALU.mult),
             reads=[modt, gpost], writes=[Gx])

    s1_mod(0)
    for L in range(depth):
      try:
        xsrc = x0 if L == 0 else xs
        xsn = "x0" if L == 0 else "xs"
        last = (L == depth - 1) and not dbg
        with_ctx = (L < DEPTH - 1) and not last
        al = Alloc(AR)
        S.dma("sp", bgate[:, :], b_gate[L], writes=[bgate])
        S.dma("sp", psc[:, :], pool_sc[L], writes=[psc])
        S.dma("sp", cw[:, :, :], conv_w[L], writes=[cw])
        S.dma("sp", sdk[:, :], ssm_d[L], writes=[sdk])
        Gmod = Gm[L % 2]

        if stop == "S1":
            raise _Stop()
        XTS = [al.get(KT * 512 * 4, F32, "p (k n) -> p k n", k=KT) for _ in range(2)]
        HBS = [al.get(KT * 512 * 2, BF16, "p (k n) -> p k n", k=KT) for _ in range(2)]

        def load_x2(ti_):
            c0_, n_ = TILES[ti_]
            S.dma("sp", XTS[ti_ % 2][:, :, 0:n_], xsrc.rearrange("(k p) n -> p k n", p=128)[:, :, c0_:c0_ + n_],
                  reads=[DB(xsn, 0, ti_)], writes=[XTS[ti_ % 2]])

        load_x2(0)
        for ti, (c0, n) in enumerate(TILES):
            s = 0 if ti < 8 else 1
            xt = XTS[ti % 2]
            hb = HBS[ti % 2]
            if ti + 1 < NTI:
                load_x2(ti + 1)
            S.op("act", lambda e, n=n, xt=xt, hb=hb: e.activation(out=hb[:, :, 0:n], in_=xt[:, :, 0:n], func=AF.Square), reads=[xt], writes=[hb])
            pb = next_bank()
            for k in range(KT):
                S.op("pe", lambda e, k=k, n=n, pb=pb, hb=hb: e.matmul(pb[:, 0:n], lhsT=ones[:, :], rhs=hb[:, k, 0:n], start=(k == 0), stop=(k == KT - 1)),
                     reads=[ones, hb], writes=[pb])
            S.op("act", lambda e, n=n, pb=pb: e.activation(out=rtmp[:, 0:n], in_=pb[:, 0:n], func=AF.Sqrt, bias=epsb[:, 0:1], scale=1.0 / D),
                 reads=[pb, epsb], writes=[rtmp])
            S.op("dve", lambda e, n=n: e.reciprocal(out=rstd[:, 0:n], in_=rtmp[:, 0:n]), reads=[rtmp], writes=[rstd])
            S.op("dve", lambda e, n=n, xt=xt: e.tensor_tensor(out=xt[:, :, 0:n], in0=xt[:, :, 0:n],
                                                       in1=rstd[:, 0:n].unsqueeze(1).to_broadcast([128, KT, n]), op=ALU.mult),
                 reads=[xt, rstd], writes=[xt])
            for k in range(KT):
                S.op("act", lambda e, k=k, n=n, s=s, xt=xt, hb=hb: e.activation(out=hb[:, k, 0:n], in_=xt[:, k, 0:n], func=AF.Identity,
                                                                 bias=modt[:, k, s:s + 1], scale=Amod[:, k, s:s + 1]),
                     reads=[xt, modt, Amod], writes=[hb])
            S.dma("sp", hT.rearrange("(k p) n -> p k n", p=128)[:, :, c0:c0 + n], hb[:, :, 0:n], reads=[hb], writes=[DB("hT", 0, ti)])

        if stop == "S2":
            raise _Stop()
        al = Alloc(AR)
        WP = [w_panel_view(al), w_panel_view(al)]
        HB = [al.get(KT * 512 * 2, BF16, "p (k n) -> p k n", k=KT) for _ in range(2)]
        vst = al.get(512 * 2 * 2, BF16, "p (a n) -> p a n", a=2)
        load_w_panel(WP[0], w_in[L], 0)
        hcnt = 0

        def load_h(cnt):
            ti_ = cnt % NTI
            c0_, n_ = TILES[ti_]
            hb_ = HB[cnt % 2]
            S.dma("sp", hb_[:, :, 0:n_], hT.rearrange("(k p) n -> p k n", p=128)[:, :, c0_:c0_ + n_], reads=[DB("hT", 0, ti_)], writes=[hb_])

        load_h(0)
        for pp in range(14):
            wp = WP[pp % 2]
            if pp + 1 < 14:
                load_w_panel(WP[(pp + 1) % 2], w_in[L], (pp + 1) * 1024)
            for ti, (c0, n) in enumerate(TILES):
                hbt = HB[hcnt % 2]
                hcnt += 1
                if hcnt < 14 * NTI:
                    load_h(hcnt)
                for bl in range(8):
                    bi = pp * 8 + bl
                    if 8 <= bi < 12:
                        if bi > 8:
                            continue
                        for tb in range(n // 128):
                            pb = next_bank()
                            for k in range(KT):
                                S.op("pe", lambda e, k=k, tb=tb, pb=pb, hbt=hbt, wp=wp: e.matmul(pb[:, :], lhsT=hbt[:, k, tb * 128:(tb + 1) * 128],
                                                                                         rhs=wp[:, k, 0:512], start=(k == 0), stop=(k == KT - 1)),
                                     reads=[hbt, wp], writes=[pb])
                            st = next_stg()
                            S.op("dve", lambda e, pb=pb, st=st: e.tensor_copy(out=st[:, :], in_=pb[:, :]), reads=[pb], writes=[st])
                            S.dma("sp", vT[c0 + tb * 128:c0 + (tb + 1) * 128, :], st[:, :], reads=[st], writes=[DB("vT", 0, ti)])
                        continue
                    pb = next_bank()
                    for k in range(KT):
                        S.op("pe", lambda e, k=k, bl=bl, pb=pb, hbt=hbt, wp=wp, n=n: e.matmul(pb[:, 0:n], lhsT=wp[:, k, bl * 128:(bl + 1) * 128],
                                                                                      rhs=hbt[:, k, 0:n], start=(k == 0), stop=(k == KT - 1)),
                             reads=[hbt, wp], writes=[pb])
                    st = next_stg()
                    if bi < 4:
                        S.op("dve", lambda e, pb=pb, st=st, n=n: e.tensor_scalar(out=st[:, 0:n], in0=pb[:, 0:n], scalar1=0.125, scalar2=None, op0=ALU.mult),
                             reads=[pb], writes=[st])
                    elif bi >= 48:
                        S.op("act", lambda e, pb=pb, st=st, n=n, bi=bi: e.activation(out=st[:, 0:n], in_=pb[:, 0:n], func=AF.Sigmoid,
                                                                                  bias=bgate[:, bi - 48:bi - 47], scale=1.0),
                             reads=[pb, bgate], writes=[st])
                    elif (12 <= bi < 16) or (20 <= bi < 24) or (36 <= bi < 40) or (44 <= bi < 48):
                        S.op("act", lambda e, pb=pb, st=st, n=n: e.activation(out=st[:, 0:n], in_=pb[:, 0:n], func=AF.Silu), reads=[pb], writes=[st])
                    else:
                        S.op("dve", lambda e, pb=pb, st=st, n=n: e.tensor_copy(out=st[:, 0:n], in_=pb[:, 0:n]), reads=[pb], writes=[st])
                    S.dma("sp", proj[bi * 128:(bi + 1) * 128, c0:c0 + n], st[:, 0:n], reads=[st], writes=[DB("proj", bi, ti)])

        if stop == "S3":
            raise _Stop()
        al = Alloc(AR)
        qs = al.get(NT * 2)
        ks = al.get(NT * 2)
        szs = al.get(NT * 2)
        ao = al.get(NT * 2)
        vs = al.get(34 * 128 * 2, BF16, "p (b c) -> p b c", b=34)
        bt = al.get(2 * 5 * 640 * 2, BF16, "p (h t k) -> p h t k", h=2, t=5)
        PT = [al.get(896 * 2) for _ in range(2)]
        rden = S.sb(f"rden{L}", [128, 128], F32) if L == 0 else rden
        otmp = S.sb(f"otmp{L}", [128, 128], F32) if L == 0 else otmp
        acnt = 0
        for hp in range(4):
            S.dma("sp", qs[:, :], proj[hp * 128:(hp + 1) * 128, :], reads=DBrow("proj", hp), writes=[qs])
            S.dma("sp", ks[:, :], proj[512 + hp * 128:512 + (hp + 1) * 128, :], reads=DBrow("proj", 4 + hp), writes=[ks])
            S.dma("sp", szs[:, :], proj[1536 + hp * 128:1536 + (hp + 1) * 128, :], reads=DBrow("proj", 12 + hp), writes=[szs])
            S.dma("sp", vs[:, :, :], vT.rearrange("(b p) c -> p b c", p=128)[:, :, hp * 128:(hp + 1) * 128], reads=DBrow("vT", 0), writes=[vs])
            S.dma("pool", bt[:, :, :, :], na_bias[L, 2 * hp:2 * hp + 2].rearrange("h p t k -> p h t k"), writes=[bt])
            qblocks = [("band", i) for i in range(32)] + ([("ctx", 0), ("ctx", 1)] if with_ctx else [])
            for kind, i in qblocks:
                for hd in range(2):
                    pbs = 64 * hd
                    sb2 = bank2(4 + 2 * (acnt % 2))
                    oc = bank(acnt % 2 + 2) if False else bank(2 + acnt % 2)
                    pt = PT[acnt % 2]
                    acnt += 1
                    if kind == "band":
                        r0 = 2 * i
                        pat = 0 if i == 0 else 1 if i == 1 else 3 if i == 30 else 4 if i == 31 else 2
                        lo = min(max(r0 - 4, 0), 54)
                        k0 = lo * 64
                        q0 = r0 * 64
                        nkb = 7
                        for b in range(5):
                            S.op("pe", lambda e, b=b, sb2=sb2, pbs=pbs, k0=k0, q0=q0: e.matmul(sb2[:, b * 128:(b + 1) * 128], lhsT=ks[pbs:pbs + 64, k0 + b * 128:k0 + (b + 1) * 128],
                                                                                     rhs=qs[pbs:pbs + 64, q0:q0 + 128], start=True, stop=False),
                                 reads=[ks, qs], writes=[sb2])
                            S.op("pe", lambda e, b=b, sb2=sb2, hd=hd, pat=pat: e.matmul(sb2[:, b * 128:(b + 1) * 128], lhsT=bt[:, hd, pat, b * 128:(b + 1) * 128],
                                                                                    rhs=ident[:, :], start=False, stop=True),
                                 reads=[bt, ident], writes=[sb2])
                        for cb in range(2):
                            S.op("pe", lambda e, cb=cb, sb2=sb2, pbs=pbs, q0=q0: e.matmul(sb2[:, 640 + cb * 128:640 + (cb + 1) * 128],
                                                                                      lhsT=ks[pbs:pbs + 64, NL + cb * 128:NL + (cb + 1) * 128],
                                                                                      rhs=qs[pbs:pbs + 64, q0:q0 + 128], start=True, stop=True),
                                 reads=[ks, qs], writes=[sb2])
                        vblk = [k0 // 128 + b for b in range(5)] + [32, 33]
                    else:
                        q0 = NL + i * 128
                        nkb = 2
                        for cb in range(2):
                            S.op("pe", lambda e, cb=cb, sb2=sb2, pbs=pbs, q0=q0: e.matmul(sb2[:, cb * 128:(cb + 1) * 128],
                                                                                      lhsT=ks[pbs:pbs + 64, NL + cb * 128:NL + (cb + 1) * 128],
                                                                                      rhs=qs[pbs:pbs + 64, q0:q0 + 128], start=True, stop=True),
                                 reads=[ks, qs], writes=[sb2])
                        vblk = [32, 33]
                    nk = nkb * 128
                    S.op("act", lambda e, sb2=sb2, pt=pt, nk=nk: e.activation(out=pt[:, 0:nk], in_=sb2[:, 0:nk], func=AF.Exp), reads=[sb2], writes=[pt])
                    for kb in range(nkb):
                        S.op("pe", lambda e, kb=kb, oc=oc, pt=pt, vb=vblk[kb], nkb=nkb: e.matmul(oc[:, 0:128], lhsT=vs[:, vb, :], rhs=pt[:, kb * 128:(kb + 1) * 128],
                                                                                       start=(kb == 0), stop=(kb == nkb - 1)),
                             reads=[vs, pt], writes=[oc])
                    for kb in range(nkb):
                        S.op("pe", lambda e, kb=kb, oc=oc, pt=pt, nkb=nkb: e.matmul(oc[:, 128:256], lhsT=ones[:, :], rhs=pt[:, kb * 128:(kb + 1) * 128],
                                                                             start=(kb == 0), stop=(kb == nkb - 1)),
                             reads=[ones, pt], writes=[oc])
                    S.op("dve", lambda e, oc=oc, pbs=pbs: e.reciprocal(out=rden[pbs:pbs + 64, :], in_=oc[pbs:pbs + 64, 128:256]), reads=[oc], writes=[rden])
                    S.op("dve", lambda e, oc=oc, pbs=pbs: e.tensor_tensor(out=otmp[pbs:pbs + 64, :], in0=oc[pbs:pbs + 64, 0:128], in1=rden[pbs:pbs + 64, :], op=ALU.mult),
                         reads=[oc, rden], writes=[otmp])
                    S.op("dve", lambda e, pbs=pbs, q0=q0: e.tensor_tensor(out=ao[pbs:pbs + 64, q0:q0 + 128], in0=otmp[pbs:pbs + 64, :], in1=szs[pbs:pbs + 64, q0:q0 + 128], op=ALU.mult),
                         reads=[otmp, szs], writes=[ao])
            ncol = NT if with_ctx else NL
            S.dma("sp", a_d[hp * 128:(hp + 1) * 128, 0:ncol], ao[:, 0:ncol], reads=[ao], writes=DBrow("a", hp))

        if stop == "S4":
            raise _Stop()
        if L + 1 < depth:
            s1_mod(L + 1)
        al = Alloc(AR)
        ubp = al.get(NT * 2)
        zbp = al.get(NT * 2)
        pob = al.get(NT * 2)
        U = al.get((NL + 16) * 4, F32)
        T1 = al.get((NL + 16) * 4, F32)
        T2 = al.get((NL + 16) * 4, F32)
        pwb = al.get(4 * 128 * 2, BF16, "p (g d) -> p g d", g=4)
        pab = al.get(NT * 2)
        S.dma("pool", pwb[:, :, :], pool_w[L], writes=[pwb])
        seqs = [(0, NL, 0)] + ([(NL, NCX, 1)] if with_ctx else [])
        for g in range(4):
            S.dma("sp", ubp[:, :], proj[2048 + g * 128:2048 + (g + 1) * 128, :], reads=DBrow("proj", 16 + g), writes=[ubp])
            S.dma("sp", zbp[:, :], proj[2560 + g * 128:2560 + (g + 1) * 128, :], reads=DBrow("proj", 20 + g), writes=[zbp])
            w = (2, 4, 8, 16)[g]
            for (c0, n, sq) in seqs:
                Ln = n + 16
                S.op("dve", lambda e, Ln=Ln: e.memset(U[:, 0:Ln], 0.0), writes=[U])
                S.op("dve", lambda e, c0=c0, n=n: e.tensor_copy(out=U[:, 8:8 + n], in_=ubp[:, c0:c0 + n]), reads=[ubp, U], writes=[U])
                S.op("dve", lambda e, Ln=Ln: e.tensor_tensor(out=T1[:, 1:Ln], in0=U[:, 0:Ln - 1], in1=U[:, 1:Ln], op=ALU.add), reads=[U], writes=[T1])
                cur, oth = T1, T2
                if w >= 4:
                    S.op("dve", lambda e, Ln=Ln: e.tensor_tensor(out=T2[:, 2:Ln - 1], in0=T1[:, 1:Ln - 2], in1=T1[:, 3:Ln], op=ALU.add), reads=[T1], writes=[T2])
                    cur, oth = T2, T1
                if w >= 8:
                    S.op("dve", lambda e, Ln=Ln: e.tensor_tensor(out=T1[:, 4:Ln - 3], in0=T2[:, 2:Ln - 5], in1=T2[:, 6:Ln - 1], op=ALU.add), reads=[T2], writes=[T1])
                    cur, oth = T1, T2
                if w >= 16:
                    S.op("dve", lambda e, Ln=Ln: e.tensor_tensor(out=T2[:, 8:Ln - 7], in0=T1[:, 4:Ln - 11], in1=T1[:, 12:Ln - 3], op=ALU.add), reads=[T1], writes=[T2])
                    cur, oth = T2, T1
                S.op("dve", lambda e, cur=cur, oth=oth, n=n, w=w: e.scalar_tensor_tensor(out=oth[:, 8:8 + n], in0=cur[:, 8:8 + n], scalar=1.0 / w, in1=U[:, 8:8 + n],
                                                                                 op0=ALU.mult, op1=ALU.subtract), reads=[cur, U], writes=[oth])
                for side, e0 in ((0, 8), (1, 8 + n - 8)):
                    S.op("dve", lambda e, cur=cur, sq=sq, g=g, side=side, e0=e0: e.tensor_tensor(out=rtmp[:, 0:8], in0=cur[:, e0:e0 + 8], in1=pedge[:, sq, g, side, :], op=ALU.mult),
                         reads=[cur, pedge], writes=[rtmp])
                    S.op("dve", lambda e, oth=oth, e0=e0: e.tensor_tensor(out=oth[:, e0:e0 + 8], in0=rtmp[:, 0:8], in1=U[:, e0:e0 + 8], op=ALU.subtract),
                         reads=[rtmp, U, oth], writes=[oth])
                S.op("act", lambda e, oth=oth, c0=c0, n=n: e.activation(out=pob[:, c0:c0 + n], in_=oth[:, 8:8 + n], func=AF.Copy), reads=[oth], writes=[pob])
                if dbg and L == 0 and sq == 0:
                    S.dma("sp", dbg_pob[g], pob[:, 0:NL], reads=[pob], writes=[DB("dbgpob", g, 0)])
                for cc in range(0, n, 512):
                    m = min(512, n - cc)
                    pb = next_bank()
                    S.op("pe", lambda e, pb=pb, g=g, c=c0 + cc, m=m: e.matmul(pb[:, 0:m], lhsT=pwb[:, g, :], rhs=pob[:, c:c + m], start=True, stop=True),
                         reads=[pwb, pob], writes=[pb])
                    S.op("dve", lambda e, pb=pb, g=g, c=c0 + cc, m=m: e.scalar_tensor_tensor(out=pab[:, c:c + m], in0=pb[:, 0:m], scalar=psc[:, g:g + 1], in1=zbp[:, c:c + m],
                                                                                    op0=ALU.mult, op1=ALU.mult), reads=[pb, psc, zbp], writes=[pab])
            ncol = NT if with_ctx else NL
            S.dma("sp", a_d[512 + g * 128:512 + (g + 1) * 128, 0:ncol], pab[:, 0:ncol], reads=[pab], writes=DBrow("a", 4 + g))

        if stop == "S5":
            raise _Stop()
        al = Alloc(AR)
        xb_ = al.get(NT * 2)
        bb_ = al.get(NT * 2)
        cb_ = al.get(NT * 2)
        zb = al.get(NT * 2)
        cab = al.get(NT * 2)
        Tt = al.get((NL + 2) * 4, F32)
        Dw = al.get(NL * 4, F32)
        for g in range(4):
            S.dma("sp", xb_[:, :], proj[3072 + g * 128:3072 + (g + 1) * 128, :], reads=DBrow("proj", 24 + g), writes=[xb_])
            S.dma("sp", bb_[:, :], proj[3584 + g * 128:3584 + (g + 1) * 128, :], reads=DBrow("proj", 28 + g), writes=[bb_])
            S.dma("sp", cb_[:, :], proj[4096 + g * 128:4096 + (g + 1) * 128, :], reads=DBrow("proj", 32 + g), writes=[cb_])
            S.dma("sp", zb[:, :], proj[4608 + g * 128:4608 + (g + 1) * 128, :], reads=DBrow("proj", 36 + g), writes=[zb])
            for (c0, n, sq) in seqs:
                S.op("dve", lambda e, n=n: e.memset(Tt[:, 0:n + 2], 0.0), writes=[Tt])
                S.op("dve", lambda e, c0=c0, n=n: e.tensor_tensor(out=Tt[:, 1:n + 1], in0=cb_[:, c0:c0 + n], in1=xb_[:, c0:c0 + n], op=ALU.mult),
                     reads=[cb_, xb_, Tt], writes=[Tt])
                S.op("dve", lambda e, n=n, g=g: e.tensor_scalar(out=Dw[:, 0:n], in0=Tt[:, 0:n], scalar1=cw[:, g, 0:1], scalar2=None, op0=ALU.mult), reads=[Tt, cw], writes=[Dw])
                for j in (1, 2):
                    S.op("dve", lambda e, n=n, g=g, j=j: e.scalar_tensor_tensor(out=Dw[:, 0:n], in0=Tt[:, j:j + n], scalar=cw[:, g, j:j + 1], in1=Dw[:, 0:n],
                                                                           op0=ALU.mult, op1=ALU.add), reads=[Tt, cw, Dw], writes=[Dw])
                S.op("dve", lambda e, c0=c0, n=n: e.tensor_tensor(out=Dw[:, 0:n], in0=Dw[:, 0:n], in1=bb_[:, c0:c0 + n], op=ALU.mult), reads=[Dw, bb_], writes=[Dw])
                S.op("dve", lambda e, c0=c0, n=n: e.tensor_tensor(out=cab[:, c0:c0 + n], in0=Dw[:, 0:n], in1=zb[:, c0:c0 + n], op=ALU.mult), reads=[Dw, zb], writes=[cab])
            ncol = NT if with_ctx else NL
            S.dma("sp", a_d[1024 + g * 128:1024 + (g + 1) * 128, 0:ncol], cab[:, 0:ncol], reads=[cab], writes=DBrow("a", 8 + g))

        if stop == "S6":
            raise _Stop()
        al = Alloc(AR)
        NH = NT // 2
        csT = al.get(NT * 4, F32)
        snT = al.get(NT * 4, F32)
        csT2 = al.get(NT * 4, F32)
        snT2 = al.get(NT * 4, F32)
        TAB = [(csT, snT), (csT2, snT2)]
        BTr = al.get(NT * 4, F32)
        BTi = al.get(NT * 4, F32)
        Yacc = al.get(NT * 4, F32)
        iot = al.get(NT * 4, F32)
        usb = al.get(NT * 2)
        sre = al.get(NT * 2)
        sim = al.get(NT * 2)
        sP3 = al.get(NT * 2)
        sP4 = al.get(NT * 2)
        S.dma("sp", iot[:, :], iota_nt, writes=[iot])
        S.dma("sp", s_are[:, :], ssm_are[L], writes=[s_are])
        S.dma("sp", s_aim[:, :], ssm_aim[L], writes=[s_aim])
        S.dma("sp", s_ldt[:, :], ssm_ldt[L], writes=[s_ldt])
        S.op("act", lambda e: e.activation(out=s_dt[:, :], in_=s_ldt[:, :], func=AF.Exp), reads=[s_ldt], writes=[s_dt])
        S.op("dve", lambda e: e.tensor_tensor(out=s_x1[:, :], in0=s_are[:, :], in1=s_dt[:, :], op=ALU.mult), reads=[s_are, s_dt], writes=[s_x1])
        S.op("dve", lambda e: e.tensor_tensor(out=s_th[:, :], in0=s_aim[:, :], in1=s_dt[:, :], op=ALU.mult), reads=[s_aim, s_dt], writes=[s_th])
        S.op("act", lambda e: e.activation(out=s_rho[:, :], in_=s_x1[:, :], func=AF.Exp), reads=[s_x1], writes=[s_rho])

        def rr(dst, src, tmp, tint, n, shift=0.0):
            S.op("dve", lambda e: e.tensor_scalar(out=tmp[:, 0:n], in0=src[:, 0:n], scalar1=shift, scalar2=1.0 / TWO_PI, op0=ALU.add, op1=ALU.mult),
                 reads=[src], writes=[tmp])
            S.op("dve", lambda e: e.tensor_copy(out=tint[:, 0:n], in_=tmp[:, 0:n]), reads=[tmp], writes=[tint])
            S.op("dve", lambda e: e.tensor_copy(out=tmp[:, 0:n], in_=tint[:, 0:n]), reads=[tint], writes=[tmp])
            S.op("dve", lambda e: e.scalar_tensor_tensor(out=tmp[:, 0:n], in0=tmp[:, 0:n], scalar=-TWO_PI, in1=src[:, 0:n], op0=ALU.mult, op1=ALU.add),
                 reads=[tmp, src], writes=[tmp])
            if shift != 0.0:
                S.op("dve", lambda e: e.tensor_scalar(out=tmp[:, 0:n], in0=tmp[:, 0:n], scalar1=shift, scalar2=None, op0=ALU.add), reads=[tmp], writes=[tmp])
            S.op("dve", lambda e: e.tensor_scalar(out=dst[:, 0:n], in0=tmp[:, 0:n], scalar1=PI, scalar2=-TWO_PI, op0=ALU.is_gt, op1=ALU.mult), reads=[tmp], writes=[dst])
            S.op("dve", lambda e: e.tensor_tensor(out=tmp[:, 0:n], in0=tmp[:, 0:n], in1=dst[:, 0:n], op=ALU.add), reads=[tmp, dst], writes=[tmp])
            S.op("dve", lambda e: e.tensor_scalar(out=dst[:, 0:n], in0=tmp[:, 0:n], scalar1=-PI, scalar2=TWO_PI, op0=ALU.is_lt, op1=ALU.mult), reads=[tmp], writes=[dst])
            S.op("dve", lambda e: e.tensor_tensor(out=dst[:, 0:n], in0=tmp[:, 0:n], in1=dst[:, 0:n], op=ALU.add), reads=[tmp, dst], writes=[dst])

        rr(s_thr, s_th, s_t1, s_ti, 32)
        S.op("dve", lambda e: e.tensor_scalar(out=s_thq[:, :], in0=s_thr[:, :], scalar1=1.0 / TWO_PI, scalar2=None, op0=ALU.mult), reads=[s_thr], writes=[s_thq])
        S.op("act", lambda e: e.activation(out=s_sn1[:, :], in_=s_thr[:, :], func=AF.Sin), reads=[s_thr], writes=[s_sn1])
        rr(s_t2, s_thr, s_t1, s_ti, 32, shift=PI / 2)
        S.op("act", lambda e: e.activation(out=s_cs1[:, :], in_=s_t2[:, :], func=AF.Sin), reads=[s_t2], writes=[s_cs1])
        S.op("dve", lambda e: e.tensor_tensor(out=s_nr[:, :], in0=s_rho[:, :], in1=s_cs1[:, :], op=ALU.mult), reads=[s_rho, s_cs1], writes=[s_nr])
        S.op("dve", lambda e: e.tensor_scalar(out=s_nr[:, :], in0=s_nr[:, :], scalar1=-1.0, scalar2=None, op0=ALU.add), reads=[s_nr], writes=[s_nr])
        S.op("dve", lambda e: e.tensor_tensor(out=s_ni[:, :], in0=s_rho[:, :], in1=s_sn1[:, :], op=ALU.mult), reads=[s_rho, s_sn1], writes=[s_ni])
        S.op("dve", lambda e: e.tensor_tensor(out=s_t1[:, :], in0=s_are[:, :], in1=s_are[:, :], op=ALU.mult), reads=[s_are], writes=[s_t1])
        S.op("dve", lambda e: e.tensor_tensor(out=s_t2[:, :], in0=s_aim[:, :], in1=s_aim[:, :], op=ALU.mult), reads=[s_aim], writes=[s_t2])
        S.op("dve", lambda e: e.tensor_tensor(out=s_t1[:, :], in0=s_t1[:, :], in1=s_t2[:, :], op=ALU.add), reads=[s_t1, s_t2], writes=[s_t1])
        S.op("dve", lambda e: e.reciprocal(out=s_den[:, :], in_=s_t1[:, :]), reads=[s_t1], writes=[s_den])
        S.op("dve", lambda e: e.tensor_tensor(out=s_t1[:, :], in0=s_nr[:, :], in1=s_are[:, :], op=ALU.mult), reads=[s_nr, s_are], writes=[s_t1])
        S.op("dve", lambda e: e.tensor_tensor(out=s_t2[:, :], in0=s_ni[:, :], in1=s_aim[:, :], op=ALU.mult), reads=[s_ni, s_aim], writes=[s_t2])
        S.op("dve", lambda e: e.tensor_tensor(out=s_t1[:, :], in0=s_t1[:, :], in1=s_t2[:, :], op=ALU.add), reads=[s_t1, s_t2], writes=[s_t1])
        S.op("dve", lambda e: e.tensor_tensor(out=s_fr[:, :], in0=s_t1[:, :], in1=s_den[:, :], op=ALU.mult), reads=[s_t1, s_den], writes=[s_fr])
        S.op("dve", lambda e: e.tensor_tensor(out=s_t1[:, :], in0=s_ni[:, :], in1=s_are[:, :], op=ALU.mult), reads=[s_ni, s_are], writes=[s_t1])
        S.op("dve", lambda e: e.tensor_tensor(out=s_t2[:, :], in0=s_nr[:, :], in1=s_aim[:, :], op=ALU.mult), reads=[s_nr, s_aim], writes=[s_t2])
        S.op("dve", lambda e: e.tensor_tensor(out=s_t1[:, :], in0=s_t1[:, :], in1=s_t2[:, :], op=ALU.subtract), reads=[s_t1, s_t2], writes=[s_t1])
        S.op("dve", lambda e: e.tensor_tensor(out=s_fi[:, :], in0=s_t1[:, :], in1=s_den[:, :], op=ALU.mult), reads=[s_t1, s_den], writes=[s_fi])

        def gen_tables(u, cs_, sn_):
            S.op("dve", lambda e: e.tensor_scalar(out=BTr[:, :], in0=iot[:, :], scalar1=s_thq[:, u:u + 1], scalar2=MAGIC, op0=ALU.mult, op1=ALU.add),
                 reads=[iot, s_thq], writes=[BTr])
            S.op("dve", lambda e: e.tensor_scalar(out=BTr[:, :], in0=BTr[:, :], scalar1=MAGIC, scalar2=-TWO_PI, op0=ALU.subtract, op1=ALU.mult), reads=[BTr], writes=[BTr])
            S.op("dve", lambda e: e.scalar_tensor_tensor(out=sn_[:, :], in0=iot[:, :], scalar=s_thr[:, u:u + 1], in1=BTr[:, :], op0=ALU.mult, op1=ALU.add),
                 reads=[iot, s_thr, BTr], writes=[sn_])
            S.op("dve", lambda e: e.tensor_scalar(out=sn_[:, :], in0=sn_[:, :], scalar1=-PI, scalar2=PI, op0=ALU.max, op1=ALU.min), reads=[sn_], writes=[sn_])
            S.op("act", lambda e: e.activation(out=cs_[:, :], in_=sn_[:, :], func=AF.Abs), reads=[sn_], writes=[cs_])
            S.op("act", lambda e: e.activation(out=cs_[:, :], in_=cs_[:, :], func=AF.Sin, bias=halfpi[:, 0:1], scale=-1.0), reads=[cs_, halfpi], writes=[cs_])
            S.op("act", lambda e: e.activation(out=sn_[:, :], in_=sn_[:, :], func=AF.Sin), reads=[sn_], writes=[sn_])

        for o in range(4):
            S.dma("sp", usb[:, 0:NCX], proj[5120 + o * 128:5120 + (o + 1) * 128, NL:NT], reads=[DB("proj", 40 + o, 8)], writes=[usb])
            S.dma("sp", usb[:, NCX:NT], proj[5120 + o * 128:5120 + (o + 1) * 128, 0:NL], reads=DBrow("proj", 40 + o), writes=[usb])
            units = [(d_, j_) for d_ in range(2) for j_ in range(4)]
            gen_tables(units[0][0] * 16 + o * 4 + units[0][1], *TAB[0])
            for k, (d, jj) in enumerate(units):
                if True:
                    first = (k == 0)
                    u = d * 16 + o * 4 + jj
                    cs_, sn_ = TAB[k % 2]
                    S.dma("sp", sB[:, :, :], ssm_B[L][:, u], writes=[sB])
                    S.dma("sp", sC[:, :, :], ssm_C[L][:, u], writes=[sC])
                    S.op("dve", lambda e, u=u: e.tensor_scalar(out=sT[:, :], in0=sB[:, 1, :], scalar1=s_fi[:, u:u + 1], scalar2=None, op0=ALU.mult), reads=[sB, s_fi], writes=[sT])
                    S.op("dve", lambda e, u=u: e.scalar_tensor_tensor(out=sBb[:, 0, :], in0=sB[:, 0, :], scalar=s_fr[:, u:u + 1], in1=sT[:, :], op0=ALU.mult, op1=ALU.subtract),
                         reads=[sB, s_fr, sT], writes=[sBb])
                    S.op("dve", lambda e, u=u: e.tensor_scalar(out=sT[:, :], in0=sB[:, 0, :], scalar1=s_fi[:, u:u + 1], scalar2=None, op0=ALU.mult), reads=[sB, s_fi], writes=[sT])
                    S.op("dve", lambda e, u=u: e.scalar_tensor_tensor(out=sBb[:, 1, :], in0=sB[:, 1, :], scalar=s_fr[:, u:u + 1], in1=sT[:, :], op0=ALU.mult, op1=ALU.add),
                         reads=[sB, s_fr, sT], writes=[sBb])
                    S.op("act", lambda e: e.activation(out=sCb[:, 0, :], in_=sC[:, 0, :], func=AF.Copy), reads=[sC], writes=[sCb])
                    S.op("act", lambda e: e.activation(out=sCb[:, 1, :], in_=sC[:, 1, :], func=AF.Copy, scale=-1.0), reads=[sC], writes=[sCb])
                    S.op("act", lambda e: e.activation(out=sCb[:, 2, :], in_=sC[:, 0, :], func=AF.Copy, scale=-1.0), reads=[sC], writes=[sCb])
                    pbt = bankbf(0)
                    for ri in range(2):
                        S.op("pe", lambda e, ri=ri, pbt=pbt: e.transpose(out=pbt[:, ri * 128:(ri + 1) * 128], in_=sBb[:, ri, :], identity=ident[:, :]),
                             reads=[sBb, ident], writes=[pbt])
                    S.op("act", lambda e, pbt=pbt: e.activation(out=sWT[:, :, :], in_=pbt[:, 0:256].rearrange("p (a b) -> p a b", a=2), func=AF.Copy), reads=[pbt], writes=[sWT])
                    if k + 1 < 8:
                        dn_, jn_ = units[k + 1]
                        gen_tables(dn_ * 16 + o * 4 + jn_, *TAB[(k + 1) % 2])
                    def rsl(buf, c0, n, d=d):
                        if d == 0:
                            return buf[:, c0:c0 + n]
                        st_ = NT - 1 - c0
                        sp_ = st_ - n
                        return buf[:, st_::-1] if sp_ < 0 else buf[:, st_:sp_:-1]
                    for c0 in range(0, NT, 512):
                        n = min(512, NT - c0)
                        p0 = bank(1 + 3 * ((c0 // 512) % 2))
                        pq = bank(2 + 3 * ((c0 // 512) % 2))
                        uc0 = c0 if d == 0 else (NCX + c0 if c0 < NL else 0)
                        S.op("pe", lambda e, uc0=uc0, n=n, p0=p0: e.matmul(p0[:, 0:n], lhsT=sWT[:, 0, :], rhs=usb[:, uc0:uc0 + n], start=True, stop=True), reads=[sWT, usb], writes=[p0])
                        S.op("pe", lambda e, uc0=uc0, n=n, pq=pq: e.matmul(pq[:, 0:n], lhsT=sWT[:, 1, :], rhs=usb[:, uc0:uc0 + n], start=True, stop=True), reads=[sWT, usb], writes=[pq])
                        ta, tb, tc, td = stgf[0], stgf[1], stgf[2], stgf[3]
                        S.op("dve", lambda e, c0=c0, n=n, p0=p0, v=rsl(cs_, c0, min(512, NT - c0)): e.tensor_tensor(out=ta[:, 0:n], in0=p0[:, 0:n], in1=v, op=ALU.mult), reads=[p0, cs_], writes=[ta])
                        S.op("dve", lambda e, c0=c0, n=n, pq=pq, v=rsl(sn_, c0, min(512, NT - c0)): e.tensor_tensor(out=tb[:, 0:n], in0=pq[:, 0:n], in1=v, op=ALU.mult), reads=[pq, sn_], writes=[tb])
                        S.op("dve", lambda e, c0=c0, n=n: e.tensor_tensor(out=BTr[:, c0:c0 + n], in0=ta[:, 0:n], in1=tb[:, 0:n], op=ALU.add), reads=[ta, tb], writes=[sub(BTr, c0, n, 4)])
                        S.op("dve", lambda e, c0=c0, n=n, pq=pq, v=rsl(cs_, c0, min(512, NT - c0)): e.tensor_tensor(out=tc[:, 0:n], in0=pq[:, 0:n], in1=v, op=ALU.mult), reads=[pq, cs_], writes=[tc])
                        S.op("dve", lambda e, c0=c0, n=n, p0=p0, v=rsl(sn_, c0, min(512, NT - c0)): e.tensor_tensor(out=td[:, 0:n], in0=p0[:, 0:n], in1=v, op=ALU.mult), reads=[p0, sn_], writes=[td])
                        S.op("dve", lambda e, c0=c0, n=n: e.tensor_tensor(out=BTi[:, c0:c0 + n], in0=tc[:, 0:n], in1=td[:, 0:n], op=ALU.subtract), reads=[tc, td], writes=[sub(BTi, c0, n, 4)])
                    for BT in (BTr, BTi):
                        bv = BT[:, :] if d == 0 else BT[:, ::-1]
                        S.op("dve", lambda e, bv=bv, u=u: e.tensor_tensor_scan(out=bv, data0=s_rho[:, u:u + 1].to_broadcast([128, NT]), data1=bv, initial=0.0,
                                                                          op0=ALU.mult, op1=ALU.add), reads=[BT, s_rho], writes=[BT])
                    cvf = rsl(cs_, 0, NT)
                    svf = rsl(sn_, 0, NT)
                    S.op("dve", lambda e, cvf=cvf: e.tensor_tensor(out=sre[:, :], in0=BTr[:, :], in1=cvf, op=ALU.mult), reads=[BTr, cs_], writes=[sre])
                    S.op("dve", lambda e, svf=svf: e.tensor_tensor(out=sim[:, :], in0=BTi[:, :], in1=svf, op=ALU.mult), reads=[BTi, sn_], writes=[sim])
                    S.op("dve", lambda e, cvf=cvf: e.tensor_tensor(out=sP3[:, :], in0=BTi[:, :], in1=cvf, op=ALU.mult), reads=[BTi, cs_], writes=[sP3])
                    S.op("dve", lambda e, svf=svf: e.tensor_tensor(out=sP4[:, :], in0=BTr[:, :], in1=svf, op=ALU.mult), reads=[BTr, sn_], writes=[sP4])
                    for c0 in range(0, NT, 512):
                        n = min(512, NT - c0)
                        p0 = bank(3 if (c0 // 512) % 2 == 0 else 6)
                        yc0 = c0 if d == 0 else (NCX + c0 if c0 < NL else 0)
                        S.op("pe", lambda e, c0=c0, n=n, p0=p0: e.matmul(p0[:, 0:n], lhsT=sCb[:, 0, :], rhs=sre[:, c0:c0 + n], start=True, stop=False), reads=[sCb, sre], writes=[p0])
                        S.op("pe", lambda e, c0=c0, n=n, p0=p0: e.matmul(p0[:, 0:n], lhsT=sCb[:, 2, :], rhs=sim[:, c0:c0 + n], start=False, stop=False), reads=[sCb, sim], writes=[p0])
                        S.op("pe", lambda e, c0=c0, n=n, p0=p0: e.matmul(p0[:, 0:n], lhsT=sCb[:, 1, :], rhs=sP3[:, c0:c0 + n], start=False, stop=False), reads=[sCb, sP3], writes=[p0])
                        S.op("pe", lambda e, c0=c0, n=n, p0=p0: e.matmul(p0[:, 0:n], lhsT=sCb[:, 1, :], rhs=sP4[:, c0:c0 + n], start=False, stop=True), reads=[sCb, sP4], writes=[p0])
                        if first:
                            S.op("dve", lambda e, c0=yc0, n=n, p0=p0: e.tensor_copy(out=Yacc[:, c0:c0 + n], in_=p0[:, 0:n]), reads=[p0], writes=[Yacc])
                        else:
                            S.op("dve", lambda e, c0=yc0, n=n, p0=p0: e.tensor_tensor(out=Yacc[:, c0:c0 + n], in0=p0[:, 0:n], in1=Yacc[:, c0:c0 + n], op=ALU.add), reads=[p0, Yacc], writes=[Yacc])
            S.op("dve", lambda e, o=o: e.scalar_tensor_tensor(out=Yacc[:, :], in0=usb[:, :], scalar=sdk[:, o:o + 1], in1=Yacc[:, :], op0=ALU.mult, op1=ALU.add),
                 reads=[usb, sdk, Yacc], writes=[Yacc])
            S.op("dve", lambda e: e.tensor_tensor(out=BTr[:, :], in0=Yacc[:, :], in1=Yacc[:, :], op=ALU.mult), reads=[Yacc], writes=[BTr])
            S.op("dve", lambda e: e.tensor_scalar(out=BTr[:, :], in0=BTr[:, :], scalar1=0.044715, scalar2=1.0, op0=ALU.mult, op1=ALU.add), reads=[BTr], writes=[BTr])
            S.op("dve", lambda e: e.tensor_tensor(out=BTr[:, :], in0=BTr[:, :], in1=Yacc[:, :], op=ALU.mult), reads=[BTr, Yacc], writes=[BTr])
            S.op("act", lambda e: e.activation(out=BTi[:, :], in_=BTr[:, :], func=AF.Sigmoid, scale=float(2.0 * np.sqrt(2.0 / np.pi))), reads=[BTr], writes=[BTi])
            S.op("dve", lambda e: e.tensor_tensor(out=sre[:, :], in0=BTi[:, :], in1=Yacc[:, :], op=ALU.mult), reads=[BTi, Yacc], writes=[sre])
            S.dma("sp", yss[o * 128:(o + 1) * 128, NL:NT], sre[:, 0:NCX], reads=[sre], writes=[DB("yss", o, 1)])
            S.dma("sp", yss[o * 128:(o + 1) * 128, 0:NL], sre[:, NCX:NT], reads=[sre], writes=[DB("yss", o, 0)])
        al = Alloc(AR)
        gw = al.get(4 * 1024 * 2, BF16, "p (k n) -> p k n", k=4)
        YT = [al.get(4 * 512 * 2, BF16, "p (k n) -> p k n", k=4) for _ in range(2)]
        ZT = [al.get(4 * 512 * 2, BF16, "p (k n) -> p k n", k=4) for _ in range(2)]
        S.dma("pool", gw[:, :, :], glu_w[L].rearrange("(k p) n -> p k n", p=128), writes=[gw])
        tiles7 = TILES if with_ctx else TILES[:8]
        for ti, (c0, n) in enumerate(tiles7):
            yt = YT[ti % 2]
            zt = ZT[ti % 2]
            S.dma("sp", yt[:, :, 0:n], yss.rearrange("(k p) n -> p k n", p=128)[:, :, c0:c0 + n], reads=[DB("yss", oo, hh) for oo in range(4) for hh in range(2)], writes=[yt])
            S.dma("sp", zt[:, :, 0:n], proj[5632:6144, :].rearrange("(k p) n -> p k n", p=128)[:, :, c0:c0 + n], reads=[DB("proj", 44 + oo, ti) for oo in range(4)], writes=[zt])
            for ob in range(4):
                pa = next_bank()
                for k in range(4):
                    S.op("pe", lambda e, k=k, ob=ob, pa=pa, yt=yt, n=n: e.matmul(pa[:, 0:n], lhsT=gw[:, k, ob * 128:(ob + 1) * 128], rhs=yt[:, k, 0:n], start=(k == 0), stop=(k == 3)),
                         reads=[gw, yt], writes=[pa])
                pg = next_bank()
                for k in range(4):
                    S.op("pe", lambda e, k=k, ob=ob, pg=pg, yt=yt, n=n: e.matmul(pg[:, 0:n], lhsT=gw[:, k, 512 + ob * 128:512 + (ob + 1) * 128], rhs=yt[:, k, 0:n], start=(k == 0), stop=(k == 3)),
                         reads=[gw, yt], writes=[pg])
                S.op("act", lambda e, pg=pg, n=n: e.activation(out=stgf[2][:, 0:n], in_=pg[:, 0:n], func=AF.Sigmoid), reads=[pg], writes=[stgf[2]])
                S.op("dve", lambda e, pa=pa, n=n: e.tensor_tensor(out=stgf[2][:, 0:n], in0=pa[:, 0:n], in1=stgf[2][:, 0:n], op=ALU.mult), reads=[pa, stgf[2]], writes=[stgf[2]])
                st = next_stg()
                S.op("dve", lambda e, st=st, zt=zt, ob=ob, n=n: e.tensor_tensor(out=st[:, 0:n], in0=stgf[2][:, 0:n], in1=zt[:, ob, 0:n], op=ALU.mult), reads=[stgf[2], zt], writes=[st])
                S.dma("sp", a_d[1536 + ob * 128:1536 + (ob + 1) * 128, c0:c0 + n], st[:, 0:n], reads=[st], writes=[DB("a", 12 + ob, ti)])

        if stop == "S7":
            raise _Stop()
        al = Alloc(AR)
        GT = [al.get(4 * 512 * 2, BF16, "p (i n) -> p i n", i=4) for _ in range(2)]
        al.off = 16 * 1024
        wbr = al.get(KT * D * 2, BF16, "p (k n) -> p k n", k=KT)
        AT = [al.get(KT * 512 * 2, BF16, "p (k n) -> p k n", k=KT) for _ in range(2)]
        wo = AR.view(112 * 1024, KT * D * 2, BF16, "p (k n) -> p k n", k=KT)
        for half in range(2):
            src = w_br[L].rearrange("(k p) n -> p k n", p=128)
            for kk in range(0, KT, 4):
                S.dma("pool", wbr[:, kk:kk + 4, half * 1024:(half + 1) * 1024], src[:, kk:kk + 4, half * 1024:(half + 1) * 1024], writes=[wbr])
        for half in range(2):
            src = w_o[L].rearrange("(k p) n -> p k n", p=128)
            for kk in range(0, KT, 4):
                S.dma("pool", wo[:, kk:kk + 4, half * 1024:(half + 1) * 1024], src[:, kk:kk + 4, half * 1024:(half + 1) * 1024], writes=[wo])
        tiles8 = TILES if with_ctx else TILES[:8]
        gcnt = 0
        def load_at(ti_):
            c0_, n_ = tiles8[ti_]
            S.dma("sp", AT[ti_ % 2][:, :, 0:n_], a_d.rearrange("(k p) n -> p k n", p=128)[:, :, c0_:c0_ + n_],
                  reads=[DB("a", rb, t2) for rb in range(16) for t2 in range(9)], writes=[AT[ti_ % 2]])

        def load_gt(cnt):
            ti_, j_ = divmod(cnt, 16)
            c0_, n_ = tiles8[ti_]
            S.dma("sp", GT[cnt % 2][:, :, 0:n_], proj[6144:14336, :].rearrange("(i j p) n -> j p i n", i=4, j=16)[j_][:, :, c0_:c0_ + n_],
                  reads=[DB("proj", 48 + i * 16 + j_, ti_) for i in range(4)], writes=[GT[cnt % 2]])

        load_at(0)
        load_gt(0)
        for ti, (c0, n) in enumerate(tiles8):
            at = AT[ti % 2]
            if ti + 1 < len(tiles8):
                load_at(ti + 1)
            for j in range(16):
                gt = GT[gcnt % 2]
                gcnt += 1
                if gcnt < 16 * len(tiles8):
                    load_gt(gcnt)
                acc = stgf[0]
                tmpf = stgf[1]
                for i in range(4):
                    pb = next_bank()
                    for kk in range(4):
                        S.op("pe", lambda e, pb=pb, i=i, kk=kk, j=j, at=at, n=n: e.matmul(pb[:, 0:n], lhsT=wbr[:, i * 4 + kk, j * 128:(j + 1) * 128],
                                                                                  rhs=at[:, i * 4 + kk, 0:n], start=(kk == 0), stop=(kk == 3)),
                             reads=[wbr, at], writes=[pb])
                    if i == 0:
                        S.op("dve", lambda e, pb=pb, gt=gt, n=n: e.tensor_tensor(out=acc[:, 0:n], in0=pb[:, 0:n], in1=gt[:, 0, 0:n], op=ALU.mult),
                             reads=[pb, gt], writes=[acc])
                    else:
                        S.op("dve", lambda e, pb=pb, gt=gt, n=n, i=i: e.tensor_tensor(out=tmpf[:, 0:n], in0=pb[:, 0:n], in1=gt[:, i, 0:n], op=ALU.mult),
                             reads=[pb, gt], writes=[tmpf])
                        if i < 3:
                            S.op("dve", lambda e, n=n: e.tensor_tensor(out=acc[:, 0:n], in0=acc[:, 0:n], in1=tmpf[:, 0:n], op=ALU.add), reads=[acc, tmpf], writes=[acc])
                        else:
                            mst = next_stg()
                            S.op("dve", lambda e, n=n, mst=mst: e.tensor_tensor(out=mst[:, 0:n], in0=acc[:, 0:n], in1=tmpf[:, 0:n], op=ALU.add), reads=[acc, tmpf], writes=[mst])
                            S.dma("sp", mg[j * 128:(j + 1) * 128, c0:c0 + n], mst[:, 0:n], reads=[mst], writes=[DB("mg", j, ti)])

        if stop == "S8":
            raise _Stop()
        al = Alloc(AR)
        MT2 = [al.get(KT * 256 * 2, BF16, "p (k n) -> p k n", k=KT) for _ in range(2)]
        Y = al.get(KT * 256 * 4, F32, "p (k n) -> p k n", k=KT)
        SQ = al.get(KT * 256 * 2, BF16, "p (k n) -> p k n", k=KT)
        XT9 = [al.get(KT * 256 * 4, F32, "p (k n) -> p k n", k=KT) for _ in range(2)]
        ntok = NT if with_ctx else NL

        def load9(qi_):
            c0_ = qi_ * 256
            ti_ = min(c0_ // 512, 8)
            S.dma("sp", MT2[qi_ % 2][:, :, :], mg.rearrange("(k p) n -> p k n", p=128)[:, :, c0_:c0_ + 256], reads=[DB("mg", j_, ti_) for j_ in range(16)], writes=[MT2[qi_ % 2]])
            S.dma("sp", XT9[qi_ % 2][:, :, :], xsrc.rearrange("(k p) n -> p k n", p=128)[:, :, c0_:c0_ + 256], reads=[DB(xsn, 0, ti_)], writes=[XT9[qi_ % 2]])

        for qi, c0 in enumerate(range(0, ntok, 256)):
            n = 256
            ti = min(c0 // 512, 8)
            s = 0 if c0 < NL else 1
            mt = MT2[qi % 2]
            XT = XT9[qi % 2]
            if qi == 0:
                load9(0)
            if c0 + 256 < ntok:
                load9(qi + 1)
            for j in range(16):
                pb = next_bank()
                for k in range(KT):
                    S.op("pe", lambda e, pb=pb, k=k, j=j, mt=mt, wo=wo: e.matmul(pb[:, 0:256], lhsT=wo[:, k, j * 128:(j + 1) * 128], rhs=mt[:, k, :],
                                                                     start=(k == 0), stop=(k == KT - 1)), reads=[wo, mt], writes=[pb])
                S.op("dve", lambda e, pb=pb, j=j: e.tensor_copy(out=Y[:, j, :], in_=pb[:, 0:256]), reads=[pb], writes=[Y])
                S.op("act", lambda e, j=j: e.activation(out=SQ[:, j, :], in_=Y[:, j, :], func=AF.Square), reads=[Y], writes=[SQ])
            if stop == "S9a":
                raise _Stop()
            pb = next_bank()
            for k in range(KT):
                S.op("pe", lambda e, k=k, pb=pb: e.matmul(pb[:, 0:256], lhsT=ones[:, :], rhs=SQ[:, k, :], start=(k == 0), stop=(k == KT - 1)),
                     reads=[ones, SQ], writes=[pb])
            S.op("act", lambda e, pb=pb: e.activation(out=rtmp[:, 0:256], in_=pb[:, 0:256], func=AF.Sqrt, bias=epsb[:, 0:1], scale=1.0 / D),
                 reads=[pb, epsb], writes=[rtmp])
            S.op("dve", lambda e: e.reciprocal(out=rstd[:, 0:256], in_=rtmp[:, 0:256]), reads=[rtmp], writes=[rstd])
            S.op("dve", lambda e: e.tensor_tensor(out=Y[:, :, :], in0=Y[:, :, :], in1=rstd[:, 0:256].unsqueeze(1).to_broadcast([128, KT, 256]), op=ALU.mult),
                 reads=[Y, rstd], writes=[Y])
            if stop == "S9b":
                raise _Stop()
            for k in range(KT):
                S.op("dve", lambda e, k=k, s=s, XT=XT, Gmod=Gmod: e.scalar_tensor_tensor(out=XT[:, k, :], in0=Y[:, k, :], scalar=Gmod[:, k, s:s + 1], in1=XT[:, k, :],
                                                                     op0=ALU.mult, op1=ALU.add), reads=[Y, Gmod, XT], writes=[XT])
            if last:
                S.dma("sp", outT.rearrange("(k p) n -> p k n", p=128)[:, :, c0:c0 + n], XT[:, :, :], reads=[XT], writes=[DB("out", 0, qi)])
            else:
                S.dma("sp", xs.rearrange("(k p) n -> p k n", p=128)[:, :, c0:c0 + n], XT[:, :, :], reads=[XT], writes=[DB("xs", 0, ti)])
            if stop == "S9c":
                raise _Stop()
      except _Stop:
        break

    finals = [o for o in S.qops["sp"] if o.dma][-40:]
    S.emit(final_ops=finals)
    return nc


def _fm(v, nb):
    return np.ascontiguousarray(np.swapaxes(v.reshape(v.shape[:-1] + (nb, 128)), -1, -2))


def _na_bias(rpb):
    pats = [(0, 0), (2, 0), (8, 4), (60, 54), (62, 54)]
    drow = np.zeros((5, 128, 640), np.int64)
    dcol = np.zeros((5, 128, 640), np.int64)
    mask = np.zeros((5, 128, 640), bool)
    qc = np.arange(64)
    kc = np.arange(64)
    cs = np.clip(qc - 8, 0, 48)
    inwin = (kc[None, :] >= cs[:, None]) & (kc[None, :] < cs[:, None] + 16)
    dc = np.clip(kc[None, :] - qc[:, None] + 15, 0, 30)
    for pi, (r0, lo) in enumerate(pats):
        for dr in range(2):
            r = r0 + dr
            bs = min(max(r - 4, 0), 56)
            for ko in range(10):
                kr = lo + ko
                inb = bs <= kr < bs + 8
                drw = min(max(kr - r + 7, 0), 14)
                drow[pi, dr * 64:(dr + 1) * 64, ko * 64:(ko + 1) * 64] = drw
                dcol[pi, dr * 64:(dr + 1) * 64, ko * 64:(ko + 1) * 64] = dc
                mask[pi, dr * 64:(dr + 1) * 64, ko * 64:(ko + 1) * 64] = inwin & inb
    g = rpb[:, :, drow, dcol]
    g = np.where(mask[None, None], g, np.float32(-30000.0)).astype(np.float32)
    return np.ascontiguousarray(g.transpose(0, 1, 3, 2, 4))


def _pool_edge():
    out = np.zeros((128, 2, 4, 2, 8), np.float32)
    for si, n in enumerate((NL, NCX)):
        for g, w in enumerate((2, 4, 8, 16)):
            t = np.arange(n)
            lo = np.clip(t - w // 2, 0, n)
            hi = np.clip(t + w - w // 2, 0, n)
            inv = (1.0 / (hi - lo)).astype(np.float32)
            out[:, si, g, 0, :] = inv[None, 0:8]
            out[:, si, g, 1, :] = inv[None, n - 8:n]
    return out


_NC_CACHE = {}
WORK_CORES = [0, 1, 4, 5]


def _prep(x, c, ctx, c_ctx, w_mod, b_mod, g_pre, g_post, w_in, b_gate, na_rpb, pool_w,
           pool_scale, conv_w, ssm_a_re, ssm_a_im, ssm_log_dt, ssm_b_re, ssm_b_im,
           ssm_c_re, ssm_c_im, ssm_d, glu_w, w_br, w_o):
    f = np.float32
    x = np.asarray(x, f); ctx = np.asarray(ctx, f); c = np.asarray(c, f); c_ctx = np.asarray(c_ctx, f)
    shared = {
        "w_mod": np.ascontiguousarray(w_mod, f),
        "b_mod": _fm(np.asarray(b_mod, f), 48),
        "g_pre": _fm(np.asarray(g_pre, f), 16),
        "g_post": _fm(np.asarray(g_post, f), 16),
        "w_in": np.ascontiguousarray(w_in, f),
        "b_gate": _fm(np.asarray(b_gate, f), 64),
        "na_bias": _na_bias(np.asarray(na_rpb, f)),
        "pool_w": np.ascontiguousarray(np.asarray(pool_w, f).transpose(0, 2, 1, 3)),
        "pool_sc": _fm(np.asarray(pool_scale, f), 4),
        "pool_edge": _pool_edge(),
        "conv_w": np.ascontiguousarray(np.asarray(conv_w, f).reshape(DEPTH, 3, 4, 128).transpose(0, 3, 2, 1)),
        "ssm_d": _fm(np.asarray(ssm_d, f), 4),
        "glu_w": np.ascontiguousarray(glu_w, f),
        "w_br": np.ascontiguousarray(w_br, f),
        "w_o": np.ascontiguousarray(w_o, f),
        "ident": np.eye(128, dtype=f),
        "iota17": np.tile(np.arange(17, dtype=f)[None], (128, 1)),
        "iota_nt": np.tile(np.arange(NT, dtype=f)[None], (128, 1)),
    }

    def upl(a):
        a = np.asarray(a, f).reshape(DEPTH, 2, 16, 2, 64)
        return np.ascontiguousarray(a.transpose(0, 3, 4, 1, 2).reshape(DEPTH, 128, 32))

    shared["ssm_are"] = upl(ssm_a_re)
    shared["ssm_aim"] = upl(ssm_a_im)
    shared["ssm_ldt"] = upl(np.broadcast_to(np.asarray(ssm_log_dt, f)[..., None], (DEPTH, 2, 32, 64)))
    Bp = np.zeros((DEPTH, 128, 32, 2, 128), f)
    Cp = np.zeros((DEPTH, 128, 32, 2, 128), f)
    bre = np.asarray(ssm_b_re, f); bim = np.asarray(ssm_b_im, f)
    cre = np.asarray(ssm_c_re, f); cim = np.asarray(ssm_c_im, f)
    for d in range(2):
        for j in range(16):
            for gl in range(2):
                g = 2 * j + gl
                u = d * 16 + j
                ch0 = (j % 4) * 32 + gl * 16
                Bp[:, gl * 64:(gl + 1) * 64, u, 0, ch0:ch0 + 16] = bre[:, d, g]
                Bp[:, gl * 64:(gl + 1) * 64, u, 1, ch0:ch0 + 16] = bim[:, d, g]
                Cp[:, gl * 64:(gl + 1) * 64, u, 0, ch0:ch0 + 16] = cre[:, d, g].transpose(0, 2, 1)
                Cp[:, gl * 64:(gl + 1) * 64, u, 1, ch0:ch0 + 16] = cim[:, d, g].transpose(0, 2, 1)
    shared["ssm_B"] = Bp
    shared["ssm_C"] = Cp
    zeros = {k: np.zeros_like(v) for k, v in shared.items()}
    in_maps = []
    for core in range(8):
        if core in WORK_CORES:
            b = WORK_CORES.index(core)
            m = dict(shared)
            m["x0"] = np.ascontiguousarray(np.concatenate([x[b].T, ctx[b].T], axis=1))
            cv = np.stack([c[b].reshape(16, 128).T, c_ctx.reshape(16, 128).T], axis=-1)
            m["cvec"] = np.ascontiguousarray(cv, f)
        else:
            m = dict(zeros)
            m["x0"] = np.zeros((D, NT), f)
            m["cvec"] = np.zeros((128, KT, 2), f)
        in_maps.append(m)
    return in_maps


def kernel(x, c, ctx, c_ctx, w_mod, b_mod, g_pre, g_post, w_in, b_gate, na_rpb, pool_w,
           pool_scale, conv_w, ssm_a_re, ssm_a_im, ssm_log_dt, ssm_b_re, ssm_b_im,
           ssm_c_re, ssm_c_im, ssm_d, glu_w, w_br, w_o):
    in_maps = _prep(x, c, ctx, c_ctx, w_mod, b_mod, g_pre, g_post, w_in, b_gate, na_rpb, pool_w,
                    pool_scale, conv_w, ssm_a_re, ssm_a_im, ssm_log_dt, ssm_b_re, ssm_b_im,
                    ssm_c_re, ssm_c_im, ssm_d, glu_w, w_br, w_o)
    if "nc" not in _NC_CACHE:
        _NC_CACHE["nc"] = build_nc()
    res = run_bass_kernel_spmd(_NC_CACHE["nc"], in_maps, core_ids=list(range(8)))
    out = np.stack([np.ascontiguousarray(res.results[WORK_CORES[b]]["outT"].T) for b in range(4)], axis=0)
    return out.astype(np.float32)
```
